# Optimizing a Trainium2 kernel written in Bass

```python
import math
import jax, jax.numpy as jnp
from jax import lax
import numpy as np

D_MODEL = 1024
BATCH = 8
SEQ = 4096
DEPTH = 2

HEAD_DIM = 64
N_META = 16
BLOCK_Q = 128
SB_HEADS = 4
SP_HEADS = 4
IDX_HEADS = 8
IDX_DIM = 32
TOPK_MAX = 256
DF_HEADS = 4
N_BUCKETS = 32
MAX_DISTANCE = 128
N_BIAS_HEADS = SP_HEADS + DF_HEADS
D_FF = 2816
CONV_WIDTH = 3
EPS = 1e-6
NEG_INF = -1e30

SB_W = SB_HEADS * HEAD_DIM
SP_W = SP_HEADS * HEAD_DIM
DF_W = DF_HEADS * 2 * HEAD_DIM
SPLITS = (SB_W, SB_W, SB_W,
          SP_W, SP_W, SP_W,
          IDX_HEADS * IDX_DIM, IDX_DIM, IDX_HEADS,
          DF_W, DF_W, DF_W,
          D_MODEL, D_MODEL, D_MODEL)
D_IN = sum(SPLITS)

kernel_name = 'hybrid_gated_sb_dsa_diff_convglu'


def _rmsnorm(a, gain):
    af = a.astype(jnp.float32)
    af = af * lax.rsqrt(jnp.mean(af * af, axis=-1, keepdims=True) + EPS)
    return (af * gain.astype(jnp.float32)).astype(a.dtype)


def _split_cols(z):
    out, off = [], 0
    for w in SPLITS:
        out.append(z[..., off:off + w])
        off += w
    return out


def _to_blocks(a):
    b, tp = a.shape[:2]
    return jnp.moveaxis(a.reshape((b, tp // BLOCK_Q, BLOCK_Q) + a.shape[2:]), 1, 0)


def _from_blocks(a):
    nb, b = a.shape[:2]
    return jnp.moveaxis(a, 0, 1).reshape((b, nb * BLOCK_Q) + a.shape[3:])


def _t5_bucket(rel):
    n = jnp.maximum(rel, 0)
    max_exact = N_BUCKETS // 2
    nf = jnp.maximum(n, 1).astype(jnp.float32)
    large = max_exact + (jnp.log(nf / max_exact) / math.log(MAX_DISTANCE / max_exact)
                         * (N_BUCKETS - max_exact)).astype(jnp.int32)
    return jnp.where(n < max_exact, n, jnp.minimum(large, N_BUCKETS - 1))


def _stick_breaking(q, k, v):
    tp, dh = q.shape[1], q.shape[-1]
    scale = dh ** -0.5
    kf, vf = k.astype(jnp.float32), v.astype(jnp.float32)
    key_pos = jnp.arange(tp)

    def block(args):
        qb, start = args
        qpos = start + jnp.arange(BLOCK_Q)
        z = jnp.einsum('bqhd,bkhd->bhqk', qb.astype(jnp.float32), kf) * scale
        mask = key_pos[None, :] < qpos[:, None]
        log_1m_beta = jnp.where(mask, jax.nn.log_sigmoid(-z), 0.0)
        between = lax.cumsum(log_1m_beta, axis=3, reverse=True) - log_1m_beta
        w = jnp.where(mask, jnp.exp(jax.nn.log_sigmoid(z) + between), 0.0)
        return jnp.einsum('bhqk,bkhd->bqhd', w, vf)

    nb = tp // BLOCK_Q
    out = lax.map(block, (_to_blocks(q), jnp.arange(nb) * BLOCK_Q))
    return _from_blocks(out).astype(v.dtype)


def _indexed_sparse_attention(q, k, v, q_ix, k_ix, w_ix, bias_table, top_k):
    tp, h, dh = q.shape[1], q.shape[2], q.shape[3]
    scale = dh ** -0.5
    k_ixf = k_ix.astype(jnp.float32)
    key_pos = jnp.arange(tp)
    table = bias_table.astype(jnp.float32)

    def block(args):
        qb, qib, wb, start = args
        qpos = start + jnp.arange(BLOCK_Q)
        dots = jnp.einsum('bqjd,bkd->bqjk', qib.astype(jnp.float32), k_ixf)
        score = jnp.einsum('bqj,bqjk->bqk', wb.astype(jnp.float32), jax.nn.relu(dots))
        score = jnp.where((key_pos[None, :] <= qpos[:, None])[None], score, NEG_INF)
        _, sel = lax.top_k(score, top_k)
        rel = qpos[None, :, None] - sel
        valid = rel >= 0
        k_sel = jax.vmap(lambda kk, ii: kk[ii])(k, sel)
        v_sel = jax.vmap(lambda vv, ii: vv[ii])(v, sel)
        s = jnp.einsum('bqhd,bqkhd->bhqk', qb.astype(jnp.float32),
                       k_sel.astype(jnp.float32)) * scale
        s = s + jnp.moveaxis(table[_t5_bucket(rel)], -1, 1)
        s = jnp.where(valid[:, None], s, NEG_INF)
        p = jax.nn.softmax(s, axis=-1)
        return jnp.einsum('bhqk,bqkhd->bqhd', p, v_sel.astype(jnp.float32))

    nb = tp // BLOCK_Q
    out = lax.map(block, (_to_blocks(q), _to_blocks(q_ix), _to_blocks(w_ix),
                          jnp.arange(nb) * BLOCK_Q))
    return _from_blocks(out).astype(v.dtype)


def _differential_attention(q1, q2, k1, k2, v, lam, bias_table):
    tp, dh = q1.shape[1], q1.shape[-1]
    scale = dh ** -0.5
    k1f, k2f, vf = k1.astype(jnp.float32), k2.astype(jnp.float32), v.astype(jnp.float32)
    key_pos = jnp.arange(tp)
    table = bias_table.astype(jnp.float32)

    def block(args):
        q1b, q2b, start = args
        qpos = start + jnp.arange(BLOCK_Q)
        mask = key_pos[None, :] <= qpos[:, None]
        bias = jnp.moveaxis(table[_t5_bucket(qpos[:, None] - key_pos[None, :])], -1, 0)

        def attn_map(qb, kf):
            s = jnp.einsum('bqhd,bkhd->bhqk', qb.astype(jnp.float32), kf) * scale + bias
            return jax.nn.softmax(jnp.where(mask, s, NEG_INF), axis=-1)

        p = attn_map(q1b, k1f) - lam * attn_map(q2b, k2f)
        return jnp.einsum('bhqk,bkhd->bqhd', p, vf)

    nb = tp // BLOCK_Q
    out = lax.map(block, (_to_blocks(q1), _to_blocks(q2), jnp.arange(nb) * BLOCK_Q))
    return _from_blocks(out).astype(v.dtype)


def _mixer(u, layer, w_in, b_gate, q_norm_sp, k_norm_sp, q_norm_df, k_norm_df,
           lam_q1, lam_k1, lam_q2, lam_k2, subln_df, w_br_sb, w_br_sp, w_br_df, w_out,
           rel_bias, top_k):
    b, tp, _ = u.shape
    (q_sb, k_sb, v_sb, q_sp, k_sp, v_sp, q_ix, k_ix, w_ix,
     q_df, k_df, v_df, g_sb, g_sp, g_df) = _split_cols(u @ w_in)

    def hd(a, n):
        return a.reshape(b, tp, n, -1)

    y_sb = _stick_breaking(hd(q_sb, SB_HEADS), hd(k_sb, SB_HEADS), hd(v_sb, SB_HEADS))
    y_sp = _indexed_sparse_attention(_rmsnorm(hd(q_sp, SP_HEADS), q_norm_sp),
                                     _rmsnorm(hd(k_sp, SP_HEADS), k_norm_sp),
                                     hd(v_sp, SP_HEADS), hd(q_ix, IDX_HEADS), k_ix, w_ix,
                                     rel_bias[:, :SP_HEADS], top_k)
    q_df = _rmsnorm(q_df.reshape(b, tp, DF_HEADS, 2, HEAD_DIM), q_norm_df)
    k_df = _rmsnorm(k_df.reshape(b, tp, DF_HEADS, 2, HEAD_DIM), k_norm_df)
    lam_init = 0.8 - 0.6 * math.exp(-0.3 * layer)
    lam = (jnp.exp(jnp.sum(lam_q1.astype(jnp.float32) * lam_k1.astype(jnp.float32)))
           - jnp.exp(jnp.sum(lam_q2.astype(jnp.float32) * lam_k2.astype(jnp.float32)))
           + lam_init)
    y_df = _differential_attention(q_df[..., 0, :], q_df[..., 1, :], k_df[..., 0, :],
                                   k_df[..., 1, :], hd(v_df, DF_HEADS), lam,
                                   rel_bias[:, SP_HEADS:])
    y_df = _rmsnorm(y_df, subln_df) * (1.0 - lam_init)

    merged = (jax.nn.sigmoid(g_sb + b_gate[:D_MODEL]) * (y_sb.reshape(b, tp, -1) @ w_br_sb)
              + jax.nn.sigmoid(g_sp + b_gate[D_MODEL:2 * D_MODEL]) * (y_sp.reshape(b, tp, -1) @ w_br_sp)
              + jax.nn.sigmoid(g_df + b_gate[2 * D_MODEL:]) * (y_df.reshape(b, tp, -1) @ w_br_df))
    return merged @ w_out


def _conv_ffn(u, w_up, conv_w, conv_b, w_down):
    tp = u.shape[1]
    z = u @ w_up
    gate, val = z[..., :D_FF], z[..., D_FF:]
    gp = jnp.pad(gate, ((0, 0), (CONV_WIDTH - 1, 0), (0, 0)))
    conv = conv_b + sum(gp[:, i:i + tp] * conv_w[i] for i in range(CONV_WIDTH))
    return (jax.nn.silu(conv) * val) @ w_down


def setup_inputs(seed: int = 0) -> dict:
    key = jax.random.key(seed)
    ks = jax.random.split(key, 24)

    def n(k, shape, s):
        return jax.random.normal(k, shape, jnp.float32) * s

    return {
        'x': n(ks[0], (BATCH, SEQ, D_MODEL), 1.0),
        'meta_tokens': n(ks[1], (N_META, D_MODEL), 1.0),
        'rel_bias': n(ks[2], (N_BUCKETS, N_BIAS_HEADS), 0.5),
        'attn_norm': 1.0 + n(ks[3], (DEPTH, D_MODEL), 0.01),
        'w_in': n(ks[4], (DEPTH, D_MODEL, D_IN), D_MODEL ** -0.5),
        'b_gate': n(ks[5], (DEPTH, 3 * D_MODEL), 0.02),
        'q_norm_sp': 1.0 + n(ks[6], (DEPTH, HEAD_DIM), 0.01),
        'k_norm_sp': 1.0 + n(ks[7], (DEPTH, HEAD_DIM), 0.01),
        'q_norm_df': 1.0 + n(ks[8], (DEPTH, HEAD_DIM), 0.01),
        'k_norm_df': 1.0 + n(ks[9], (DEPTH, HEAD_DIM), 0.01),
        'lam_q1': n(ks[10], (DEPTH, HEAD_DIM), 0.1),
        'lam_k1': n(ks[11], (DEPTH, HEAD_DIM), 0.1),
        'lam_q2': n(ks[12], (DEPTH, HEAD_DIM), 0.1),
        'lam_k2': n(ks[13], (DEPTH, HEAD_DIM), 0.1),
        'subln_df': 1.0 + n(ks[14], (DEPTH, 2 * HEAD_DIM), 0.01),
        'w_br_sb': n(ks[15], (DEPTH, SB_W, D_MODEL), SB_W ** -0.5),
        'w_br_sp': n(ks[16], (DEPTH, SP_W, D_MODEL), SP_W ** -0.5),
        'w_br_df': n(ks[17], (DEPTH, DF_W, D_MODEL), DF_W ** -0.5),
        'w_out': n(ks[18], (DEPTH, D_MODEL, D_MODEL), D_MODEL ** -0.5),
        'ffn_norm': 1.0 + n(ks[19], (DEPTH, D_MODEL), 0.01),
        'w_up': n(ks[20], (DEPTH, D_MODEL, 2 * D_FF), D_MODEL ** -0.5),
        'conv_w': n(ks[21], (DEPTH, CONV_WIDTH, D_FF), CONV_WIDTH ** -0.5),
        'conv_b': n(ks[22], (DEPTH, D_FF), 0.02),
        'w_down': n(ks[23], (DEPTH, D_FF, D_MODEL), D_FF ** -0.5),
    }


def reference(x, meta_tokens, rel_bias, attn_norm, w_in, b_gate, q_norm_sp, k_norm_sp,
              q_norm_df, k_norm_df, lam_q1, lam_k1, lam_q2, lam_k2, subln_df, w_br_sb,
              w_br_sp, w_br_df, w_out, ffn_norm, w_up, conv_w, conv_b, w_down):
    b, s, d = x.shape
    t = N_META + s
    tp = -(-t // BLOCK_Q) * BLOCK_Q
    top_k = min(TOPK_MAX, t // 4)
    meta = jnp.broadcast_to(meta_tokens[None].astype(x.dtype), (b, N_META, d))
    h = jnp.concatenate([meta, x, jnp.zeros((b, tp - t, d), x.dtype)], axis=1)
    for l in range(DEPTH):
        h = h + _mixer(_rmsnorm(h, attn_norm[l]), l, w_in[l], b_gate[l], q_norm_sp[l],
                       k_norm_sp[l], q_norm_df[l], k_norm_df[l], lam_q1[l], lam_k1[l],
                       lam_q2[l], lam_k2[l], subln_df[l], w_br_sb[l], w_br_sp[l],
                       w_br_df[l], w_out[l], rel_bias, top_k)
        h = h + _conv_ffn(_rmsnorm(h, ffn_norm[l]), w_up[l], conv_w[l], conv_b[l], w_down[l])
    return h[:, N_META:t]
```

```python
import math
from contextlib import ExitStack

import numpy as np
import concourse.bass as bass
import concourse.mybir as mybir
from concourse.bass_utils import run_bass_kernel_spmd

F32 = mybir.dt.float32
BF16 = mybir.dt.bfloat16
AF = mybir.ActivationFunctionType
ALU = mybir.AluOpType
AX = mybir.AxisListType

D = 1024
SEQ = 4096
NMETA = 16
T = 4224
NB = 33
DEPTH = 2
DIN = 6440
DFF = 2816
NFC = 22
EPS = 1e-6
TOPK = 256
NIT = 15
TILES = [(i * 512, 512) for i in range(8)] + [(4096, 128)]
TILES_Q = [(i * 512, 512) for i in range(8)] + [(4096, NMETA + SEQ - 4096)]
NCORES = 8
SKIP_D1 = False
PE_FILL_B = 0
SKIP_D2 = False

CHUNKS = []
for i in range(2):
    CHUNKS.append((0 + 128 * i, 128, 'copy', 1.0))
for i in range(2):
    CHUNKS.append((256 + 128 * i, 128, 'copy', 0.125))
for i in range(2):
    CHUNKS.append((768 + 128 * i, 128, 'norm', 0))
for i in range(2):
    CHUNKS.append((1024 + 128 * i, 128, 'norm', 1))
for i in range(2):
    CHUNKS.append((1536 + 128 * i, 128, 'copy', 1.0))
CHUNKS.append((1792, 32, 'copy', 1.0))
for i in range(4):
    CHUNKS.append((1832 + 128 * i, 128, 'norm', 2))
for i in range(4):
    CHUNKS.append((2344 + 128 * i, 128, 'norm', 3))
for i in range(24):
    CHUNKS.append((3368 + 128 * i, 128, 'gate', i))
NCH = len(CHUNKS)

CB_ONES, CB_NEGONES, CB_NEGTRI, CB_MSTRICT, CB_ZEROS, CB_IDENT, CB_ONES512 = 0, 128, 256, 384, 512, 640, 768
CB_W = 768 + 512
CF_IDENT, CF_ONESD, CF_BLK64, CF_ONES128TH, CF_MNEG = 0, 128, 256, 384, 512
CF_W = 640


class Sched:
    NDSEM = 32

    def __init__(self, nc, stack):
        self.nc = nc
        self.eng = {'pe': nc.tensor, 'act': nc.scalar, 'dve': nc.vector,
                    'pool': nc.gpsimd, 'sp': nc.sync}
        self.ops = {k: [] for k in self.eng}
        self.sem = {k: stack.enter_context(nc.semaphore("s_" + k)) for k in self.eng}
        self.cnt = {k: 0 for k in self.eng}
        self.seen = {k: {} for k in self.eng}
        self.dsem, self.dval, self.dcnt = {}, {}, {}
        for q in ('sp', 'pool'):
            self.dsem[q] = [stack.enter_context(nc.semaphore("d_%s%d" % (q, i)))
                            for i in range(self.NDSEM)]
            self.dval[q] = [0] * self.NDSEM
            self.dcnt[q] = 0
        self.w, self.r, self.semobj = {}, {}, {}
        self.n_ops = 0
        self.n_waits = 0

    def _sid(self, sem):
        i = id(sem)
        self.semobj[i] = sem
        return i

    def op(self, eng, fn, reads=(), writes=(), dma=False, noinc=False):
        psr = [k for k in reads if isinstance(k, tuple) and k[0] == 'ps']
        if psr:
            reads = [k for k in reads if not (isinstance(k, tuple) and k[0] == 'ps')]
            writes = list(writes) + psr
        waits = {}
        seen = self.seen[eng]
        own = self._sid(self.sem[eng])

        def need(ev):
            if ev is None:
                return
            if isinstance(ev, list):
                for e1 in ev:
                    need(e1)
                return
            sid, val = ev
            if eng == 'pe' and sid == own:
                return
            if seen.get(sid, 0) >= val:
                return
            if waits.get(sid, 0) < val:
                waits[sid] = val

        for k in reads:
            need(self.w.get(k))
        for k in writes:
            need(self.w.get(k))
            rd = self.r.get(k)
            if rd:
                for ev in rd.items():
                    need(ev)
        if dma:
            i = self.dcnt[eng] % self.NDSEM
            self.dcnt[eng] += 1
            sem = self.dsem[eng][i]
            prev = self.dval[eng][i]
            if prev > 0:
                need((self._sid(sem), prev))
            val = prev + 16
            self.dval[eng][i] = val
            inc = 16
        elif noinc:
            sem = self.sem[eng]
            val = self.cnt[eng] + 1
            inc = 0
        else:
            sem = self.sem[eng]
            self.cnt[eng] += 1
            val = self.cnt[eng]
            inc = 1
        sid = self._sid(sem)
        for s, v in waits.items():
            seen[s] = v
        wl = [(self.semobj[s], v) for s, v in waits.items()]
        self.n_ops += 1
        self.n_waits += len(wl)

        def emit(e, wl=wl, fn=fn, sem=sem, inc=inc):
            for s, v in wl:
                e.wait_ge(s, v)
            ins = fn(e)
            if inc:
                ins.then_inc(sem, inc)

        self.ops[eng].append(emit)
        for k in reads:
            d = self.r.setdefault(k, {})
            if d.get(sid, 0) < val:
                d[sid] = val
        for k in writes:
            self.w[k] = (sid, val)
            self.r[k] = {}
        return (sid, val)

    def dma_group(self, eng, fns, reads=(), writes=()):
        waits = {}
        seen = self.seen[eng]

        def need(ev):
            if ev is None:
                return
            if isinstance(ev, list):
                for e1 in ev:
                    need(e1)
                return
            sid, val = ev
            if seen.get(sid, 0) >= val:
                return
            if waits.get(sid, 0) < val:
                waits[sid] = val

        for k in reads:
            need(self.w.get(k))
        for k in writes:
            need(self.w.get(k))
            rd = self.r.get(k)
            if rd:
                for ev in rd.items():
                    need(ev)
        evs = []
        for fn in fns:
            i = self.dcnt[eng] % self.NDSEM
            self.dcnt[eng] += 1
            sem = self.dsem[eng][i]
            prev = self.dval[eng][i]
            if prev > 0:
                need((self._sid(sem), prev))
            val = prev + 16
            self.dval[eng][i] = val
            for s_, v_ in waits.items():
                seen[s_] = v_
            wl = [(self.semobj[s_], v_) for s_, v_ in waits.items()]
            waits = {}
            self.n_ops += 1
            self.n_waits += len(wl)

            def emit(e, wl=wl, fn=fn, sem=sem):
                for s_, v_ in wl:
                    e.wait_ge(s_, v_)
                fn(e).then_inc(sem, 16)

            self.ops[eng].append(emit)
            evs.append((self._sid(sem), val))
        for k in reads:
            d = self.r.setdefault(k, {})
            for sid, val in evs:
                if d.get(sid, 0) < val:
                    d[sid] = val
        for k in writes:
            self.w[k] = list(evs)
            self.r[k] = {}

    def barrier(self):
        evs = []
        for e in self.eng:
            if self.cnt[e] > 0:
                evs.append((self._sid(self.sem[e]), self.cnt[e], e))
        for q in self.dsem:
            for i in range(self.NDSEM):
                if self.dval[q][i] > 0:
                    evs.append((self._sid(self.dsem[q][i]), self.dval[q][i], None))
        for eng in self.eng:
            wl = []
            for sid, val, src in evs:
                if eng == 'pe' and src == 'pe':
                    continue
                if self.seen[eng].get(sid, 0) >= val:
                    continue
                self.seen[eng][sid] = val
                wl.append((self.semobj[sid], val))

            def emit(e, wl=wl):
                for s, v in wl:
                    e.wait_ge(s, v)
            self.ops[eng].append(emit)
        self.w.clear()
        self.r.clear()

    def emit_all(self):
        nc = self.nc
        ops = self.ops
        with nc.Block() as block:
            @block.tensor
            def _(e):
                for f in ops['pe']:
                    f(e)

            @block.scalar
            def _(e):
                for f in ops['act']:
                    f(e)

            @block.vector
            def _(e):
                for f in ops['dve']:
                    f(e)

            @block.gpsimd
            def _(e):
                for f in ops['pool']:
                    f(e)

            @block.sync
            def _(e):
                for f in ops['sp']:
                    f(e)


_minrem = [1 << 30]
_uid = [0]


def _sbt(nc, name, shape, dt):
    _uid[0] += 1
    return nc.sbuf_tensor("%s_u%d" % (name, _uid[0]), shape, dt)


class Rot:
    def __init__(self, nc, stack, name, n, shape, dt):
        self.t = [stack.enter_context(_sbt(nc, "%s%d" % (name, i), list(shape), dt)) for i in range(n)]
        _minrem[0] = min(_minrem[0], nc.sbuf_bytes_remaining)
        self.k = [(name, i) for i in range(n)]
        self.i = 0

    def next(self):
        j = self.i % len(self.t)
        self.i += 1
        return self.t[j], self.k[j]


class RotI:
    def __init__(self, items):
        self.items = list(items)
        self.i = 0

    def next(self):
        j = self.i % len(self.items)
        self.i += 1
        return self.items[j]


def build(n_layers=DEPTH, dbg=False, stop_after=None, run_phases='BCDEF'):
    nc = bass.Bass("TRN2", target_bir_lowering=False)
    skind = "ExternalOutput" if dbg else "Internal"

    def din(name, shape, dt=F32):
        return nc.dram_tensor(name, list(shape), dt, kind="ExternalInput").ap()

    def dscr(name, shape, dt):
        return nc.dram_tensor(name, list(shape), dt, kind=skind).ap()

    h0 = din("h0", [T, D])
    w_in = din("w_in", [DEPTH, D, DIN])
    w_br_sb = din("w_br_sb", [DEPTH, 256, D])
    w_br_sp = din("w_br_sp", [DEPTH, 256, D])
    w_br_df = din("w_br_df", [DEPTH, 512, D])
    w_out = din("w_out", [DEPTH, D, D])
    w_up = din("w_up", [DEPTH, D, 2 * DFF])
    w_down = din("w_down", [DEPTH, DFF, D])
    p_an = din("p_an", [DEPTH, 128, 8])
    p_fn = din("p_fn", [DEPTH, 128, 8])
    p_bg = din("p_bg", [DEPTH, 128, 24])
    p_gn = din("p_gn", [DEPTH, 128, 4])
    p_sub = din("p_sub", [DEPTH, 128, 1])
    p_lam = din("p_lam", [DEPTH, 128, 256])
    p_cw = din("p_cw", [DEPTH, 128, NFC * 3])
    p_cb = din("p_cb", [DEPTH, 128, NFC])
    p_eb = din("p_eb", [8, 128, 2560])
    p_b31 = din("p_b31", [128, 8])
    c_b = din("c_b", [128, CB_W])
    c_f = din("c_f", [128, CF_W])
    out = nc.dram_tensor("out", [SEQ, D], F32, kind="ExternalOutput").ap()

    hT = dscr("hT", [128, 8 * T], F32)
    zT = dscr("zT", [19 * 128, T], BF16)
    gT = dscr("gT", [128, 24 * T], BF16)
    v_all = dscr("v_all", [128, NB * 1024], BF16)
    wix_d = dscr("wix", [128, NB * 8], F32)
    yT = dscr("yT", [128, 8 * T], BF16)
    maskT = dscr("maskT", [128, NB * T], BF16)

    hT_v = hT.rearrange("p (c t) -> p c t", c=8)
    g_v = gT.rearrange("p (c t) -> p c t", c=24)
    maskT_v = maskT.rearrange("p (k t) -> p k t", k=NB)
    actT = dscr("actT", [128, NFC * T], BF16)
    actT_v = actT.rearrange("p (c t) -> p c t", c=NFC)

    with ExitStack() as st:
        S = Sched(nc, st)
        ps = [st.enter_context(nc.psum_tensor("ps%d" % i, [128, 512], F32)) for i in range(8)]
        PK = [('ps', i) for i in range(8)]
        cb = st.enter_context(_sbt(nc, "cb", [128, CB_W], BF16))
        cf = st.enter_context(_sbt(nc, "cf", [128, CF_W], F32))
        b31 = st.enter_context(_sbt(nc, "b31", [128, 8], F32))

        def DMA(q, out_, in_, r=(), w=(), **kw):
            if len(out_.shape) == 3:
                assert len(in_.shape) == 3 and in_.shape[1] == out_.shape[1]
                fns = [(lambda e, i=i: e.dma_start(out=out_[:, i, :], in_=in_[:, i, :], **kw))
                       for i in range(out_.shape[1])]
                S.dma_group(q, fns, r, w)
            else:
                assert len(in_.shape) == 2, in_.shape
                S.op(q, lambda e: e.dma_start(out=out_, in_=in_, **kw), r, w, dma=True)

        def MM(out_, lhsT, rhs, start=True, stop=True, r=(), w=(), noinc=False):
            S.op('pe', lambda e: e.matmul(out_, lhsT=lhsT, rhs=rhs, start=start, stop=stop), r, w, noinc=noinc)

        def TR(out_, in_, ident, r=(), w=(), noinc=False):
            S.op('pe', lambda e: e.transpose(out_, in_, ident), r, w, noinc=noinc)

        def ACT(out_, in_, func, r=(), w=(), **kw):
            S.op('act', lambda e: e.activation(out=out_, in_=in_, func=func, **kw), r, w)

        def TS(eng, out_, in0, s1, s2, op0, op1=None, r=(), w=(), **kw):
            if op1 is None:
                S.op(eng, lambda e: e.tensor_scalar(out=out_, in0=in0, scalar1=s1, scalar2=None, op0=op0, **kw), r, w)
            else:
                S.op(eng, lambda e: e.tensor_scalar(out=out_, in0=in0, scalar1=s1, scalar2=s2, op0=op0, op1=op1, **kw), r, w)

        def STT(out_, in0, scalar, in1, op0, op1, r=(), w=()):
            S.op('dve', lambda e: e.scalar_tensor_tensor(out=out_, in0=in0, scalar=scalar, in1=in1, op0=op0, op1=op1), r, w)

        def TT(eng, out_, in0, in1, op, r=(), w=()):
            S.op(eng, lambda e: e.tensor_tensor(out=out_, in0=in0, in1=in1, op=op), r, w)

        def CP(eng, out_, in_, r=(), w=()):
            if eng == 'act':
                S.op('act', lambda e: e.copy(out=out_, in_=in_), r, w)
            else:
                S.op(eng, lambda e: e.tensor_copy(out=out_, in_=in_), r, w)

        def MSET(eng, ap, val, w=()):
            S.op(eng, lambda e: e.memset(ap, val), (), w)

        def RED(out_, in_, op, r=(), w=()):
            S.op('dve', lambda e: e.tensor_reduce(out=out_, in_=in_, axis=AX.X, op=op), r, w)

        DMA('pool', cb[:], c_b, w=['cb'])
        DMA('sp', cf[:], c_f, w=['cf'])
        DMA('sp', b31[:], p_b31, w=['b31'])
        ones_bf = cb[:, CB_ONES:CB_ONES + 128]
        negones_bf = cb[:, CB_NEGONES:CB_NEGONES + 128]
        negtri_bf = cb[:, CB_NEGTRI:CB_NEGTRI + 128]
        mstrict_bf = cb[:, CB_MSTRICT:CB_MSTRICT + 128]
        zeros_bf = cb[:, CB_ZEROS:CB_ZEROS + 128]
        ident_bf = cb[:, CB_IDENT:CB_IDENT + 128]
        ones512_bf = cb[:, CB_ONES512:CB_ONES512 + 512]
        ident_f = cf[:, CF_IDENT:CF_IDENT + 128]
        onesD_f = cf[:, CF_ONESD:CF_ONESD + 128]
        blk64_f = cf[:, CF_BLK64:CF_BLK64 + 128]
        ones128th_f = cf[:, CF_ONES128TH:CF_ONES128TH + 128]
        mneg_f = cf[:, CF_MNEG:CF_MNEG + 128]

        evac_rr = RotI(['act', 'dve'])

        def phase0():
            with ExitStack() as ph:
                xin = Rot(nc, ph, 'xin', 3, [128, D], F32)
                hto = Rot(nc, ph, 'hto', 2, [128, 8, 512], F32)
                banks = RotI([0, 1, 2, 3])
                for (t0, n) in TILES:
                    ht, hk = hto.next()
                    for bi in range(n // 128):
                        b = t0 // 128 + bi
                        xt, xk = xin.next()
                        DMA('sp', xt[:], h0[b * 128:(b + 1) * 128, :], w=[xk])
                        for half in range(2):
                            pb = banks.next()
                            for j in range(4):
                                c = half * 4 + j
                                TR(ps[pb][:, j * 128:(j + 1) * 128], xt[:, c * 128:(c + 1) * 128], ident_f,
                                   r=[xk, 'cf'], w=[PK[pb]], noinc=(j < 3))
                            CP(evac_rr.next(), ht[:, half * 4:(half + 1) * 4, bi * 128:(bi + 1) * 128],
                               ps[pb][:, :].rearrange("p (c t) -> p c t", c=4), r=[PK[pb]], w=[hk])
                    DMA('sp', hT_v[:, :, t0:t0 + n], ht[:, :, :n], r=[hk])
            S.barrier()

        def phase_out():
            with ExitStack() as ph:
                hin = Rot(nc, ph, 'hin', 2, [128, 8, 512], F32)
                oto = Rot(nc, ph, 'oto', 3, [128, D], F32)
                banks = RotI([0, 1, 2, 3])
                for (t0, n) in TILES:
                    ht, hk = hin.next()
                    DMA('sp', ht[:, :, :n], hT_v[:, :, t0:t0 + n], w=[hk])
                    for bi in range(n // 128):
                        b = t0 // 128 + bi
                        ot, ok = oto.next()
                        for half in range(2):
                            pb = banks.next()
                            for j in range(4):
                                c = half * 4 + j
                                TR(ps[pb][:, j * 128:(j + 1) * 128], ht[:, c, bi * 128:(bi + 1) * 128], ident_f,
                                   r=[hk, 'cf'], w=[PK[pb]], noinc=(j < 3))
                            CP(evac_rr.next(), ot[:, half * 512:(half + 1) * 512], ps[pb][:, :], r=[PK[pb]], w=[ok])
                        lo_tok = max(b * 128, NMETA)
                        hi_tok = min((b + 1) * 128, NMETA + SEQ)
                        DMA('sp', out[lo_tok - NMETA:hi_tok - NMETA, :], ot[lo_tok - b * 128:hi_tok - b * 128, :],
                            r=[ok], w=[('out', b)])
            S.barrier()

        def rms_tile(ht, hk, n, gn, gk, uT, uk, sqr, lnr, rsr, bank):
            for c in range(8):
                sq, sk = sqr.next()
                ACT(sq[:, :n], ht[:, c, :n], AF.Square, r=[hk], w=[sk])
                MM(ps[bank][:, :n], onesD_f, sq[:, :n], start=(c == 0), stop=(c == 7), r=[sk, 'cf'], w=[PK[bank]])
            lnv, lk = lnr.next()
            ACT(lnv[:, :n], ps[bank][:, :n], AF.Ln, r=[PK[bank]], w=[lk], bias=EPS)
            rs, rk = rsr.next()
            ACT(rs[:, :n], lnv[:, :n], AF.Exp, r=[lk], w=[rk], scale=-0.5)
            for c in range(8):
                STT(uT[:, c, :n], ht[:, c, :n], gn[:, c:c + 1], rs[:, :n], ALU.mult, ALU.mult,
                    r=[hk, gk, rk], w=[uk])

        def phaseA(l):
            with ExitStack() as ph:
                wbf = ph.enter_context(_sbt(nc, "wbf", [128, 8, DIN], BF16))
                for c in range(8):
                    DMA('pool', wbf[:, c, :], w_in[l, c * 128:(c + 1) * 128, :], w=[('wbf', c)], max_dma_last_dim=4096)
                WK = [('wbf', c) for c in range(8)]
                gn = ph.enter_context(_sbt(nc, "gnA", [128, 8], F32))
                g4 = ph.enter_context(_sbt(nc, "g4A", [128, 4], F32))
                bg = ph.enter_context(_sbt(nc, "bgA", [128, 24], F32))
                DMA('sp', gn[:], p_an[l], w=['gn'])
                DMA('sp', g4[:], p_gn[l], w=['g4'])
                DMA('sp', bg[:], p_bg[l], w=['bg'])
                TS('dve', g4[:, 1:2], g4[:, 1:2], 0.125, None, ALU.mult, r=['g4'], w=['g4'])
                TS('dve', g4[:, 3:4], g4[:, 3:4], 0.125, None, ALU.mult, r=['g4'], w=['g4'])
                htr = Rot(nc, ph, 'htA', 2, [128, 8, 512], F32)
                sqr = Rot(nc, ph, 'sqA', 2, [128, 512], F32)
                lnr = Rot(nc, ph, 'lnA', 2, [128, 512], F32)
                rsr = Rot(nc, ph, 'rsA', 2, [128, 512], F32)
                uTr = Rot(nc, ph, 'uTA', 2, [128, 8, 512], BF16)
                stg = Rot(nc, ph, 'stgA', 4, [128, 512], BF16)
                vstr = Rot(nc, ph, 'vstA', 2, [128, 1024], BF16)
                wstr = Rot(nc, ph, 'wstA', 2, [128, 8], F32)
                zbanks = RotI([0, 1, 2])
                mbanks = RotI([3, 4])
                tbanks = RotI([6, 7])
                def a_load(ti):
                    t0_, n_ = TILES[ti]
                    ht_, hk_ = htr.next()
                    DMA('sp', ht_[:, :, :n_], hT_v[:, :, t0_:t0_ + n_], w=[hk_])
                    return ht_, hk_, n_

                def a_norm(ld):
                    ht_, hk_, n_ = ld
                    uT_, uk_ = uTr.next()
                    rms_tile(ht_, hk_, n_, gn, 'gn', uT_, uk_, sqr, lnr, rsr, 5)
                    return uT_, uk_

                nxt_a = a_norm(a_load(0))
                for ti, (t0, n) in enumerate(TILES):
                    uT, uk = nxt_a
                    for ci, (col0, rows, kind, param) in enumerate(CHUNKS):
                        if ci == 0 and ti + 1 < len(TILES):
                            nxt_ld = a_load(ti + 1)
                        if ci == 19 and ti + 1 < len(TILES):
                            nxt_a = a_norm(nxt_ld)
                        zb = zbanks.next()
                        for c in range(8):
                            MM(ps[zb][:rows, :n], wbf[:, c, col0:col0 + rows], uT[:, c, :n],
                               start=(c == 0), stop=(c == 7), r=[WK[c], uk], w=[PK[zb]], noinc=(c < 7))
                        sg, sk = stg.next()
                        if kind == 'copy':
                            if evac_rr.next() == 'act':
                                ACT(sg[:rows, :n], ps[zb][:rows, :n], AF.Copy, r=[PK[zb]], w=[sk], scale=float(param))
                            else:
                                TS('dve', sg[:rows, :n], ps[zb][:rows, :n], float(param), None, ALU.mult,
                                   r=[PK[zb]], w=[sk])
                        elif kind == 'gate':
                            ACT(sg[:, :n], ps[zb][:, :n], AF.Sigmoid, r=[PK[zb], 'bg'], w=[sk],
                                bias=bg[:, param:param + 1])
                        else:
                            sq, sqk = sqr.next()
                            ACT(sq[:, :n], ps[zb][:, :n], AF.Square, r=[PK[zb]], w=[sqk])
                            mb = mbanks.next()
                            MM(ps[mb][:, :n], blk64_f, sq[:, :n], r=[sqk, 'cf'], w=[PK[mb]])
                            lnv, lk = lnr.next()
                            ACT(lnv[:, :n], ps[mb][:, :n], AF.Ln, r=[PK[mb]], w=[lk], bias=EPS)
                            rs, rk = rsr.next()
                            ACT(rs[:, :n], lnv[:, :n], AF.Exp, r=[lk], w=[rk], scale=-0.5)
                            STT(sg[:, :n], ps[zb][:, :n], g4[:, param:param + 1], rs[:, :n], ALU.mult, ALU.mult,
                                r=[PK[zb], 'g4', rk], w=[sk])
                        if kind == 'gate':
                            DMA('sp', g_v[:, param, t0:t0 + n], sg[:, :n], r=[sk])
                        else:
                            DMA('sp', zT[ci * 128:ci * 128 + rows, t0:t0 + n], sg[:rows, :n], r=[sk])
                    for b in range(n // 128):
                        vs, vk = vstr.next()
                        for (col0, wd, dst0) in ((512, 256, 0), (1280, 256, 256), (2856, 512, 512)):
                            tb = tbanks.next()
                            for c in range(8):
                                MM(ps[tb][:, :wd], uT[:, c, b * 128:(b + 1) * 128], wbf[:, c, col0:col0 + wd],
                                   start=(c == 0), stop=(c == 7), r=[WK[c], uk], w=[PK[tb]], noinc=(c < 7))
                            CP(evac_rr.next(), vs[:, dst0:dst0 + wd], ps[tb][:, :wd], r=[PK[tb]], w=[vk])
                        tb = tbanks.next()
                        for c in range(8):
                            MM(ps[tb][:, :8], uT[:, c, b * 128:(b + 1) * 128], wbf[:, c, 1824:1832],
                               start=(c == 0), stop=(c == 7), r=[WK[c], uk], w=[PK[tb]], noinc=(c < 7))
                        ws, wk = wstr.next()
                        CP('dve', ws[:, :], ps[tb][:, :8], r=[PK[tb]], w=[wk])
                        bb = (t0 + b * 128) // 128
                        DMA('sp', v_all[:, bb * 1024:(bb + 1) * 1024], vs[:, :], r=[vk])
                        DMA('sp', wix_d[:, bb * 8:(bb + 1) * 8], ws[:, :], r=[wk])
            assert _minrem[0] >= 33000, ('SBUF over the 192KiB partition', _minrem[0])
            _minrem[0] = 1 << 30
            S.barrier()

        v_view = v_all.rearrange("p (b c) -> p b c", c=1024)
        yT_v = yT.rearrange("p (c t) -> p c t", c=8)
        psbf = [ps[i][:, :].bitcast(BF16) for i in range(8)]

        def emit_pipelined(iters, skews):
            n = len(iters)
            for t in range(n + max(skews)):
                for j, sk in enumerate(skews):
                    i = t - sk
                    if 0 <= i < n:
                        iters[i][j]()

        def phaseB(l):
            with ExitStack() as ph:
                qr_ = Rot(nc, ph, 'qT_B', 2, [128, T], BF16)
                kzr = [Rot(nc, ph, 'kz0_B', 2, [128, T], BF16), Rot(nc, ph, 'kz1_B', 2, [128, T], BF16)]
                vr_ = Rot(nc, ph, 'vt_B', 2, [128, NB, 128], BF16)
                bufs = []
                for hp in range(2):
                    qTt, qk = qr_.next()
                    kz0, kk0 = kzr[0].next()
                    kz1, kk1 = kzr[1].next()
                    vt, vk = vr_.next()
                    DMA('sp', qTt[:], zT[(0 + hp) * 128:(1 + hp) * 128, :], w=[qk])
                    MSET('pool', kz0[64:128, :], 0.0, w=[(kk0, 'z')])
                    MSET('pool', kz1[0:64, :], 0.0, w=[(kk1, 'z')])
                    DMA('sp', kz0[0:64, :], zT[(2 + hp) * 128:(2 + hp) * 128 + 64, :], w=[(kk0, 'd')])
                    DMA('sp', kz1[64:128, :], zT[(2 + hp) * 128 + 64:(3 + hp) * 128, :], w=[(kk1, 'd')])
                    DMA('sp', vt[:], v_view[:, :, hp * 128:(hp + 1) * 128], w=[vk])
                    bufs.append((qTt, qk, (kz0, kz1), (kk0, kk1), vt, vk))
                R32 = ph.enter_context(_sbt(nc, "R32_B", [128, 512], F32))
                Rbr = Rot(nc, ph, 'RbfB', 3, [128, 512], BF16)
                er = Rot(nc, ph, 'eB', 3, [128, 512], F32)
                spr = Rot(nc, ph, 'spB', 5, [128, 512], BF16)
                Ar = Rot(nc, ph, 'AB', 4, [128, 512], BF16)
                ystr = Rot(nc, ph, 'yB', 2, [128, 512], BF16)
                zbanks = RotI([0, 1])
                lbanks = RotI([2, 3])
                obanks = RotI([4, 5])
                for hp in range(2):
                    qTt, qk, kzs, kks, vt, vk = bufs[hp]
                    for hh in range(2):
                        pb = 64 * hh
                        kTt = kzs[hh]
                        kkd, kkz = (kks[hh], 'd'), (kks[hh], 'z')
                        for (q0, n) in TILES_Q:
                            nkb = (q0 + n + 127) // 128
                            ob = obanks.next()
                            MM(ps[ob][:, :n], zeros_bf[:, 0:128], ones512_bf[:, :n], start=True, stop=False,
                               r=['cb'], w=[PK[ob]])
                            MSET('pool', R32[:, :n], 0.0, w=['R32'])
                            iters = []
                            order = list(reversed(range(nkb)))
                            rb_prev = [None]
                            for pi, kb in enumerate(order):
                                c0 = max(0, kb * 128 - q0)
                                nn = n - c0
                                diag = kb * 128 >= q0
                                first = (pi == 0)
                                lastp = (pi == len(order) - 1)
                                ksl = kTt[:, kb * 128:(kb + 1) * 128]
                                qsl = qTt[:, q0 + c0:q0 + n]
                                zb = zbanks.next()
                                lb = lbanks.next()
                                et, ek = er.next()
                                spt, spk = spr.next()
                                At, Ak = Ar.next()
                                rb_in = rb_prev[0]
                                if not lastp:
                                    rb_out = Rbr.next()
                                    rb_prev[0] = rb_out
                                else:
                                    rb_out = None

                                def S1(zb=zb, ksl=ksl, qsl=qsl, nn=nn, et=et, ek=ek):
                                    MM(ps[zb][:, :nn], ksl, qsl, r=[kkd, kkz, qk], w=[PK[zb]])
                                    ACT(et[:, :nn], ps[zb][:, :nn], AF.Exp, r=[PK[zb]], w=[ek])

                                def S1b(nn=nn, et=et, ek=ek, spt=spt, spk=spk, diag=diag):
                                    ACT(spt[:, :nn], et[:, :nn], AF.Ln, r=[ek], w=[spk], bias=1.0)
                                    if diag:
                                        TT('dve', spt[:, 0:min(128, nn)], spt[:, 0:min(128, nn)], mstrict_bf[:, 0:min(128, nn)], ALU.mult, r=[spk, 'cb'], w=[spk])

                                def SR(spt=spt, spk=spk, nn=nn, c0=c0, rb_out=rb_out, n=n):
                                    if rb_out is None:
                                        return
                                    TT('dve', R32[:, c0:n], R32[:, c0:n], spt[:, :nn], ALU.add, r=['R32', spk], w=['R32'])
                                    CP('dve', rb_out[0][:, :n], R32[:, :n], r=['R32'], w=[rb_out[1]])

                                def S2(lb=lb, ksl=ksl, qsl=qsl, nn=nn, spt=spt, spk=spk, first=first, rb_in=rb_in, c0=c0, n=n,
                                       At=At, Ak=Ak, diag=diag):
                                    MM(ps[lb][:, :nn], ksl, qsl, start=True, stop=False, r=[kkd, kkz, qk], w=[PK[lb]], noinc=True)
                                    MM(ps[lb][:, :nn], negtri_bf, spt[:, :nn], start=False, stop=first,
                                       r=['cb', spk], w=[PK[lb]], noinc=(not first))
                                    if not first:
                                        MM(ps[lb][:, :nn], negones_bf, rb_in[0][:, c0:n], start=False, stop=True,
                                           r=['cb', rb_in[1]], w=[PK[lb]])
                                    ACT(At[:, :nn], ps[lb][:, :nn], AF.Exp, r=[PK[lb]], w=[Ak])
                                    if diag:
                                        TT('dve', At[:, 0:min(128, nn)], At[:, 0:min(128, nn)], mstrict_bf[:, 0:min(128, nn)], ALU.mult, r=[Ak, 'cb'], w=[Ak])

                                def S3(ob=ob, kb=kb, c0=c0, n=n, nn=nn, At=At, Ak=Ak, hh=hh, lastp=lastp):
                                    MM(ps[ob][:, c0:n], vt[:, kb, :], At[:, :nn], start=False,
                                       stop=lastp, r=[vk, Ak], w=[PK[ob]])
                                    for _f in range(PE_FILL_B):
                                        MM(ps[6 + (_f % 2)][:, :512], ones_bf, ones512_bf, r=['cb'], w=[], noinc=True)

                                iters.append([S1, S2, S1b, SR, S3])
                            emit_pipelined(iters, [0, 2, 0, 1, 3])
                            ys, yk = ystr.next()
                            CP(evac_rr.next(), ys[pb:pb + 64, :n], ps[ob][pb:pb + 64, :n], r=[PK[ob]], w=[yk])
                            DMA('sp', yT_v[pb:pb + 64, hp, q0:q0 + n], ys[pb:pb + 64, :n], r=[yk])
            S.barrier()

        def softmax_pass(qTt, kz, r0, vt, vsl, dv, q0, n, eb, head, use_mask, o_dst, o_key, Rr):
            qt4 = q0 // 128
            nkb = (q0 + n + 127) // 128
            ob = Rr['ob'].next()
            lb = Rr['lb'].next()
            MM(ps[ob][:, :n], zeros_bf[:, :128], ones512_bf[:, :n], start=True, stop=False, r=['cb'], w=[PK[ob]])
            MM(ps[lb][:, :n], zeros_bf[:, :128], ones512_bf[:, :n], start=True, stop=False, r=['cb'], w=[PK[lb]])
            iters = []
            for kb in range(nkb):
                c0 = max(0, kb * 128 - q0)
                nn = n - c0
                delta = qt4 - kb
                sbk = Rr['sb'].next()
                P, Pk = Rr['pr'].next()
                P0k = None
                if delta <= 1:
                    P0, P0k = Rr['p0r'].next()
                else:
                    P0 = None
                if use_mask:
                    mk, mkk = Rr['mkr'].next()
                else:
                    mk, mkk = None, None
                last = (kb == nkb - 1)

                def S1(sbk=sbk, kb=kb, c0=c0, nn=nn, mk=mk, mkk=mkk):
                    MM(ps[sbk][:, :nn], kz[:, kb * 128:(kb + 1) * 128], qTt[:, q0 + c0:q0 + n],
                       r=list(Rr['kk']) + [Rr['qk']], w=[PK[sbk]])
                    if mk is not None:
                        DMA('sp', mk[:, :nn], maskT_v[:, kb, q0 + c0:q0 + n], w=[mkk])

                def S2(sbk=sbk, c0=c0, nn=nn, delta=delta, P=P, Pk=Pk, P0=P0, P0k=P0k, mk=mk, mkk=mkk):
                    if delta <= 1:
                        ACT(P0[:, :nn], ps[sbk][:, :nn], AF.Exp, r=[PK[sbk]], w=[P0k])
                        TT('dve', P[:, :nn], P0[:, :nn], eb[:, delta + 3, c0:n], ALU.mult, r=[P0k, 'eb'], w=[Pk])
                    else:
                        ACT(P[:, :nn], ps[sbk][:, :nn], AF.Exp, r=[PK[sbk], 'b31'], w=[Pk], bias=b31[:, head:head + 1])
                    if mk is not None:
                        TT('dve', P[:, :nn], P[:, :nn], mk[:, :nn], ALU.mult, r=[Pk, mkk], w=[Pk])

                def S3(kb=kb, c0=c0, nn=nn, P=P, Pk=Pk, last=last):
                    MM(ps[ob][:, c0:n], vt[:, kb, vsl], P[:, :nn], start=False, stop=last, r=[Rr['vk'], Pk], w=[PK[ob]],
                       noinc=True)
                    MM(ps[lb][:, c0:n], ones_bf[:, :128], P[:, :nn], start=False, stop=last, r=['cb', Pk], w=[PK[lb]])

                iters.append([S1, S2, S3])
            emit_pipelined(iters, [0, 2, 4])
            rl, rlk = Rr['rlr'].next()
            S.op('dve', lambda e: e.reciprocal(out=rl[r0:r0 + dv, :n], in_=ps[lb][r0:r0 + dv, :n]), [PK[lb]], [rlk])
            TT('dve', o_dst, ps[ob][r0:r0 + dv, :n], rl[r0:r0 + dv, :n], ALU.mult, r=[PK[ob], rlk], w=[o_key])

        def load_eb_dma(ebraw, head):
            DMA('sp', ebraw[:], p_eb[head], w=['ebraw'])

        def load_eb_exp(ebraw, eb):
            ACT(eb[:].rearrange("p a b -> p (a b)"), ebraw[:], AF.Exp, r=['ebraw'], w=['eb'])

        def phaseC(l):
            lam_init = 0.8 - 0.6 * math.exp(-0.3 * l)
            with ExitStack() as ph:
                sb_ = lambda name, shape, dt: ph.enter_context(_sbt(nc, name, list(shape), dt))
                lamv = sb_("lamv", [128, 256], F32)
                sub = sb_("sub", [128, 1], F32)
                prod = sb_("prod", [128, 128], F32)
                s12 = sb_("s12", [128, 2], F32)
                e12 = sb_("e12", [128, 2], F32)
                nlam = sb_("nlam", [128, 1], F32)
                subc = sb_("subc", [128, 1], F32)
                DMA('sp', lamv[:], p_lam[l], w=['lamv'])
                DMA('sp', sub[:], p_sub[l], w=['sub'])
                TT('dve', prod[:, 0:64], lamv[:, 0:64], lamv[:, 64:128], ALU.mult, r=['lamv'], w=['prod'])
                TT('dve', prod[:, 64:128], lamv[:, 128:192], lamv[:, 192:256], ALU.mult, r=['lamv'], w=['prod'])
                RED(s12[:, 0:1], prod[:, 0:64], ALU.add, r=['prod'], w=['s12'])
                RED(s12[:, 1:2], prod[:, 64:128], ALU.add, r=['prod'], w=['s12'])
                ACT(e12[:], s12[:], AF.Exp, r=['s12'], w=['e12'])
                TT('dve', nlam[:], e12[:, 1:2], e12[:, 0:1], ALU.subtract, r=['e12'], w=['nlam'])
                TS('dve', nlam[:], nlam[:], -lam_init, None, ALU.add, r=['nlam'], w=['nlam'])
                TS('dve', subc[:], sub[:], 1.0 - lam_init, None, ALU.mult, r=['sub'], w=['subc'])
                qr_ = Rot(nc, ph, 'qT_C', 2, [128, T], BF16)
                kzr = [Rot(nc, ph, 'kz0_C', 2, [128, T], BF16), Rot(nc, ph, 'kz1_C', 2, [128, T], BF16)]
                vr_ = Rot(nc, ph, 'vt_C', 2, [128, NB, 128], BF16)
                ebraw = sb_("ebraw_C", [128, 2560], F32)

                def prefetchC(h):
                    qTt, qk = qr_.next()
                    kz0, kk0 = kzr[0].next()
                    kz1, kk1 = kzr[1].next()
                    vt, vk = vr_.next()
                    DMA('sp', qTt[:], zT[(11 + h) * 128:(12 + h) * 128, :], w=[qk])
                    MSET('pool', kz0[64:128, :], 0.0, w=[(kk0, 'z')])
                    MSET('pool', kz1[0:64, :], 0.0, w=[(kk1, 'z')])
                    DMA('sp', kz0[0:64, :], zT[(15 + h) * 128:(15 + h) * 128 + 64, :], w=[(kk0, 'd')])
                    DMA('sp', kz1[64:128, :], zT[(15 + h) * 128 + 64:(16 + h) * 128, :], w=[(kk1, 'd')])
                    DMA('sp', vt[:], v_view[:, :, 512 + 128 * h:512 + 128 * (h + 1)], w=[vk])
                    load_eb_dma(ebraw, 4 + h)
                    return (qTt, qk, (kz0, kz1), (kk0, kk1), vt, vk)
                eb = sb_("eb_C", [128, 5, 512], BF16)
                Rr = {'sb': RotI([0, 1, 7]), 'ob': RotI([2, 4]), 'lb': RotI([3, 5]),
                      'pr': Rot(nc, ph, 'P_C', 6, [128, 512], BF16), 'p0r': Rot(nc, ph, 'P0_C', 4, [128, 512], BF16),
                      'rlr': Rot(nc, ph, 'rl_C', 2, [128, 512], F32)}
                o1r = Rot(nc, ph, 'o1_C', 2, [128, 512], F32)
                o2r = Rot(nc, ph, 'o2_C', 2, [128, 512], F32)
                yr = Rot(nc, ph, 'y_C', 2, [128, 512], F32)
                sqr = Rot(nc, ph, 'sq_C', 2, [128, 512], F32)
                lnr = Rot(nc, ph, 'ln_C', 2, [128, 512], F32)
                yor = Rot(nc, ph, 'yo_C', 2, [128, 512], BF16)
                nxt = prefetchC(0)
                for h in range(4):
                    qTt, qk, kzs, kks, vt, vk = nxt
                    Rr['qk'], Rr['vk'] = qk, vk
                    load_eb_exp(ebraw, eb)
                    if h + 1 < 4:
                        nxt = prefetchC(h + 1)
                    for (q0, n) in TILES_Q:
                        o1, o1k = o1r.next()
                        Rr['kk'] = [(kks[0], 'd'), (kks[0], 'z')]
                        softmax_pass(qTt, kzs[0], 0, vt, slice(0, 128), 128, q0, n, eb, 4 + h, False, o1[:, :n], o1k, Rr)
                        o2, o2k = o2r.next()
                        Rr['kk'] = [(kks[1], 'd'), (kks[1], 'z')]
                        softmax_pass(qTt, kzs[1], 0, vt, slice(0, 128), 128, q0, n, eb, 4 + h, False, o2[:, :n], o2k, Rr)
                        y, yk = yr.next()
                        STT(y[:, :n], o2[:, :n], nlam[:, 0:1], o1[:, :n], ALU.mult, ALU.add, r=[o1k, o2k, 'nlam'], w=[yk])
                        sq, sqk = sqr.next()
                        ACT(sq[:, :n], y[:, :n], AF.Square, r=[yk], w=[sqk])
                        MM(ps[6][:, :n], ones128th_f, sq[:, :n], r=['cf', sqk], w=[PK[6]])
                        lnv, lk = lnr.next()
                        ACT(lnv[:, :n], ps[6][:, :n], AF.Ln, r=[PK[6]], w=[lk], bias=EPS)
                        ACT(sq[:, :n], lnv[:, :n], AF.Exp, r=[lk], w=[sqk], scale=-0.5)
                        yo, yok = yor.next()
                        STT(yo[:, :n], y[:, :n], subc[:, 0:1], sq[:, :n], ALU.mult, ALU.mult, r=[yk, 'subc', sqk], w=[yok])
                        DMA('sp', yT_v[:, 4 + h, q0:q0 + n], yo[:, :n], r=[yok])
            assert _minrem[0] >= 33000, ('SBUF over the 192KiB partition', _minrem[0])
            _minrem[0] = 1 << 30
            S.barrier()

        def phaseD(l):
            with ExitStack() as ph:
                sb_ = lambda name, shape, dt: ph.enter_context(_sbt(nc, name, list(shape), dt))
                kix = sb_("kix", [32, T], BF16)
                wixa = sb_("wixa", [128, NB, 8], F32)
                aw = sb_("aw", [128, NB, 8], F32)
                sg = sb_("sg", [128, NB, 8], F32)
                DMA('sp', kix[:], zT[10 * 128:10 * 128 + 32, :], w=['kix'])
                DMA('sp', wixa[:].rearrange("p b j -> p (b j)"), wix_d, w=['wixa'])
                TS('dve', aw[:], wixa[:], -1.0, None, ALU.mult, r=['wixa'], w=['aw'])
                TT('dve', aw[:], aw[:], wixa[:], ALU.max, r=['wixa', 'aw'], w=['aw'])
                TS('dve', sg[:], wixa[:], 0.0, 2.0, ALU.is_ge, ALU.mult, r=['wixa'], w=['sg'])
                TS('dve', sg[:], sg[:], -1.0, None, ALU.add, r=['sg'], w=['sg'])
                qixr = Rot(nc, ph, 'qix', 2, [32, 8, 128], BF16)
                scr = Rot(nc, ph, 'sc', 2, [128, T], F32)
                rr = Rot(nc, ph, 'rl', 4, [128, 512], BF16)
                dgr = Rot(nc, ph, 'dg', 2, [128, 8, 128], BF16)
                scbanks = RotI([6, 7])
                junk = sb_("junk", [128, T], BF16)
                maskr = Rot(nc, ph, 'mk', 2, [128, T], BF16)
                mstr = Rot(nc, ph, 'mst', 3, [128, 512], BF16)
                hi = sb_("hi", [128, 1], F32)
                lo = sb_("lo", [128, 1], F32)
                w0 = sb_("w0", [128, 1], F32)
                mid = sb_("mid", [128, 1], F32)
                cnt = sb_("cnt", [128, 1], F32)
                tmp = sb_("tmp", [128, 1], F32)
                tauc = sb_("tauc", [128, 1], F32)
                MSET('dve', tauc[:], -1e29, w=['tauc'])
                xbanks = RotI([0, 1, 2, 3])
                tbanks = RotI([4, 5])
                d1state = {}

                def stageS(qb):
                        nk = 128 * (qb + 1)
                        qx, qxk = qixr.next()
                        for j in range(8):
                            DMA('sp', qx[:, j, :], zT[1024 + 32 * j:1024 + 32 * (j + 1), qb * 128:(qb + 1) * 128], w=[(qxk, j)])
                        sc, sck = scr.next()
                        allk = []
                        dg, dgk = dgr.next()
                        for j in range(8):
                            TS('dve', dg[:, j, :], ident_bf, sg[:, qb, j:j + 1], None, ALU.mult, r=['cb', 'sg'], w=[(dgk, j)])
                        iters = []
                        for s0 in range(0, nk, 512):
                            wd = min(512, nk - s0)
                            sk0 = (sck, s0)
                            allk.append(sk0)
                            scb = scbanks.next()
                            for j in range(8):
                                xb = xbanks.next()
                                rt, rtk = rr.next()

                                def S1(xb=xb, j=j, s0=s0, wd=wd, qx=qx, qxk=qxk):
                                    MM(ps[xb][:, :wd], qx[:, j, :], kix[:, s0:s0 + wd], r=[(qxk, j), 'kix'], w=[PK[xb]])

                                def S2(xb=xb, j=j, wd=wd, rt=rt, rtk=rtk, qb=qb):
                                    ACT(rt[:, :wd], ps[xb][:, :wd], AF.Relu, r=[PK[xb], 'aw'], w=[rtk], scale=aw[:, qb, j:j + 1])

                                def S3(scb=scb, j=j, wd=wd, rt=rt, rtk=rtk, dg=dg, dgk=dgk, sc=sc, sk0=sk0, s0=s0):
                                    MM(ps[scb][:, :wd], dg[:, j, :], rt[:, :wd], start=(j == 0), stop=(j == 7),
                                       r=[(dgk, j), rtk], w=[PK[scb]], noinc=(j < 7))
                                    if j == 7:
                                        CP('act', sc[:, s0:s0 + wd], ps[scb][:, :wd], r=[PK[scb]], w=[sk0])

                                iters.append([S1, S2, S3])
                        emit_pipelined(iters, [0, 0, 2])
                        d1state[qb] = (sc, sck, allk, nk)

                def stageB(qb):
                        sc, sck, allk, nk = d1state.pop(qb)
                        if nk > TOPK:
                            RED(hi[:], sc[:, :nk], ALU.max, r=allk, w=['hi'])
                            RED(lo[:], sc[:, :nk], ALU.min, r=allk, w=['lo'])
                        dk = (sck, ((nk - 128) // 512) * 512)
                        TT('dve', sc[:, nk - 128:nk], sc[:, nk - 128:nk], mneg_f, ALU.add, r=[dk, 'cf'], w=[dk])
                        if nk > TOPK:
                            TT('dve', w0[:], hi[:], lo[:], ALU.subtract, r=['hi', 'lo'], w=['w0'])
                            TS('dve', w0[:], w0[:], 1.001, 1e-6, ALU.mult, ALU.add, r=['w0'], w=['w0'])
                            for k in range(1, NIT + 1):
                                f = 2.0 ** -k
                                STT(mid[:], w0[:], f, lo[:], ALU.mult, ALU.add, r=['w0', 'lo'], w=['mid'])
                                S.op('dve', lambda e, nk=nk, sc=sc: e.tensor_scalar(
                                    out=junk[:, :nk], in0=sc[:, :nk], scalar1=mid[:, 0:1], scalar2=None,
                                    op0=ALU.is_ge, op1=ALU.add, accum_out=cnt[:, 0:1]), allk + ['mid'], ['junk', 'cnt'])
                                STT(tmp[:], cnt[:], TOPK - 0.5, w0[:], ALU.is_ge, ALU.mult, r=['cnt', 'w0'], w=['tmp'])
                                STT(lo[:], tmp[:], f, lo[:], ALU.mult, ALU.add, r=['tmp', 'lo'], w=['lo'])
                            tau, tauk = lo, 'lo'
                        else:
                            tau, tauk = tauc, 'tauc'
                        mk, mkk = maskr.next()
                        TS('dve', mk[:, :nk], sc[:, :nk], tau[:, 0:1], None, ALU.is_ge, r=allk + [tauk], w=[mkk])
                        for g0 in range(0, qb + 1, 4):
                            nb_ = min(4, qb + 1 - g0)
                            tb = tbanks.next()
                            for j in range(nb_):
                                kb = g0 + j
                                TR(psbf[tb][:, j * 128:(j + 1) * 128], mk[:, kb * 128:(kb + 1) * 128], ident_bf,
                                   r=[mkk, 'cb'], w=[PK[tb]], noinc=(j < nb_ - 1))
                            ms, msk = mstr.next()
                            CP('act', ms[:, :nb_ * 128], psbf[tb][:, :nb_ * 128], r=[PK[tb]], w=[msk])
                            DMA('sp', maskT_v[:, g0:g0 + nb_, qb * 128:(qb + 1) * 128],
                                ms[:, :nb_ * 128].rearrange("p (k t) -> p k t", k=nb_), r=[msk])
                assert _minrem[0] >= 33000, ('SBUF over the 192KiB partition', _minrem[0])
                _minrem[0] = 1 << 30

                nqb = NB if not SKIP_D1 else 0
                if nqb:
                    stageS(0)
                for qb in range(nqb):
                    if qb + 1 < nqb:
                        stageS(qb + 1)
                    stageB(qb)
            S.barrier()
            with ExitStack() as ph:
                sb_ = lambda name, shape, dt: ph.enter_context(_sbt(nc, name, list(shape), dt))
                qr_ = Rot(nc, ph, 'qT_D', 2, [128, T], BF16)
                kr_ = Rot(nc, ph, 'kz_D', 2, [128, T], BF16)
                vr_ = Rot(nc, ph, 'vt_D', 2, [128, NB, 128], BF16)
                ebraw = sb_("ebraw_D", [128, 2560], F32)
                qkcur = [None]

                def prefetchD(h):
                    if h % 2 == 0:
                        qTt, qk = qr_.next()
                        vt, vk = vr_.next()
                        DMA('sp', qTt[:], zT[(4 + h // 2) * 128:(5 + h // 2) * 128, :], w=[qk])
                        DMA('sp', vt[:], v_view[:, :, 256 + 128 * (h // 2):256 + 128 * (h // 2 + 1)], w=[vk])
                        qkcur[0] = (qTt, qk, vt, vk)
                    kz, kk = kr_.next()
                    r0 = 64 * (h % 2)
                    MSET('pool', kz[64 - r0:128 - r0, :], 0.0, w=[(kk, 'z')])
                    DMA('sp', kz[r0:r0 + 64, :], zT[(6 + h // 2) * 128 + r0:(6 + h // 2) * 128 + r0 + 64, :], w=[(kk, 'd')])
                    load_eb_dma(ebraw, h)
                    return qkcur[0] + (kz, kk)
                eb = sb_("eb_D", [128, 5, 512], BF16)
                Rr = {'sb': RotI([0, 1, 6, 7]), 'ob': RotI([2, 4]), 'lb': RotI([3, 5]),
                      'pr': Rot(nc, ph, 'P_D', 6, [128, 512], BF16), 'p0r': Rot(nc, ph, 'P0_D', 4, [128, 512], BF16),
                      'rlr': Rot(nc, ph, 'rl_D', 2, [128, 512], F32), 'mkr': Rot(nc, ph, 'mk_D', 8, [128, 512], BF16)}
                yor = Rot(nc, ph, 'yo_D', 3, [128, 512], BF16)
                nh_ = 4 if not SKIP_D2 else 0
                pend_st = [None]
                if nh_:
                    nxt = prefetchD(0)
                for h in range(nh_):
                    qTt, qk, vt, vk, kz, kk = nxt
                    Rr['qk'], Rr['kk'], Rr['vk'] = qk, [(kk, 'd'), (kk, 'z')], vk
                    load_eb_exp(ebraw, eb)
                    if h + 1 < nh_:
                        nxt = prefetchD(h + 1)
                    for (q0, n) in TILES_Q:
                        yo, yok = yor.next()
                        r0 = 64 * (h % 2)
                        softmax_pass(qTt, kz, r0, vt, slice(0, 128), 64, q0, n, eb, h, True, yo[r0:r0 + 64, :n], yok, Rr)
                        if pend_st[0] is not None:
                            pend_st[0]()

                        def _store(yo=yo, yok=yok, r0=r0, h=h, q0=q0, n=n):
                            DMA('sp', yT_v[r0:r0 + 64, 2 + h // 2, q0:q0 + n], yo[r0:r0 + 64, :n], r=[yok])

                        pend_st[0] = _store
                if pend_st[0] is not None:
                    pend_st[0]()
                    pend_st[0] = None
            assert _minrem[0] >= 33000, ('SBUF over the 192KiB partition', _minrem[0])
            _minrem[0] = 1 << 30
            S.barrier()

        def phaseE(l):
            with ExitStack() as ph:
                sb_ = lambda name, shape, dt: ph.enter_context(_sbt(nc, name, list(shape), dt))
                wbr = sb_("wbr", [128, 8, D], BF16)
                wo = sb_("wo", [128, 8, D], BF16)
                for c in range(2):
                    DMA('pool', wbr[:, c, :], w_br_sb[l, c * 128:(c + 1) * 128, :], w=[('wbr', c)], max_dma_last_dim=4096)
                    DMA('pool', wbr[:, 2 + c, :], w_br_sp[l, c * 128:(c + 1) * 128, :], w=[('wbr', 2 + c)], max_dma_last_dim=4096)
                for c in range(4):
                    DMA('pool', wbr[:, 4 + c, :], w_br_df[l, c * 128:(c + 1) * 128, :], w=[('wbr', 4 + c)], max_dma_last_dim=4096)
                for c in range(8):
                    DMA('pool', wo[:, c, :], w_out[l, c * 128:(c + 1) * 128, :], w=[('wo', c)], max_dma_last_dim=4096)
                ytr = Rot(nc, ph, 'yt_E', 2, [128, 8, 512], BF16)
                gtr = Rot(nc, ph, 'gt_E', 2, [128, 24, 512], BF16)
                htr = Rot(nc, ph, 'ht_E', 2, [128, 8, 512], F32)
                mgr = Rot(nc, ph, 'mg_E', 2, [128, 8, 512], BF16)
                tar = Rot(nc, ph, 'ta_E', 2, [128, 512], F32)
                tbr = Rot(nc, ph, 'tb_E', 2, [128, 512], F32)
                tcr = Rot(nc, ph, 'tc_E', 2, [128, 512], F32)
                bbanks = RotI([0, 1, 2, 3, 4, 5])
                obanks = RotI([6, 7])
                def e_load(ti):
                    t0_, n_ = TILES_Q[ti]
                    yt_, ytk_ = ytr.next()
                    DMA('sp', yt_[:, :, :n_], yT_v[:, :, t0_:t0_ + n_], w=[ytk_])
                    gt_, gtk_ = gtr.next()
                    DMA('sp', gt_[:, :, :n_], g_v[:, :, t0_:t0_ + n_], w=[gtk_])
                    ht_, htk_ = htr.next()
                    DMA('sp', ht_[:, :, :n_], hT_v[:, :, t0_:t0_ + n_], w=[htk_])
                    return yt_, ytk_, gt_, gtk_, ht_, htk_

                nxt_e = e_load(0)
                for ti, (t0, n) in enumerate(TILES_Q):
                    yt, ytk, gt, gtk, ht, htk = nxt_e
                    if ti + 1 < len(TILES_Q):
                        nxt_e = e_load(ti + 1)
                    mg, mgk = mgr.next()
                    for fc in range(8):
                        b0, b1, b2 = bbanks.next(), bbanks.next(), bbanks.next()
                        fs = slice(fc * 128, (fc + 1) * 128)
                        for c in range(2):
                            MM(ps[b0][:, :n], wbr[:, c, fs], yt[:, c, :n], start=(c == 0), stop=(c == 1),
                               r=[('wbr', c), ytk], w=[PK[b0]], noinc=(c < 1))
                        for c in range(2, 4):
                            MM(ps[b1][:, :n], wbr[:, c, fs], yt[:, c, :n], start=(c == 2), stop=(c == 3),
                               r=[('wbr', c), ytk], w=[PK[b1]], noinc=(c < 3))
                        for c in range(4, 8):
                            MM(ps[b2][:, :n], wbr[:, c, fs], yt[:, c, :n], start=(c == 4), stop=(c == 7),
                               r=[('wbr', c), ytk], w=[PK[b2]], noinc=(c < 7))
                        ta, tak = tar.next()
                        tb_, tbk = tbr.next()
                        tc_, tck = tcr.next()
                        TT('dve', ta[:, :n], ps[b0][:, :n], gt[:, fc, :n], ALU.mult, r=[PK[b0], gtk], w=[tak])
                        TT('dve', tb_[:, :n], ps[b1][:, :n], gt[:, 8 + fc, :n], ALU.mult, r=[PK[b1], gtk], w=[tbk])
                        TT('dve', tc_[:, :n], ps[b2][:, :n], gt[:, 16 + fc, :n], ALU.mult, r=[PK[b2], gtk], w=[tck])
                        TT('pool', ta[:, :n], ta[:, :n], tb_[:, :n], ALU.add, r=[tak, tbk], w=[tak])
                        TT('pool', mg[:, fc, :n], ta[:, :n], tc_[:, :n], ALU.add, r=[tak, tck], w=[(mgk, fc)])
                    for oc in range(8):
                        bo = obanks.next()
                        for fc in range(8):
                            MM(ps[bo][:, :n], wo[:, fc, oc * 128:(oc + 1) * 128], mg[:, fc, :n], start=(fc == 0),
                               stop=(fc == 7), r=[('wo', fc), (mgk, fc)], w=[PK[bo]], noinc=(fc < 7))
                        TT('dve', ht[:, oc, :n], ps[bo][:, :n], ht[:, oc, :n], ALU.add, r=[PK[bo], htk], w=[htk])
                    DMA('sp', hT_v[:, :, t0:t0 + n], ht[:, :, :n], r=[htk])
            assert _minrem[0] >= 33000, ('SBUF over the 192KiB partition', _minrem[0])
            _minrem[0] = 1 << 30
            S.barrier()

        def phaseF(l):
            with ExitStack() as ph:
                sb_ = lambda name, shape, dt: ph.enter_context(_sbt(nc, name, list(shape), dt))
                wup = sb_("wup", [128, 8, 2 * DFF], BF16)
                for c in range(8):
                    DMA('pool', wup[:, c, :], w_up[l, c * 128:(c + 1) * 128, :], w=[('wup', c)], max_dma_last_dim=4096)
                WUK = [('wup', c) for c in range(8)]
                gn = sb_("gnF", [128, 8], F32)
                cw = sb_("cwF", [128, NFC * 3], F32)
                cbs = sb_("cbF", [128, NFC], F32)
                halo = sb_("haloF", [128, NFC, 2], F32)
                DMA('sp', gn[:], p_fn[l], w=['gn'])
                DMA('sp', cw[:], p_cw[l], w=['cw'])
                DMA('sp', cbs[:], p_cb[l], w=['cbs'])
                MSET('pool', halo[:].rearrange("p a b -> p (a b)"), 0.0, w=[('halo', fc) for fc in range(NFC)])
                htr = Rot(nc, ph, 'ht_F', 1, [128, 8, 512], F32)
                uTr = Rot(nc, ph, 'uT_F', 2, [128, 8, 512], BF16)
                actr = Rot(nc, ph, 'act_F', 4, [128, 512], BF16)
                sqr = Rot(nc, ph, 'sq_F', 2, [128, 512], F32)
                lnr = Rot(nc, ph, 'ln_F', 1, [128, 512], F32)
                rsr = Rot(nc, ph, 'rs_F', 1, [128, 512], F32)
                gtr = Rot(nc, ph, 'g_F', 3, [128, 514], F32)
                ccr = Rot(nc, ph, 'cc_F', 3, [128, 512], F32)
                ssr = Rot(nc, ph, 'ss_F', 3, [128, 512], F32)
                gbanks = RotI([0, 1, 2])
                vbanks = RotI([3, 4, 6, 7])
                def f1_load(ti):
                    t0_, n_ = TILES_Q[ti]
                    ht, htk = htr.next()
                    DMA('sp', ht[:, :, :n_], hT_v[:, :, t0_:t0_ + n_], w=[htk])
                    return ht, htk, n_

                def f1_norm(ld):
                    ht, htk, n_ = ld
                    uT_, uk_ = uTr.next()
                    rms_tile(ht, htk, n_, gn, 'gn', uT_, uk_, sqr, lnr, rsr, 5)
                    return uT_, uk_

                nxt_u = f1_norm(f1_load(0))
                for ti, (t0, n) in enumerate(TILES_Q):
                    uT, uk = nxt_u
                    nn = n
                    for fc in range(NFC):
                        if fc == 0 and ti + 1 < len(TILES_Q):
                            nxt_ld = f1_load(ti + 1)
                        if fc == 8 and ti + 1 < len(TILES_Q):
                            nxt_u = f1_norm(nxt_ld)
                        gb = gbanks.next()
                        vb = vbanks.next()
                        for c in range(8):
                            MM(ps[gb][:, :nn], wup[:, c, fc * 128:(fc + 1) * 128], uT[:, c, :nn], start=(c == 0),
                               stop=(c == 7), r=[WUK[c], uk], w=[PK[gb]], noinc=(c < 7))
                        for c in range(8):
                            MM(ps[vb][:, :nn], wup[:, c, DFF + fc * 128:DFF + (fc + 1) * 128], uT[:, c, :nn],
                               start=(c == 0), stop=(c == 7), r=[WUK[c], uk], w=[PK[vb]], noinc=(c < 7))
                        g_, gk = gtr.next()
                        CP('pool', g_[:, 0:2], halo[:, fc, :], r=[('halo', fc)], w=[(gk, 'h')])
                        ACT(g_[:, 2:2 + nn], ps[gb][:, :nn], AF.Copy, r=[PK[gb]], w=[(gk, 'b')])
                        cc, cck = ccr.next()
                        ACT(cc[:, :nn], ps[gb][:, :nn], AF.Identity, r=[PK[gb], 'cw', 'cbs'], w=[cck],
                            scale=cw[:, fc * 3 + 2:fc * 3 + 3], bias=cbs[:, fc:fc + 1])
                        STT(cc[:, :nn], g_[:, 1:1 + nn], cw[:, fc * 3 + 1:fc * 3 + 2], cc[:, :nn], ALU.mult, ALU.add,
                            r=[(gk, 'h'), (gk, 'b'), 'cw', cck], w=[cck])
                        STT(cc[:, :nn], g_[:, 0:nn], cw[:, fc * 3:fc * 3 + 1], cc[:, :nn], ALU.mult, ALU.add,
                            r=[(gk, 'h'), (gk, 'b'), 'cw', cck], w=[cck])
                        CP('pool', halo[:, fc, :], g_[:, nn:nn + 2], r=[(gk, 'b'), (gk, 'h')], w=[('halo', fc)])
                        ss, ssk = ssr.next()
                        ACT(ss[:, :nn], cc[:, :nn], AF.Silu, r=[cck], w=[ssk])
                        at, atk = actr.next()
                        TT('dve', at[:, :nn], ss[:, :nn], ps[vb][:, :nn], ALU.mult, r=[ssk, PK[vb]], w=[atk])
                        DMA('sp', actT_v[:, fc, t0:t0 + n], at[:, :n], r=[atk])
            S.barrier()
            with ExitStack() as ph:
                sb_ = lambda name, shape, dt: ph.enter_context(_sbt(nc, name, list(shape), dt))
                wdn = sb_("wdn", [128, NFC, D], BF16)
                for c in range(NFC):
                    DMA('pool', wdn[:, c, :], w_down[l, c * 128:(c + 1) * 128, :], w=[('wdn', c)], max_dma_last_dim=4096)
                htr = Rot(nc, ph, 'ht_G', 2, [128, 8, 512], F32)
                actr = Rot(nc, ph, 'act_G', 2, [128, NFC, 512], BF16)
                obanks = RotI([0, 1, 2, 3])
                def f2_load(ti):
                    t0_, n_ = TILES_Q[ti]
                    ht_, htk_ = htr.next()
                    DMA('sp', ht_[:, :, :n_], hT_v[:, :, t0_:t0_ + n_], w=[htk_])
                    at_, atk_ = actr.next()
                    DMA('sp', at_[:, :, :n_], actT_v[:, :, t0_:t0_ + n_], w=[atk_])
                    return ht_, htk_, at_, atk_

                nxt_l = f2_load(0)
                for ti, (t0, n) in enumerate(TILES_Q):
                    ht, htk, at, atk = nxt_l
                    if ti + 1 < len(TILES_Q):
                        nxt_l = f2_load(ti + 1)
                    for oc in range(8):
                        bo = obanks.next()
                        for fc in range(NFC):
                            MM(ps[bo][:, :n], wdn[:, fc, oc * 128:(oc + 1) * 128], at[:, fc, :n], start=(fc == 0),
                               stop=(fc == NFC - 1), r=[('wdn', fc), atk], w=[PK[bo]], noinc=(fc < NFC - 1))
                        TT('dve', ht[:, oc, :n], ps[bo][:, :n], ht[:, oc, :n], ALU.add, r=[PK[bo], htk], w=[htk])
                    DMA('sp', hT_v[:, :, t0:t0 + n], ht[:, :, :n], r=[htk])
            S.barrier()

        phase0()
        for l in range(n_layers):
            phaseA(l)
            if stop_after == 'A':
                break
            if 'B' in run_phases:
                phaseB(l)
            if 'C' in run_phases:
                phaseC(l)
            if 'D' in run_phases:
                phaseD(l)
            if stop_after == 'D':
                break
            if 'E' in run_phases:
                phaseE(l)
            if 'F' in run_phases:
                phaseF(l)
        phase_out()
        S.emit_all()
        print("ops", S.n_ops, "waits", S.n_waits)
    return nc


def _bucket_idx():
    b = np.zeros(128, np.int64)
    for n in range(128):
        if n < 16:
            b[n] = n
        else:
            v = np.float32(np.log(np.float32(n) / np.float32(16.0))) / np.float32(math.log(128 / 16)) * np.float32(16.0)
            b[n] = min(16 + int(np.float32(v)), 31)
    return b


def _consts():
    cbv = np.zeros((128, CB_W), np.float32)
    p = np.arange(128)[:, None]
    j = np.arange(128)[None, :]
    cbv[:, CB_ONES:CB_ONES + 128] = 1.0
    cbv[:, CB_NEGONES:CB_NEGONES + 128] = -1.0
    cbv[:, CB_NEGTRI:CB_NEGTRI + 128] = np.where(p >= j, -1.0, 0.0)
    cbv[:, CB_MSTRICT:CB_MSTRICT + 128] = np.where(p < j, 1.0, 0.0)
    cbv[:, CB_IDENT:CB_IDENT + 128] = np.eye(128)
    cbv[:, CB_ONES512:CB_ONES512 + 512] = 1.0
    cfv = np.zeros((128, CF_W), np.float32)
    cfv[:, CF_IDENT:CF_IDENT + 128] = np.eye(128)
    cfv[:, CF_ONESD:CF_ONESD + 128] = 1.0 / D
    blk = np.zeros((128, 128), np.float32)
    blk[:64, :64] = 1.0 / 64
    blk[64:, 64:] = 1.0 / 64
    cfv[:, CF_BLK64:CF_BLK64 + 128] = blk
    cfv[:, CF_ONES128TH:CF_ONES128TH + 128] = 1.0 / 128
    cfv[:, CF_MNEG:CF_MNEG + 128] = np.where(j > p, -1e30, 0.0)
    return cbv, cfv


def _prep_shared(inp):
    f = lambda a: np.ascontiguousarray(np.asarray(a, dtype=np.float32))
    sh = {}
    for k in ('w_in', 'w_br_sb', 'w_br_sp', 'w_br_df', 'w_out', 'w_up', 'w_down'):
        sh[k] = f(inp[k])
    sh['p_an'] = f(np.asarray(inp['attn_norm']).reshape(DEPTH, 8, 128).transpose(0, 2, 1))
    sh['p_fn'] = f(np.asarray(inp['ffn_norm']).reshape(DEPTH, 8, 128).transpose(0, 2, 1))
    sh['p_bg'] = f(np.asarray(inp['b_gate']).reshape(DEPTH, 24, 128).transpose(0, 2, 1))
    g = np.stack([np.tile(np.asarray(inp[k]), (1, 2)) for k in ('q_norm_sp', 'k_norm_sp', 'q_norm_df', 'k_norm_df')],
                 axis=-1)
    sh['p_gn'] = f(g)
    sh['p_sub'] = f(np.asarray(inp['subln_df']).reshape(DEPTH, 128, 1))
    lam = np.concatenate([np.asarray(inp[k]) for k in ('lam_q1', 'lam_k1', 'lam_q2', 'lam_k2')], axis=-1)
    sh['p_lam'] = f(np.broadcast_to(lam[:, None, :], (DEPTH, 128, 256)))
    cw = np.asarray(inp['conv_w']).reshape(DEPTH, 3, NFC, 128).transpose(0, 3, 2, 1)
    sh['p_cw'] = f(cw.reshape(DEPTH, 128, NFC * 3))
    sh['p_cb'] = f(np.asarray(inp['conv_b']).reshape(DEPTH, NFC, 128).transpose(0, 2, 1))
    rb = np.asarray(inp['rel_bias'], dtype=np.float32)
    bidx = _bucket_idx()
    bv = np.full((8, 1151), -30000.0, np.float32)
    nn = np.arange(0, 640)
    bsel = np.where(nn < 128, bidx[np.minimum(nn, 127)], 31)
    bv[:, 511:] = rb[bsel, :].T
    pp = np.arange(128)[:, None, None]
    aa = np.arange(5)[None, :, None]
    cc = np.arange(512)[None, None, :]
    tidx = (127 + 128 * aa + cc - pp).reshape(128, 2560)
    sh['p_eb'] = f(bv[:, tidx])
    sh['p_b31'] = f(np.broadcast_to(rb[31][None, :], (128, 8)))
    cbv, cfv = _consts()
    sh['c_b'] = cbv
    sh['c_f'] = cfv
    return sh


def _h0(inp, b):
    x = np.asarray(inp['x'], dtype=np.float32)
    meta = np.asarray(inp['meta_tokens'], dtype=np.float32)
    h = np.zeros((T, D), np.float32)
    h[:NMETA] = meta
    h[NMETA:NMETA + SEQ] = x[b]
    return h


def kernel(**inputs):
    sh = _prep_shared(inputs)
    nc = build()
    in_maps = []
    for b in range(NCORES):
        m = dict(sh)
        m['h0'] = _h0(inputs, b)
        in_maps.append(m)
    res = run_bass_kernel_spmd(nc, in_maps, core_ids=list(range(NCORES)))
    return np.stack([np.asarray(r['out'], dtype=np.float32) for r in res.results], axis=0)
```

```python
import math
from contextlib import ExitStack

import numpy as np
import concourse.bass as bass
import concourse.mybir as mybir
from concourse.bass_utils import run_bass_kernel_spmd

F32 = mybir.dt.float32
BF16 = mybir.dt.bfloat16
AF = mybir.ActivationFunctionType
ALU = mybir.AluOpType
AX = mybir.AxisListType

D = 1024
SEQ = 4096
NMETA = 16
T = 4224
NB = 33
DEPTH = 2
DIN = 6440
DFF = 2816
NFC = 22
EPS = 1e-6
TOPK = 256
NIT = 15
TILES = [(i * 512, 512) for i in range(8)] + [(4096, 128)]
TILES_Q = [(i * 512, 512) for i in range(8)] + [(4096, NMETA + SEQ - 4096)]
NCORES = 8
SKIP_D1 = False
PE_FILL_B = 0
SKIP_D2 = False

CHUNKS = []
for i in range(2):
    CHUNKS.append((0 + 128 * i, 128, 'copy', 1.0))
for i in range(2):
    CHUNKS.append((256 + 128 * i, 128, 'copy', 0.125))
for i in range(2):
    CHUNKS.append((768 + 128 * i, 128, 'norm', 0))
for i in range(2):
    CHUNKS.append((1024 + 128 * i, 128, 'norm', 1))
for i in range(2):
    CHUNKS.append((1536 + 128 * i, 128, 'copy', 1.0))
CHUNKS.append((1792, 32, 'copy', 1.0))
for i in range(4):
    CHUNKS.append((1832 + 128 * i, 128, 'norm', 2))
for i in range(4):
    CHUNKS.append((2344 + 128 * i, 128, 'norm', 3))
for i in range(24):
    CHUNKS.append((3368 + 128 * i, 128, 'gate', i))
NCH = len(CHUNKS)

CB_ONES, CB_NEGONES, CB_NEGTRI, CB_MSTRICT, CB_ZEROS, CB_IDENT, CB_ONES512 = 0, 128, 256, 384, 512, 640, 768
CB_W = 768 + 512
CF_IDENT, CF_ONESD, CF_BLK64, CF_ONES128TH, CF_MNEG = 0, 128, 256, 384, 512
CF_W = 640


class Sched:
    NDSEM = 16

    def __init__(self, nc, stack):
        self.nc = nc
        self.eng = {'pe': nc.tensor, 'act': nc.scalar, 'dve': nc.vector,
                    'pool': nc.gpsimd, 'sp': nc.sync}
        self.ops = {k: [] for k in self.eng}
        self.sem = {k: stack.enter_context(nc.semaphore("s_" + k)) for k in self.eng}
        self.cnt = {k: 0 for k in self.eng}
        self.seen = {k: {} for k in self.eng}
        self.dsem, self.dval, self.dcnt = {}, {}, {}
        for q in ('sp', 'pool'):
            self.dsem[q] = [stack.enter_context(nc.semaphore("d_%s%d" % (q, i)))
                            for i in range(self.NDSEM)]
            self.dval[q] = [0] * self.NDSEM
            self.dcnt[q] = 0
        self.w, self.r, self.semobj = {}, {}, {}
        self.n_ops = 0
        self.n_waits = 0

    def _sid(self, sem):
        i = id(sem)
        self.semobj[i] = sem
        return i

    def op(self, eng, fn, reads=(), writes=(), dma=False, noinc=False):
        psr = [k for k in reads if isinstance(k, tuple) and k[0] == 'ps']
        if psr:
            reads = [k for k in reads if not (isinstance(k, tuple) and k[0] == 'ps')]
            writes = list(writes) + psr
        waits = {}
        seen = self.seen[eng]
        own = self._sid(self.sem[eng])

        def need(ev):
            if ev is None:
                return
            if isinstance(ev, list):
                for e1 in ev:
                    need(e1)
                return
            sid, val = ev
            if eng == 'pe' and sid == own:
                return
            if seen.get(sid, 0) >= val:
                return
            if waits.get(sid, 0) < val:
                waits[sid] = val

        for k in reads:
            need(self.w.get(k))
        for k in writes:
            need(self.w.get(k))
            rd = self.r.get(k)
            if rd:
                for ev in rd.items():
                    need(ev)
        if dma:
            i = self.dcnt[eng] % self.NDSEM
            self.dcnt[eng] += 1
            sem = self.dsem[eng][i]
            prev = self.dval[eng][i]
            if prev > 0:
                need((self._sid(sem), prev))
            val = prev + 16
            self.dval[eng][i] = val
            inc = 16
        elif noinc:
            sem = self.sem[eng]
            val = self.cnt[eng] + 1
            inc = 0
        else:
            sem = self.sem[eng]
            self.cnt[eng] += 1
            val = self.cnt[eng]
            inc = 1
        sid = self._sid(sem)
        for s, v in waits.items():
            seen[s] = v
        wl = [(self.semobj[s], v) for s, v in waits.items()]
        self.n_ops += 1
        self.n_waits += len(wl)

        def emit(e, wl=wl, fn=fn, sem=sem, inc=inc):
            for s, v in wl:
                e.wait_ge(s, v)
            ins = fn(e)
            if inc:
                ins.then_inc(sem, inc)

        self.ops[eng].append(emit)
        for k in reads:
            d = self.r.setdefault(k, {})
            if d.get(sid, 0) < val:
                d[sid] = val
        for k in writes:
            self.w[k] = (sid, val)
            self.r[k] = {}
        return (sid, val)

    def dma_group(self, eng, fns, reads=(), writes=()):
        waits = {}
        seen = self.seen[eng]

        def need(ev):
            if ev is None:
                return
            if isinstance(ev, list):
                for e1 in ev:
                    need(e1)
                return
            sid, val = ev
            if seen.get(sid, 0) >= val:
                return
            if waits.get(sid, 0) < val:
                waits[sid] = val

        for k in reads:
            need(self.w.get(k))
        for k in writes:
            need(self.w.get(k))
            rd = self.r.get(k)
            if rd:
                for ev in rd.items():
                    need(ev)
        evs = []
        for fn in fns:
            i = self.dcnt[eng] % self.NDSEM
            self.dcnt[eng] += 1
            sem = self.dsem[eng][i]
            prev = self.dval[eng][i]
            if prev > 0:
                need((self._sid(sem), prev))
            val = prev + 16
            self.dval[eng][i] = val
            for s_, v_ in waits.items():
                seen[s_] = v_
            wl = [(self.semobj[s_], v_) for s_, v_ in waits.items()]
            waits = {}
            self.n_ops += 1
            self.n_waits += len(wl)

            def emit(e, wl=wl, fn=fn, sem=sem):
                for s_, v_ in wl:
                    e.wait_ge(s_, v_)
                fn(e).then_inc(sem, 16)

            self.ops[eng].append(emit)
            evs.append((self._sid(sem), val))
        for k in reads:
            d = self.r.setdefault(k, {})
            for sid, val in evs:
                if d.get(sid, 0) < val:
                    d[sid] = val
        for k in writes:
            self.w[k] = list(evs)
            self.r[k] = {}

    def barrier(self):
        evs = []
        for e in self.eng:
            if self.cnt[e] > 0:
                evs.append((self._sid(self.sem[e]), self.cnt[e], e))
        for q in self.dsem:
            for i in range(self.NDSEM):
                if self.dval[q][i] > 0:
                    evs.append((self._sid(self.dsem[q][i]), self.dval[q][i], None))
        for eng in self.eng:
            wl = []
            for sid, val, src in evs:
                if eng == 'pe' and src == 'pe':
                    continue
                if self.seen[eng].get(sid, 0) >= val:
                    continue
                self.seen[eng][sid] = val
                wl.append((self.semobj[sid], val))

            def emit(e, wl=wl):
                for s, v in wl:
                    e.wait_ge(s, v)
            self.ops[eng].append(emit)
        self.w.clear()
        self.r.clear()

    def emit_all(self):
        nc = self.nc
        ops = self.ops
        with nc.Block() as block:
            @block.tensor
            def _(e):
                for f in ops['pe']:
                    f(e)

            @block.scalar
            def _(e):
                for f in ops['act']:
                    f(e)

            @block.vector
            def _(e):
                for f in ops['dve']:
                    f(e)

            @block.gpsimd
            def _(e):
                for f in ops['pool']:
                    f(e)

            @block.sync
            def _(e):
                for f in ops['sp']:
                    f(e)


_minrem = [1 << 30]
_uid = [0]


def _sbt(nc, name, shape, dt):
    _uid[0] += 1
    return nc.sbuf_tensor("%s_u%d" % (name, _uid[0]), shape, dt)


class Rot:
    def __init__(self, nc, stack, name, n, shape, dt):
        self.t = [stack.enter_context(_sbt(nc, "%s%d" % (name, i), list(shape), dt)) for i in range(n)]
        _minrem[0] = min(_minrem[0], nc.sbuf_bytes_remaining)
        self.k = [(name, i) for i in range(n)]
        self.i = 0

    def next(self):
        j = self.i % len(self.t)
        self.i += 1
        return self.t[j], self.k[j]


class RotI:
    def __init__(self, items):
        self.items = list(items)
        self.i = 0

    def next(self):
        j = self.i % len(self.items)
        self.i += 1
        return self.items[j]


def build(n_layers=DEPTH, dbg=False, stop_after=None, run_phases='BCDEF'):
    nc = bass.Bass("TRN2", target_bir_lowering=False)
    skind = "ExternalOutput" if dbg else "Internal"

    def din(name, shape, dt=F32):
        return nc.dram_tensor(name, list(shape), dt, kind="ExternalInput").ap()

    def dscr(name, shape, dt):
        return nc.dram_tensor(name, list(shape), dt, kind=skind).ap()

    h0 = din("h0", [T, D])
    w_in = din("w_in", [DEPTH, D, DIN])
    w_br_sb = din("w_br_sb", [DEPTH, 256, D])
    w_br_sp = din("w_br_sp", [DEPTH, 256, D])
    w_br_df = din("w_br_df", [DEPTH, 512, D])
    w_out = din("w_out", [DEPTH, D, D])
    w_up = din("w_up", [DEPTH, D, 2 * DFF])
    w_down = din("w_down", [DEPTH, DFF, D])
    p_an = din("p_an", [DEPTH, 128, 8])
    p_fn = din("p_fn", [DEPTH, 128, 8])
    p_bg = din("p_bg", [DEPTH, 128, 24])
    p_gn = din("p_gn", [DEPTH, 128, 4])
    p_sub = din("p_sub", [DEPTH, 128, 1])
    p_lam = din("p_lam", [DEPTH, 128, 256])
    p_cw = din("p_cw", [DEPTH, 128, NFC * 3])
    p_cb = din("p_cb", [DEPTH, 128, NFC])
    p_eb = din("p_eb", [8, 128, 2560])
    p_b31 = din("p_b31", [128, 8])
    c_b = din("c_b", [128, CB_W])
    c_f = din("c_f", [128, CF_W])
    out = nc.dram_tensor("out", [SEQ, D], F32, kind="ExternalOutput").ap()

    hT = dscr("hT", [128, 8 * T], F32)
    zT = dscr("zT", [19 * 128, T], BF16)
    gT = dscr("gT", [128, 24 * T], BF16)
    v_all = dscr("v_all", [128, NB * 1024], BF16)
    wix_d = dscr("wix", [128, NB * 8], F32)
    yT = dscr("yT", [128, 8 * T], BF16)
    maskT = dscr("maskT", [128, NB * T], BF16)

    hT_v = hT.rearrange("p (c t) -> p c t", c=8)
    g_v = gT.rearrange("p (c t) -> p c t", c=24)
    maskT_v = maskT.rearrange("p (k t) -> p k t", k=NB)
    actT = dscr("actT", [128, NFC * T], BF16)
    actT_v = actT.rearrange("p (c t) -> p c t", c=NFC)

    with ExitStack() as st:
        S = Sched(nc, st)
        ps = [st.enter_context(nc.psum_tensor("ps%d" % i, [128, 512], F32)) for i in range(8)]
        PK = [('ps', i) for i in range(8)]
        cb = st.enter_context(_sbt(nc, "cb", [128, CB_W], BF16))
        cf = st.enter_context(_sbt(nc, "cf", [128, CF_W], F32))
        b31 = st.enter_context(_sbt(nc, "b31", [128, 8], F32))

        def DMA(q, out_, in_, r=(), w=(), **kw):
            if len(out_.shape) == 3:
                assert len(in_.shape) == 3 and in_.shape[1] == out_.shape[1]
                fns = [(lambda e, i=i: e.dma_start(out=out_[:, i, :], in_=in_[:, i, :], **kw))
                       for i in range(out_.shape[1])]
                S.dma_group(q, fns, r, w)
            else:
                assert len(in_.shape) == 2, in_.shape
                S.op(q, lambda e: e.dma_start(out=out_, in_=in_, **kw), r, w, dma=True)

        def MM(out_, lhsT, rhs, start=True, stop=True, r=(), w=(), noinc=False):
            S.op('pe', lambda e: e.matmul(out_, lhsT=lhsT, rhs=rhs, start=start, stop=stop), r, w, noinc=noinc)

        def TR(out_, in_, ident, r=(), w=(), noinc=False):
            S.op('pe', lambda e: e.transpose(out_, in_, ident), r, w, noinc=noinc)

        def ACT(out_, in_, func, r=(), w=(), **kw):
            S.op('act', lambda e: e.activation(out=out_, in_=in_, func=func, **kw), r, w)

        def TS(eng, out_, in0, s1, s2, op0, op1=None, r=(), w=(), **kw):
            if op1 is None:
                S.op(eng, lambda e: e.tensor_scalar(out=out_, in0=in0, scalar1=s1, scalar2=None, op0=op0, **kw), r, w)
            else:
                S.op(eng, lambda e: e.tensor_scalar(out=out_, in0=in0, scalar1=s1, scalar2=s2, op0=op0, op1=op1, **kw), r, w)

        def STT(out_, in0, scalar, in1, op0, op1, r=(), w=()):
            S.op('dve', lambda e: e.scalar_tensor_tensor(out=out_, in0=in0, scalar=scalar, in1=in1, op0=op0, op1=op1), r, w)

        def TT(eng, out_, in0, in1, op, r=(), w=()):
            S.op(eng, lambda e: e.tensor_tensor(out=out_, in0=in0, in1=in1, op=op), r, w)

        def CP(eng, out_, in_, r=(), w=()):
            if eng == 'act':
                S.op('act', lambda e: e.copy(out=out_, in_=in_), r, w)
            else:
                S.op(eng, lambda e: e.tensor_copy(out=out_, in_=in_), r, w)

        def MSET(eng, ap, val, w=()):
            S.op(eng, lambda e: e.memset(ap, val), (), w)

        def RED(out_, in_, op, r=(), w=()):
            S.op('dve', lambda e: e.tensor_reduce(out=out_, in_=in_, axis=AX.X, op=op), r, w)

        DMA('pool', cb[:], c_b, w=['cb'])
        DMA('sp', cf[:], c_f, w=['cf'])
        DMA('sp', b31[:], p_b31, w=['b31'])
        ones_bf = cb[:, CB_ONES:CB_ONES + 128]
        negones_bf = cb[:, CB_NEGONES:CB_NEGONES + 128]
        negtri_bf = cb[:, CB_NEGTRI:CB_NEGTRI + 128]
        mstrict_bf = cb[:, CB_MSTRICT:CB_MSTRICT + 128]
        zeros_bf = cb[:, CB_ZEROS:CB_ZEROS + 128]
        ident_bf = cb[:, CB_IDENT:CB_IDENT + 128]
        ones512_bf = cb[:, CB_ONES512:CB_ONES512 + 512]
        ident_f = cf[:, CF_IDENT:CF_IDENT + 128]
        onesD_f = cf[:, CF_ONESD:CF_ONESD + 128]
        blk64_f = cf[:, CF_BLK64:CF_BLK64 + 128]
        ones128th_f = cf[:, CF_ONES128TH:CF_ONES128TH + 128]
        mneg_f = cf[:, CF_MNEG:CF_MNEG + 128]

        evac_rr = RotI(['act', 'dve'])

        def phase0():
            with ExitStack() as ph:
                xin = Rot(nc, ph, 'xin', 3, [128, D], F32)
                hto = Rot(nc, ph, 'hto', 2, [128, 8, 512], F32)
                banks = RotI([0, 1, 2, 3])
                for (t0, n) in TILES:
                    ht, hk = hto.next()
                    for bi in range(n // 128):
                        b = t0 // 128 + bi
                        xt, xk = xin.next()
                        DMA('sp', xt[:], h0[b * 128:(b + 1) * 128, :], w=[xk])
                        for half in range(2):
                            pb = banks.next()
                            for j in range(4):
                                c = half * 4 + j
                                TR(ps[pb][:, j * 128:(j + 1) * 128], xt[:, c * 128:(c + 1) * 128], ident_f,
                                   r=[xk, 'cf'], w=[PK[pb]], noinc=(j < 3))
                            CP(evac_rr.next(), ht[:, half * 4:(half + 1) * 4, bi * 128:(bi + 1) * 128],
                               ps[pb][:, :].rearrange("p (c t) -> p c t", c=4), r=[PK[pb]], w=[hk])
                    DMA('sp', hT_v[:, :, t0:t0 + n], ht[:, :, :n], r=[hk])
            S.barrier()

        def phase_out():
            with ExitStack() as ph:
                hin = Rot(nc, ph, 'hin', 2, [128, 8, 512], F32)
                oto = Rot(nc, ph, 'oto', 3, [128, D], F32)
                banks = RotI([0, 1, 2, 3])
                for (t0, n) in TILES:
                    ht, hk = hin.next()
                    DMA('sp', ht[:, :, :n], hT_v[:, :, t0:t0 + n], w=[hk])
                    for bi in range(n // 128):
                        b = t0 // 128 + bi
                        ot, ok = oto.next()
                        for half in range(2):
                            pb = banks.next()
                            for j in range(4):
                                c = half * 4 + j
                                TR(ps[pb][:, j * 128:(j + 1) * 128], ht[:, c, bi * 128:(bi + 1) * 128], ident_f,
                                   r=[hk, 'cf'], w=[PK[pb]], noinc=(j < 3))
                            CP(evac_rr.next(), ot[:, half * 512:(half + 1) * 512], ps[pb][:, :], r=[PK[pb]], w=[ok])
                        lo_tok = max(b * 128, NMETA)
                        hi_tok = min((b + 1) * 128, NMETA + SEQ)
                        DMA('sp', out[lo_tok - NMETA:hi_tok - NMETA, :], ot[lo_tok - b * 128:hi_tok - b * 128, :],
                            r=[ok], w=[('out', b)])
            S.barrier()

        def rms_tile(ht, hk, n, gn, gk, uT, uk, sqr, lnr, rsr, bank):
            for c in range(8):
                sq, sk = sqr.next()
                ACT(sq[:, :n], ht[:, c, :n], AF.Square, r=[hk], w=[sk])
                MM(ps[bank][:, :n], onesD_f, sq[:, :n], start=(c == 0), stop=(c == 7), r=[sk, 'cf'], w=[PK[bank]])
            lnv, lk = lnr.next()
            ACT(lnv[:, :n], ps[bank][:, :n], AF.Ln, r=[PK[bank]], w=[lk], bias=EPS)
            rs, rk = rsr.next()
            ACT(rs[:, :n], lnv[:, :n], AF.Exp, r=[lk], w=[rk], scale=-0.5)
            for c in range(8):
                STT(uT[:, c, :n], ht[:, c, :n], gn[:, c:c + 1], rs[:, :n], ALU.mult, ALU.mult,
                    r=[hk, gk, rk], w=[uk])

        def phaseA(l):
            with ExitStack() as ph:
                wbf = ph.enter_context(_sbt(nc, "wbf", [128, 8, DIN], BF16))
                for c in range(8):
                    DMA('pool', wbf[:, c, :], w_in[l, c * 128:(c + 1) * 128, :], w=[('wbf', c)], max_dma_last_dim=4096)
                WK = [('wbf', c) for c in range(8)]
                gn = ph.enter_context(_sbt(nc, "gnA", [128, 8], F32))
                g4 = ph.enter_context(_sbt(nc, "g4A", [128, 4], F32))
                bg = ph.enter_context(_sbt(nc, "bgA", [128, 24], F32))
                DMA('sp', gn[:], p_an[l], w=['gn'])
                DMA('sp', g4[:], p_gn[l], w=['g4'])
                DMA('sp', bg[:], p_bg[l], w=['bg'])
                TS('dve', g4[:, 1:2], g4[:, 1:2], 0.125, None, ALU.mult, r=['g4'], w=['g4'])
                TS('dve', g4[:, 3:4], g4[:, 3:4], 0.125, None, ALU.mult, r=['g4'], w=['g4'])
                htr = Rot(nc, ph, 'htA', 2, [128, 8, 512], F32)
                sqr = Rot(nc, ph, 'sqA', 2, [128, 512], F32)
                lnr = Rot(nc, ph, 'lnA', 2, [128, 512], F32)
                rsr = Rot(nc, ph, 'rsA', 2, [128, 512], F32)
                uTr = Rot(nc, ph, 'uTA', 2, [128, 8, 512], BF16)
                stg = Rot(nc, ph, 'stgA', 4, [128, 512], BF16)
                vstr = Rot(nc, ph, 'vstA', 2, [128, 1024], BF16)
                wstr = Rot(nc, ph, 'wstA', 2, [128, 8], F32)
                zbanks = RotI([0, 1, 2])
                mbanks = RotI([3, 4])
                tbanks = RotI([6, 7])
                def a_load(ti):
                    t0_, n_ = TILES[ti]
                    ht_, hk_ = htr.next()
                    DMA('sp', ht_[:, :, :n_], hT_v[:, :, t0_:t0_ + n_], w=[hk_])
                    return ht_, hk_, n_

                def a_norm(ld):
                    ht_, hk_, n_ = ld
                    uT_, uk_ = uTr.next()
                    rms_tile(ht_, hk_, n_, gn, 'gn', uT_, uk_, sqr, lnr, rsr, 5)
                    return uT_, uk_

                nxt_a = a_norm(a_load(0))
                for ti, (t0, n) in enumerate(TILES):
                    uT, uk = nxt_a
                    for ci, (col0, rows, kind, param) in enumerate(CHUNKS):
                        if ci == 0 and ti + 1 < len(TILES):
                            nxt_ld = a_load(ti + 1)
                        if ci == 19 and ti + 1 < len(TILES):
                            nxt_a = a_norm(nxt_ld)
                        zb = zbanks.next()
                        for c in range(8):
                            MM(ps[zb][:rows, :n], wbf[:, c, col0:col0 + rows], uT[:, c, :n],
                               start=(c == 0), stop=(c == 7), r=[WK[c], uk], w=[PK[zb]], noinc=(c < 7))
                        sg, sk = stg.next()
                        if kind == 'copy':
                            if evac_rr.next() == 'act':
                                ACT(sg[:rows, :n], ps[zb][:rows, :n], AF.Copy, r=[PK[zb]], w=[sk], scale=float(param))
                            else:
                                TS('dve', sg[:rows, :n], ps[zb][:rows, :n], float(param), None, ALU.mult,
                                   r=[PK[zb]], w=[sk])
                        elif kind == 'gate':
                            ACT(sg[:, :n], ps[zb][:, :n], AF.Sigmoid, r=[PK[zb], 'bg'], w=[sk],
                                bias=bg[:, param:param + 1])
                        else:
                            sq, sqk = sqr.next()
                            ACT(sq[:, :n], ps[zb][:, :n], AF.Square, r=[PK[zb]], w=[sqk])
                            mb = mbanks.next()
                            MM(ps[mb][:, :n], blk64_f, sq[:, :n], r=[sqk, 'cf'], w=[PK[mb]])
                            lnv, lk = lnr.next()
                            ACT(lnv[:, :n], ps[mb][:, :n], AF.Ln, r=[PK[mb]], w=[lk], bias=EPS)
                            rs, rk = rsr.next()
                            ACT(rs[:, :n], lnv[:, :n], AF.Exp, r=[lk], w=[rk], scale=-0.5)
                            STT(sg[:, :n], ps[zb][:, :n], g4[:, param:param + 1], rs[:, :n], ALU.mult, ALU.mult,
                                r=[PK[zb], 'g4', rk], w=[sk])
                        if kind == 'gate':
                            DMA('sp', g_v[:, param, t0:t0 + n], sg[:, :n], r=[sk])
                        else:
                            DMA('sp', zT[ci * 128:ci * 128 + rows, t0:t0 + n], sg[:rows, :n], r=[sk])
                    for b in range(n // 128):
                        vs, vk = vstr.next()
                        for (col0, wd, dst0) in ((512, 256, 0), (1280, 256, 256), (2856, 512, 512)):
                            tb = tbanks.next()
                            for c in range(8):
                                MM(ps[tb][:, :wd], uT[:, c, b * 128:(b + 1) * 128], wbf[:, c, col0:col0 + wd],
                                   start=(c == 0), stop=(c == 7), r=[WK[c], uk], w=[PK[tb]], noinc=(c < 7))
                            CP(evac_rr.next(), vs[:, dst0:dst0 + wd], ps[tb][:, :wd], r=[PK[tb]], w=[vk])
                        tb = tbanks.next()
                        for c in range(8):
                            MM(ps[tb][:, :8], uT[:, c, b * 128:(b + 1) * 128], wbf[:, c, 1824:1832],
                               start=(c == 0), stop=(c == 7), r=[WK[c], uk], w=[PK[tb]], noinc=(c < 7))
                        ws, wk = wstr.next()
                        CP('dve', ws[:, :], ps[tb][:, :8], r=[PK[tb]], w=[wk])
                        bb = (t0 + b * 128) // 128
                        DMA('sp', v_all[:, bb * 1024:(bb + 1) * 1024], vs[:, :], r=[vk])
                        DMA('sp', wix_d[:, bb * 8:(bb + 1) * 8], ws[:, :], r=[wk])
            assert _minrem[0] >= 33000, ('SBUF over the 192KiB partition', _minrem[0])
            _minrem[0] = 1 << 30
            S.barrier()

        v_view = v_all.rearrange("p (b c) -> p b c", c=1024)
        yT_v = yT.rearrange("p (c t) -> p c t", c=8)
        psbf = [ps[i][:, :].bitcast(BF16) for i in range(8)]

        def emit_pipelined(iters, skews):
            n = len(iters)
            for t in range(n + max(skews)):
                for j, sk in enumerate(skews):
                    i = t - sk
                    if 0 <= i < n:
                        iters[i][j]()

        def phaseB(l):
            with ExitStack() as ph:
                qr_ = Rot(nc, ph, 'qT_B', 2, [128, T], BF16)
                kzr = [Rot(nc, ph, 'kz0_B', 2, [128, T], BF16), Rot(nc, ph, 'kz1_B', 2, [128, T], BF16)]
                vr_ = Rot(nc, ph, 'vt_B', 2, [128, NB, 128], BF16)
                bufs = []
                for hp in range(2):
                    qTt, qk = qr_.next()
                    kz0, kk0 = kzr[0].next()
                    kz1, kk1 = kzr[1].next()
                    vt, vk = vr_.next()
                    DMA('sp', qTt[:], zT[(0 + hp) * 128:(1 + hp) * 128, :], w=[qk])
                    MSET('pool', kz0[64:128, :], 0.0, w=[(kk0, 'z')])
                    MSET('pool', kz1[0:64, :], 0.0, w=[(kk1, 'z')])
                    DMA('sp', kz0[0:64, :], zT[(2 + hp) * 128:(2 + hp) * 128 + 64, :], w=[(kk0, 'd')])
                    DMA('sp', kz1[64:128, :], zT[(2 + hp) * 128 + 64:(3 + hp) * 128, :], w=[(kk1, 'd')])
                    DMA('sp', vt[:], v_view[:, :, hp * 128:(hp + 1) * 128], w=[vk])
                    bufs.append((qTt, qk, (kz0, kz1), (kk0, kk1), vt, vk))
                R32 = ph.enter_context(_sbt(nc, "R32_B", [128, 512], F32))
                Rbr = Rot(nc, ph, 'RbfB', 3, [128, 512], BF16)
                er = Rot(nc, ph, 'eB', 3, [128, 512], F32)
                spr = Rot(nc, ph, 'spB', 5, [128, 512], BF16)
                Ar = Rot(nc, ph, 'AB', 4, [128, 512], BF16)
                ystr = Rot(nc, ph, 'yB', 2, [128, 512], BF16)
                zbanks = RotI([0, 1])
                lbanks = RotI([2, 3])
                obanks = RotI([4, 5])
                for hp in range(2):
                    qTt, qk, kzs, kks, vt, vk = bufs[hp]
                    for hh in range(2):
                        pb = 64 * hh
                        kTt = kzs[hh]
                        kkd, kkz = (kks[hh], 'd'), (kks[hh], 'z')
                        for (q0, n) in TILES_Q:
                            nkb = (q0 + n + 127) // 128
                            ob = obanks.next()
                            MM(ps[ob][:, :n], zeros_bf[:, 0:128], ones512_bf[:, :n], start=True, stop=False,
                               r=['cb'], w=[PK[ob]])
                            MSET('pool', R32[:, :n], 0.0, w=['R32'])
                            iters = []
                            order = list(reversed(range(nkb)))
                            rb_prev = [None]
                            for pi, kb in enumerate(order):
                                c0 = max(0, kb * 128 - q0)
                                nn = n - c0
                                diag = kb * 128 >= q0
                                first = (pi == 0)
                                lastp = (pi == len(order) - 1)
                                ksl = kTt[:, kb * 128:(kb + 1) * 128]
                                qsl = qTt[:, q0 + c0:q0 + n]
                                zb = zbanks.next()
                                lb = lbanks.next()
                                et, ek = er.next()
                                spt, spk = spr.next()
                                At, Ak = Ar.next()
                                rb_in = rb_prev[0]
                                if not lastp:
                                    rb_out = Rbr.next()
                                    rb_prev[0] = rb_out
                                else:
                                    rb_out = None

                                def S1(zb=zb, ksl=ksl, qsl=qsl, nn=nn, et=et, ek=ek):
                                    MM(ps[zb][:, :nn], ksl, qsl, r=[kkd, kkz, qk], w=[PK[zb]])
                                    ACT(et[:, :nn], ps[zb][:, :nn], AF.Exp, r=[PK[zb]], w=[ek])

                                def S1b(nn=nn, et=et, ek=ek, spt=spt, spk=spk, diag=diag):
                                    ACT(spt[:, :nn], et[:, :nn], AF.Ln, r=[ek], w=[spk], bias=1.0)
                                    if diag:
                                        TT('dve', spt[:, 0:min(128, nn)], spt[:, 0:min(128, nn)], mstrict_bf[:, 0:min(128, nn)], ALU.mult, r=[spk, 'cb'], w=[spk])

                                def SR(spt=spt, spk=spk, nn=nn, c0=c0, rb_out=rb_out, n=n):
                                    if rb_out is None:
                                        return
                                    TT('dve', R32[:, c0:n], R32[:, c0:n], spt[:, :nn], ALU.add, r=['R32', spk], w=['R32'])
                                    CP('dve', rb_out[0][:, :n], R32[:, :n], r=['R32'], w=[rb_out[1]])

                                def S2(lb=lb, ksl=ksl, qsl=qsl, nn=nn, spt=spt, spk=spk, first=first, rb_in=rb_in, c0=c0, n=n,
                                       At=At, Ak=Ak, diag=diag):
                                    MM(ps[lb][:, :nn], ksl, qsl, start=True, stop=False, r=[kkd, kkz, qk], w=[PK[lb]], noinc=True)
                                    MM(ps[lb][:, :nn], negtri_bf, spt[:, :nn], start=False, stop=first,
                                       r=['cb', spk], w=[PK[lb]], noinc=(not first))
                                    if not first:
                                        MM(ps[lb][:, :nn], negones_bf, rb_in[0][:, c0:n], start=False, stop=True,
                                           r=['cb', rb_in[1]], w=[PK[lb]])
                                    ACT(At[:, :nn], ps[lb][:, :nn], AF.Exp, r=[PK[lb]], w=[Ak])
                                    if diag:
                                        TT('dve', At[:, 0:min(128, nn)], At[:, 0:min(128, nn)], mstrict_bf[:, 0:min(128, nn)], ALU.mult, r=[Ak, 'cb'], w=[Ak])

                                def S3(ob=ob, kb=kb, c0=c0, n=n, nn=nn, At=At, Ak=Ak, hh=hh, lastp=lastp):
                                    MM(ps[ob][:, c0:n], vt[:, kb, :], At[:, :nn], start=False,
                                       stop=lastp, r=[vk, Ak], w=[PK[ob]])
                                    for _f in range(PE_FILL_B):
                                        MM(ps[6 + (_f % 2)][:, :512], ones_bf, ones512_bf, r=['cb'], w=[], noinc=True)

                                iters.append([S1, S2, S1b, SR, S3])
                            emit_pipelined(iters, [0, 2, 0, 1, 3])
                            ys, yk = ystr.next()
                            CP(evac_rr.next(), ys[pb:pb + 64, :n], ps[ob][pb:pb + 64, :n], r=[PK[ob]], w=[yk])
                            DMA('sp', yT_v[pb:pb + 64, hp, q0:q0 + n], ys[pb:pb + 64, :n], r=[yk])
            S.barrier()

        def softmax_pass(qTt, kz, r0, vt, vsl, dv, q0, n, eb, head, use_mask, o_dst, o_key, Rr):
            qt4 = q0 // 128
            nkb = (q0 + n + 127) // 128
            ob = Rr['ob'].next()
            lb = Rr['lb'].next()
            MM(ps[ob][:, :n], zeros_bf[:, :128], ones512_bf[:, :n], start=True, stop=False, r=['cb'], w=[PK[ob]])
            MM(ps[lb][:, :n], zeros_bf[:, :128], ones512_bf[:, :n], start=True, stop=False, r=['cb'], w=[PK[lb]])
            iters = []
            for kb in range(nkb):
                c0 = max(0, kb * 128 - q0)
                nn = n - c0
                delta = qt4 - kb
                sbk = Rr['sb'].next()
                P, Pk = Rr['pr'].next()
                P0k = None
                if delta <= 1:
                    P0, P0k = Rr['p0r'].next()
                else:
                    P0 = None
                if use_mask:
                    mk, mkk = Rr['mkr'].next()
                else:
                    mk, mkk = None, None
                last = (kb == nkb - 1)

                def S1(sbk=sbk, kb=kb, c0=c0, nn=nn, mk=mk, mkk=mkk):
                    MM(ps[sbk][:, :nn], kz[:, kb * 128:(kb + 1) * 128], qTt[:, q0 + c0:q0 + n],
                       r=list(Rr['kk']) + [Rr['qk']], w=[PK[sbk]])
                    if mk is not None:
                        DMA('sp', mk[:, :nn], maskT_v[:, kb, q0 + c0:q0 + n], w=[mkk])

                def S2(sbk=sbk, c0=c0, nn=nn, delta=delta, P=P, Pk=Pk, P0=P0, P0k=P0k, mk=mk, mkk=mkk):
                    if delta <= 1:
                        ACT(P0[:, :nn], ps[sbk][:, :nn], AF.Exp, r=[PK[sbk]], w=[P0k])
                        TT('dve', P[:, :nn], P0[:, :nn], eb[:, delta + 3, c0:n], ALU.mult, r=[P0k, 'eb'], w=[Pk])
                    else:
                        ACT(P[:, :nn], ps[sbk][:, :nn], AF.Exp, r=[PK[sbk], 'b31'], w=[Pk], bias=b31[:, head:head + 1])
                    if mk is not None:
                        TT('dve', P[:, :nn], P[:, :nn], mk[:, :nn], ALU.mult, r=[Pk, mkk], w=[Pk])

                def S3(kb=kb, c0=c0, nn=nn, P=P, Pk=Pk, last=last):
                    MM(ps[ob][:, c0:n], vt[:, kb, vsl], P[:, :nn], start=False, stop=last, r=[Rr['vk'], Pk], w=[PK[ob]],
                       noinc=True)
                    MM(ps[lb][:, c0:n], ones_bf[:, :128], P[:, :nn], start=False, stop=last, r=['cb', Pk], w=[PK[lb]])

                iters.append([S1, S2, S3])
            emit_pipelined(iters, [0, 2, 4])
            rl, rlk = Rr['rlr'].next()
            S.op('dve', lambda e: e.reciprocal(out=rl[r0:r0 + dv, :n], in_=ps[lb][r0:r0 + dv, :n]), [PK[lb]], [rlk])
            TT('dve', o_dst, ps[ob][r0:r0 + dv, :n], rl[r0:r0 + dv, :n], ALU.mult, r=[PK[ob], rlk], w=[o_key])

        def load_eb_dma(ebraw, head):
            DMA('sp', ebraw[:], p_eb[head], w=['ebraw'])

        def load_eb_exp(ebraw, eb):
            ACT(eb[:].rearrange("p a b -> p (a b)"), ebraw[:], AF.Exp, r=['ebraw'], w=['eb'])

        def phaseC(l):
            lam_init = 0.8 - 0.6 * math.exp(-0.3 * l)
            with ExitStack() as ph:
                sb_ = lambda name, shape, dt: ph.enter_context(_sbt(nc, name, list(shape), dt))
                lamv = sb_("lamv", [128, 256], F32)
                sub = sb_("sub", [128, 1], F32)
                prod = sb_("prod", [128, 128], F32)
                s12 = sb_("s12", [128, 2], F32)
                e12 = sb_("e12", [128, 2], F32)
                nlam = sb_("nlam", [128, 1], F32)
                subc = sb_("subc", [128, 1], F32)
                DMA('sp', lamv[:], p_lam[l], w=['lamv'])
                DMA('sp', sub[:], p_sub[l], w=['sub'])
                TT('dve', prod[:, 0:64], lamv[:, 0:64], lamv[:, 64:128], ALU.mult, r=['lamv'], w=['prod'])
                TT('dve', prod[:, 64:128], lamv[:, 128:192], lamv[:, 192:256], ALU.mult, r=['lamv'], w=['prod'])
                RED(s12[:, 0:1], prod[:, 0:64], ALU.add, r=['prod'], w=['s12'])
                RED(s12[:, 1:2], prod[:, 64:128], ALU.add, r=['prod'], w=['s12'])
                ACT(e12[:], s12[:], AF.Exp, r=['s12'], w=['e12'])
                TT('dve', nlam[:], e12[:, 1:2], e12[:, 0:1], ALU.subtract, r=['e12'], w=['nlam'])
                TS('dve', nlam[:], nlam[:], -lam_init, None, ALU.add, r=['nlam'], w=['nlam'])
                TS('dve', subc[:], sub[:], 1.0 - lam_init, None, ALU.mult, r=['sub'], w=['subc'])
                qr_ = Rot(nc, ph, 'qT_C', 2, [128, T], BF16)
                kzr = [Rot(nc, ph, 'kz0_C', 2, [128, T], BF16), Rot(nc, ph, 'kz1_C', 2, [128, T], BF16)]
                vr_ = Rot(nc, ph, 'vt_C', 2, [128, NB, 128], BF16)
                ebraw = sb_("ebraw_C", [128, 2560], F32)

                def prefetchC(h):
                    qTt, qk = qr_.next()
                    kz0, kk0 = kzr[0].next()
                    kz1, kk1 = kzr[1].next()
                    vt, vk = vr_.next()
                    DMA('sp', qTt[:], zT[(11 + h) * 128:(12 + h) * 128, :], w=[qk])
                    MSET('pool', kz0[64:128, :], 0.0, w=[(kk0, 'z')])
                    MSET('pool', kz1[0:64, :], 0.0, w=[(kk1, 'z')])
                    DMA('sp', kz0[0:64, :], zT[(15 + h) * 128:(15 + h) * 128 + 64, :], w=[(kk0, 'd')])
                    DMA('sp', kz1[64:128, :], zT[(15 + h) * 128 + 64:(16 + h) * 128, :], w=[(kk1, 'd')])
                    DMA('sp', vt[:], v_view[:, :, 512 + 128 * h:512 + 128 * (h + 1)], w=[vk])
                    load_eb_dma(ebraw, 4 + h)
                    return (qTt, qk, (kz0, kz1), (kk0, kk1), vt, vk)
                eb = sb_("eb_C", [128, 5, 512], BF16)
                Rr = {'sb': RotI([0, 1, 7]), 'ob': RotI([2, 4]), 'lb': RotI([3, 5]),
                      'pr': Rot(nc, ph, 'P_C', 6, [128, 512], BF16), 'p0r': Rot(nc, ph, 'P0_C', 4, [128, 512], BF16),
                      'rlr': Rot(nc, ph, 'rl_C', 2, [128, 512], F32)}
                o1r = Rot(nc, ph, 'o1_C', 2, [128, 512], F32)
                o2r = Rot(nc, ph, 'o2_C', 2, [128, 512], F32)
                yr = Rot(nc, ph, 'y_C', 2, [128, 512], F32)
                sqr = Rot(nc, ph, 'sq_C', 2, [128, 512], F32)
                lnr = Rot(nc, ph, 'ln_C', 2, [128, 512], F32)
                yor = Rot(nc, ph, 'yo_C', 2, [128, 512], BF16)
                nxt = prefetchC(0)
                for h in range(4):
                    qTt, qk, kzs, kks, vt, vk = nxt
                    Rr['qk'], Rr['vk'] = qk, vk
                    load_eb_exp(ebraw, eb)
                    if h + 1 < 4:
                        nxt = prefetchC(h + 1)
                    for (q0, n) in TILES_Q:
                        o1, o1k = o1r.next()
                        Rr['kk'] = [(kks[0], 'd'), (kks[0], 'z')]
                        softmax_pass(qTt, kzs[0], 0, vt, slice(0, 128), 128, q0, n, eb, 4 + h, False, o1[:, :n], o1k, Rr)
                        o2, o2k = o2r.next()
                        Rr['kk'] = [(kks[1], 'd'), (kks[1], 'z')]
                        softmax_pass(qTt, kzs[1], 0, vt, slice(0, 128), 128, q0, n, eb, 4 + h, False, o2[:, :n], o2k, Rr)
                        y, yk = yr.next()
                        STT(y[:, :n], o2[:, :n], nlam[:, 0:1], o1[:, :n], ALU.mult, ALU.add, r=[o1k, o2k, 'nlam'], w=[yk])
                        sq, sqk = sqr.next()
                        ACT(sq[:, :n], y[:, :n], AF.Square, r=[yk], w=[sqk])
                        MM(ps[6][:, :n], ones128th_f, sq[:, :n], r=['cf', sqk], w=[PK[6]])
                        lnv, lk = lnr.next()
                        ACT(lnv[:, :n], ps[6][:, :n], AF.Ln, r=[PK[6]], w=[lk], bias=EPS)
                        ACT(sq[:, :n], lnv[:, :n], AF.Exp, r=[lk], w=[sqk], scale=-0.5)
                        yo, yok = yor.next()
                        STT(yo[:, :n], y[:, :n], subc[:, 0:1], sq[:, :n], ALU.mult, ALU.mult, r=[yk, 'subc', sqk], w=[yok])
                        DMA('sp', yT_v[:, 4 + h, q0:q0 + n], yo[:, :n], r=[yok])
            assert _minrem[0] >= 33000, ('SBUF over the 192KiB partition', _minrem[0])
            _minrem[0] = 1 << 30
            S.barrier()

        def phaseD(l):
            with ExitStack() as ph:
                sb_ = lambda name, shape, dt: ph.enter_context(_sbt(nc, name, list(shape), dt))
                kix = sb_("kix", [32, T], BF16)
                wixa = sb_("wixa", [128, NB, 8], F32)
                aw = sb_("aw", [128, NB, 8], F32)
                sg = sb_("sg", [128, NB, 8], F32)
                DMA('sp', kix[:], zT[10 * 128:10 * 128 + 32, :], w=['kix'])
                DMA('sp', wixa[:].rearrange("p b j -> p (b j)"), wix_d, w=['wixa'])
                TS('dve', aw[:], wixa[:], -1.0, None, ALU.mult, r=['wixa'], w=['aw'])
                TT('dve', aw[:], aw[:], wixa[:], ALU.max, r=['wixa', 'aw'], w=['aw'])
                TS('dve', sg[:], wixa[:], 0.0, 2.0, ALU.is_ge, ALU.mult, r=['wixa'], w=['sg'])
                TS('dve', sg[:], sg[:], -1.0, None, ALU.add, r=['sg'], w=['sg'])
                qixr = Rot(nc, ph, 'qix', 2, [32, 8, 128], BF16)
                scr = Rot(nc, ph, 'sc', 2, [128, T], F32)
                rr = Rot(nc, ph, 'rl', 4, [128, 512], BF16)
                dgr = Rot(nc, ph, 'dg', 2, [128, 8, 128], BF16)
                scbanks = RotI([6, 7])
                junk = sb_("junk", [128, T], BF16)
                maskr = Rot(nc, ph, 'mk', 2, [128, T], BF16)
                mstr = Rot(nc, ph, 'mst', 3, [128, 512], BF16)
                hi = sb_("hi", [128, 1], F32)
                lo = sb_("lo", [128, 1], F32)
                w0 = sb_("w0", [128, 1], F32)
                mid = sb_("mid", [128, 1], F32)
                cnt = sb_("cnt", [128, 1], F32)
                tmp = sb_("tmp", [128, 1], F32)
                tauc = sb_("tauc", [128, 1], F32)
                MSET('dve', tauc[:], -1e29, w=['tauc'])
                xbanks = RotI([0, 1, 2, 3])
                tbanks = RotI([4, 5])
                d1state = {}

                def stageS(qb):
                        nk = 128 * (qb + 1)
                        qx, qxk = qixr.next()
                        for j in range(8):
                            DMA('sp', qx[:, j, :], zT[1024 + 32 * j:1024 + 32 * (j + 1), qb * 128:(qb + 1) * 128], w=[(qxk, j)])
                        sc, sck = scr.next()
                        allk = []
                        dg, dgk = dgr.next()
                        for j in range(8):
                            TS('dve', dg[:, j, :], ident_bf, sg[:, qb, j:j + 1], None, ALU.mult, r=['cb', 'sg'], w=[(dgk, j)])
                        iters = []
                        for s0 in range(0, nk, 512):
                            wd = min(512, nk - s0)
                            sk0 = (sck, s0)
                            allk.append(sk0)
                            scb = scbanks.next()
                            for j in range(8):
                                xb = xbanks.next()
                                rt, rtk = rr.next()

                                def S1(xb=xb, j=j, s0=s0, wd=wd, qx=qx, qxk=qxk):
                                    MM(ps[xb][:, :wd], qx[:, j, :], kix[:, s0:s0 + wd], r=[(qxk, j), 'kix'], w=[PK[xb]])

                                def S2(xb=xb, j=j, wd=wd, rt=rt, rtk=rtk, qb=qb):
                                    ACT(rt[:, :wd], ps[xb][:, :wd], AF.Relu, r=[PK[xb], 'aw'], w=[rtk], scale=aw[:, qb, j:j + 1])

                                def S3(scb=scb, j=j, wd=wd, rt=rt, rtk=rtk, dg=dg, dgk=dgk, sc=sc, sk0=sk0, s0=s0):
                                    MM(ps[scb][:, :wd], dg[:, j, :], rt[:, :wd], start=(j == 0), stop=(j == 7),
                                       r=[(dgk, j), rtk], w=[PK[scb]], noinc=(j < 7))
                                    if j == 7:
                                        CP('act', sc[:, s0:s0 + wd], ps[scb][:, :wd], r=[PK[scb]], w=[sk0])

                                iters.append([S1, S2, S3])
                        emit_pipelined(iters, [0, 0, 2])
                        d1state[qb] = (sc, sck, allk, nk)

                def stageB(qb):
                        sc, sck, allk, nk = d1state.pop(qb)
                        if nk > TOPK:
                            S.op('dve', lambda e, sc=sc, nk=nk: e.tensor_reduce(
                                out=hi[:], in_=sc[:, :nk], axis=AX.X, op=ALU.max, apply_absolute_value=True),
                                allk, ['hi'])
                            TS('dve', lo[:], hi[:], -1.0, None, ALU.mult, r=['hi'], w=['lo'])
                        dk = (sck, ((nk - 128) // 512) * 512)
                        TT('dve', sc[:, nk - 128:nk], sc[:, nk - 128:nk], mneg_f, ALU.add, r=[dk, 'cf'], w=[dk])
                        if nk > TOPK:
                            TS('dve', w0[:], hi[:], 2.002, 1e-6, ALU.mult, ALU.add, r=['hi'], w=['w0'])
                            for k in range(1, NIT + 1):
                                f = 2.0 ** -k
                                STT(mid[:], w0[:], f, lo[:], ALU.mult, ALU.add, r=['w0', 'lo'], w=['mid'])
                                S.op('dve', lambda e, nk=nk, sc=sc: e.tensor_scalar(
                                    out=junk[:, :nk], in0=sc[:, :nk], scalar1=mid[:, 0:1], scalar2=None,
                                    op0=ALU.is_ge, op1=ALU.add, accum_out=cnt[:, 0:1]), allk + ['mid'], ['junk', 'cnt'])
                                STT(tmp[:], cnt[:], TOPK - 0.5, w0[:], ALU.is_ge, ALU.mult, r=['cnt', 'w0'], w=['tmp'])
                                STT(lo[:], tmp[:], f, lo[:], ALU.mult, ALU.add, r=['tmp', 'lo'], w=['lo'])
                            tau, tauk = lo, 'lo'
                        else:
                            tau, tauk = tauc, 'tauc'
                        mk, mkk = maskr.next()
                        TS('dve', mk[:, :nk], sc[:, :nk], tau[:, 0:1], None, ALU.is_ge, r=allk + [tauk], w=[mkk])
                        for g0 in range(0, qb + 1, 4):
                            nb_ = min(4, qb + 1 - g0)
                            tb = tbanks.next()
                            for j in range(nb_):
                                kb = g0 + j
                                TR(psbf[tb][:, j * 128:(j + 1) * 128], mk[:, kb * 128:(kb + 1) * 128], ident_bf,
                                   r=[mkk, 'cb'], w=[PK[tb]], noinc=(j < nb_ - 1))
                            ms, msk = mstr.next()
                            CP('act', ms[:, :nb_ * 128], psbf[tb][:, :nb_ * 128], r=[PK[tb]], w=[msk])
                            DMA('sp', maskT_v[:, g0:g0 + nb_, qb * 128:(qb + 1) * 128],
                                ms[:, :nb_ * 128].rearrange("p (k t) -> p k t", k=nb_), r=[msk])
                assert _minrem[0] >= 33000, ('SBUF over the 192KiB partition', _minrem[0])
                _minrem[0] = 1 << 30

                nqb = NB if not SKIP_D1 else 0
                if nqb:
                    stageS(0)
                for qb in range(nqb):
                    if qb + 1 < nqb:
                        stageS(qb + 1)
                    stageB(qb)
            S.barrier()
            with ExitStack() as ph:
                sb_ = lambda name, shape, dt: ph.enter_context(_sbt(nc, name, list(shape), dt))
                qr_ = Rot(nc, ph, 'qT_D', 2, [128, T], BF16)
                kr_ = Rot(nc, ph, 'kz_D', 2, [128, T], BF16)
                vr_ = Rot(nc, ph, 'vt_D', 2, [128, NB, 128], BF16)
                ebraw = sb_("ebraw_D", [128, 2560], F32)
                qkcur = [None]

                def prefetchD(h):
                    if h % 2 == 0:
                        qTt, qk = qr_.next()
                        vt, vk = vr_.next()
                        DMA('sp', qTt[:], zT[(4 + h // 2) * 128:(5 + h // 2) * 128, :], w=[qk])
                        DMA('sp', vt[:], v_view[:, :, 256 + 128 * (h // 2):256 + 128 * (h // 2 + 1)], w=[vk])
                        qkcur[0] = (qTt, qk, vt, vk)
                    kz, kk = kr_.next()
                    r0 = 64 * (h % 2)
                    MSET('pool', kz[64 - r0:128 - r0, :], 0.0, w=[(kk, 'z')])
                    DMA('sp', kz[r0:r0 + 64, :], zT[(6 + h // 2) * 128 + r0:(6 + h // 2) * 128 + r0 + 64, :], w=[(kk, 'd')])
                    load_eb_dma(ebraw, h)
                    return qkcur[0] + (kz, kk)
                eb = sb_("eb_D", [128, 5, 512], BF16)
                Rr = {'sb': RotI([0, 1, 6, 7]), 'ob': RotI([2, 4]), 'lb': RotI([3, 5]),
                      'pr': Rot(nc, ph, 'P_D', 6, [128, 512], BF16), 'p0r': Rot(nc, ph, 'P0_D', 4, [128, 512], BF16),
                      'rlr': Rot(nc, ph, 'rl_D', 2, [128, 512], F32), 'mkr': Rot(nc, ph, 'mk_D', 8, [128, 512], BF16)}
                yor = Rot(nc, ph, 'yo_D', 3, [128, 512], BF16)
                nh_ = 4 if not SKIP_D2 else 0
                pend_st = [None]
                if nh_:
                    nxt = prefetchD(0)
                for h in range(nh_):
                    qTt, qk, vt, vk, kz, kk = nxt
                    Rr['qk'], Rr['kk'], Rr['vk'] = qk, [(kk, 'd'), (kk, 'z')], vk
                    load_eb_exp(ebraw, eb)
                    if h + 1 < nh_:
                        nxt = prefetchD(h + 1)
                    for (q0, n) in TILES_Q:
                        yo, yok = yor.next()
                        r0 = 64 * (h % 2)
                        softmax_pass(qTt, kz, r0, vt, slice(0, 128), 64, q0, n, eb, h, True, yo[r0:r0 + 64, :n], yok, Rr)
                        if pend_st[0] is not None:
                            pend_st[0]()

                        def _store(yo=yo, yok=yok, r0=r0, h=h, q0=q0, n=n):
                            DMA('sp', yT_v[r0:r0 + 64, 2 + h // 2, q0:q0 + n], yo[r0:r0 + 64, :n], r=[yok])

                        pend_st[0] = _store
                if pend_st[0] is not None:
                    pend_st[0]()
                    pend_st[0] = None
            assert _minrem[0] >= 33000, ('SBUF over the 192KiB partition', _minrem[0])
            _minrem[0] = 1 << 30
            S.barrier()

        def phaseE(l):
            with ExitStack() as ph:
                sb_ = lambda name, shape, dt: ph.enter_context(_sbt(nc, name, list(shape), dt))
                wbr = sb_("wbr", [128, 8, D], BF16)
                wo = sb_("wo", [128, 8, D], BF16)
                for c in range(2):
                    DMA('pool', wbr[:, c, :], w_br_sb[l, c * 128:(c + 1) * 128, :], w=[('wbr', c)], max_dma_last_dim=4096)
                    DMA('pool', wbr[:, 2 + c, :], w_br_sp[l, c * 128:(c + 1) * 128, :], w=[('wbr', 2 + c)], max_dma_last_dim=4096)
                for c in range(4):
                    DMA('pool', wbr[:, 4 + c, :], w_br_df[l, c * 128:(c + 1) * 128, :], w=[('wbr', 4 + c)], max_dma_last_dim=4096)
                for c in range(8):
                    DMA('pool', wo[:, c, :], w_out[l, c * 128:(c + 1) * 128, :], w=[('wo', c)], max_dma_last_dim=4096)
                ytr = Rot(nc, ph, 'yt_E', 2, [128, 8, 512], BF16)
                gtr = Rot(nc, ph, 'gt_E', 2, [128, 24, 512], BF16)
                htr = Rot(nc, ph, 'ht_E', 2, [128, 8, 512], F32)
                mgr = Rot(nc, ph, 'mg_E', 2, [128, 8, 512], BF16)
                tar = Rot(nc, ph, 'ta_E', 2, [128, 512], F32)
                tbr = Rot(nc, ph, 'tb_E', 2, [128, 512], F32)
                tcr = Rot(nc, ph, 'tc_E', 2, [128, 512], F32)
                bbanks = RotI([0, 1, 2, 3, 4, 5])
                obanks = RotI([6, 7])
                def e_load(ti):
                    t0_, n_ = TILES_Q[ti]
                    yt_, ytk_ = ytr.next()
                    DMA('sp', yt_[:, :, :n_], yT_v[:, :, t0_:t0_ + n_], w=[ytk_])
                    gt_, gtk_ = gtr.next()
                    DMA('sp', gt_[:, :, :n_], g_v[:, :, t0_:t0_ + n_], w=[gtk_])
                    ht_, htk_ = htr.next()
                    DMA('sp', ht_[:, :, :n_], hT_v[:, :, t0_:t0_ + n_], w=[htk_])
                    return yt_, ytk_, gt_, gtk_, ht_, htk_

                nxt_e = e_load(0)
                for ti, (t0, n) in enumerate(TILES_Q):
                    yt, ytk, gt, gtk, ht, htk = nxt_e
                    if ti + 1 < len(TILES_Q):
                        nxt_e = e_load(ti + 1)
                    mg, mgk = mgr.next()
                    for fc in range(8):
                        b0, b1, b2 = bbanks.next(), bbanks.next(), bbanks.next()
                        fs = slice(fc * 128, (fc + 1) * 128)
                        for c in range(2):
                            MM(ps[b0][:, :n], wbr[:, c, fs], yt[:, c, :n], start=(c == 0), stop=(c == 1),
                               r=[('wbr', c), ytk], w=[PK[b0]], noinc=(c < 1))
                        for c in range(2, 4):
                            MM(ps[b1][:, :n], wbr[:, c, fs], yt[:, c, :n], start=(c == 2), stop=(c == 3),
                               r=[('wbr', c), ytk], w=[PK[b1]], noinc=(c < 3))
                        for c in range(4, 8):
                            MM(ps[b2][:, :n], wbr[:, c, fs], yt[:, c, :n], start=(c == 4), stop=(c == 7),
                               r=[('wbr', c), ytk], w=[PK[b2]], noinc=(c < 7))
                        ta, tak = tar.next()
                        tb_, tbk = tbr.next()
                        tc_, tck = tcr.next()
                        TT('dve', ta[:, :n], ps[b0][:, :n], gt[:, fc, :n], ALU.mult, r=[PK[b0], gtk], w=[tak])
                        TT('dve', tb_[:, :n], ps[b1][:, :n], gt[:, 8 + fc, :n], ALU.mult, r=[PK[b1], gtk], w=[tbk])
                        TT('dve', tc_[:, :n], ps[b2][:, :n], gt[:, 16 + fc, :n], ALU.mult, r=[PK[b2], gtk], w=[tck])
                        TT('pool', ta[:, :n], ta[:, :n], tb_[:, :n], ALU.add, r=[tak, tbk], w=[tak])
                        TT('pool', mg[:, fc, :n], ta[:, :n], tc_[:, :n], ALU.add, r=[tak, tck], w=[(mgk, fc)])
                    for oc in range(8):
                        bo = obanks.next()
                        for fc in range(8):
                            MM(ps[bo][:, :n], wo[:, fc, oc * 128:(oc + 1) * 128], mg[:, fc, :n], start=(fc == 0),
                               stop=(fc == 7), r=[('wo', fc), (mgk, fc)], w=[PK[bo]], noinc=(fc < 7))
                        TT('dve', ht[:, oc, :n], ps[bo][:, :n], ht[:, oc, :n], ALU.add, r=[PK[bo], htk], w=[htk])
                    DMA('sp', hT_v[:, :, t0:t0 + n], ht[:, :, :n], r=[htk])
            assert _minrem[0] >= 33000, ('SBUF over the 192KiB partition', _minrem[0])
            _minrem[0] = 1 << 30
            S.barrier()

        def phaseF(l):
            with ExitStack() as ph:
                sb_ = lambda name, shape, dt: ph.enter_context(_sbt(nc, name, list(shape), dt))
                wup = sb_("wup", [128, 8, 2 * DFF], BF16)
                for c in range(8):
                    DMA('pool', wup[:, c, :], w_up[l, c * 128:(c + 1) * 128, :], w=[('wup', c)], max_dma_last_dim=4096)
                WUK = [('wup', c) for c in range(8)]
                gn = sb_("gnF", [128, 8], F32)
                cw = sb_("cwF", [128, NFC * 3], F32)
                cbs = sb_("cbF", [128, NFC], F32)
                halo = sb_("haloF", [128, NFC, 2], F32)
                DMA('sp', gn[:], p_fn[l], w=['gn'])
                DMA('sp', cw[:], p_cw[l], w=['cw'])
                DMA('sp', cbs[:], p_cb[l], w=['cbs'])
                MSET('pool', halo[:].rearrange("p a b -> p (a b)"), 0.0, w=[('halo', fc) for fc in range(NFC)])
                htr = Rot(nc, ph, 'ht_F', 1, [128, 8, 512], F32)
                uTr = Rot(nc, ph, 'uT_F', 2, [128, 8, 512], BF16)
                actr = Rot(nc, ph, 'act_F', 4, [128, 512], BF16)
                sqr = Rot(nc, ph, 'sq_F', 2, [128, 512], F32)
                lnr = Rot(nc, ph, 'ln_F', 1, [128, 512], F32)
                rsr = Rot(nc, ph, 'rs_F', 1, [128, 512], F32)
                gtr = Rot(nc, ph, 'g_F', 3, [128, 514], F32)
                ccr = Rot(nc, ph, 'cc_F', 3, [128, 512], F32)
                ssr = Rot(nc, ph, 'ss_F', 3, [128, 512], F32)
                gbanks = RotI([0, 1, 2])
                vbanks = RotI([3, 4, 6, 7])
                def f1_load(ti):
                    t0_, n_ = TILES_Q[ti]
                    ht, htk = htr.next()
                    DMA('sp', ht[:, :, :n_], hT_v[:, :, t0_:t0_ + n_], w=[htk])
                    return ht, htk, n_

                def f1_norm(ld):
                    ht, htk, n_ = ld
                    uT_, uk_ = uTr.next()
                    rms_tile(ht, htk, n_, gn, 'gn', uT_, uk_, sqr, lnr, rsr, 5)
                    return uT_, uk_

                nxt_u = f1_norm(f1_load(0))
                for ti, (t0, n) in enumerate(TILES_Q):
                    uT, uk = nxt_u
                    nn = n
                    for fc in range(NFC):
                        if fc == 0 and ti + 1 < len(TILES_Q):
                            nxt_ld = f1_load(ti + 1)
                        if fc == 8 and ti + 1 < len(TILES_Q):
                            nxt_u = f1_norm(nxt_ld)
                        gb = gbanks.next()
                        vb = vbanks.next()
                        for c in range(8):
                            MM(ps[gb][:, :nn], wup[:, c, fc * 128:(fc + 1) * 128], uT[:, c, :nn], start=(c == 0),
                               stop=(c == 7), r=[WUK[c], uk], w=[PK[gb]], noinc=(c < 7))
                        for c in range(8):
                            MM(ps[vb][:, :nn], wup[:, c, DFF + fc * 128:DFF + (fc + 1) * 128], uT[:, c, :nn],
                               start=(c == 0), stop=(c == 7), r=[WUK[c], uk], w=[PK[vb]], noinc=(c < 7))
                        g_, gk = gtr.next()
                        CP('pool', g_[:, 0:2], halo[:, fc, :], r=[('halo', fc)], w=[(gk, 'h')])
                        ACT(g_[:, 2:2 + nn], ps[gb][:, :nn], AF.Copy, r=[PK[gb]], w=[(gk, 'b')])
                        cc, cck = ccr.next()
                        ACT(cc[:, :nn], ps[gb][:, :nn], AF.Identity, r=[PK[gb], 'cw', 'cbs'], w=[cck],
                            scale=cw[:, fc * 3 + 2:fc * 3 + 3], bias=cbs[:, fc:fc + 1])
                        STT(cc[:, :nn], g_[:, 1:1 + nn], cw[:, fc * 3 + 1:fc * 3 + 2], cc[:, :nn], ALU.mult, ALU.add,
                            r=[(gk, 'h'), (gk, 'b'), 'cw', cck], w=[cck])
                        STT(cc[:, :nn], g_[:, 0:nn], cw[:, fc * 3:fc * 3 + 1], cc[:, :nn], ALU.mult, ALU.add,
                            r=[(gk, 'h'), (gk, 'b'), 'cw', cck], w=[cck])
                        CP('pool', halo[:, fc, :], g_[:, nn:nn + 2], r=[(gk, 'b'), (gk, 'h')], w=[('halo', fc)])
                        ss, ssk = ssr.next()
                        ACT(ss[:, :nn], cc[:, :nn], AF.Silu, r=[cck], w=[ssk])
                        at, atk = actr.next()
                        TT('dve', at[:, :nn], ss[:, :nn], ps[vb][:, :nn], ALU.mult, r=[ssk, PK[vb]], w=[atk])
                        DMA('sp', actT_v[:, fc, t0:t0 + n], at[:, :n], r=[atk])
            S.barrier()
            with ExitStack() as ph:
                sb_ = lambda name, shape, dt: ph.enter_context(_sbt(nc, name, list(shape), dt))
                wdn = sb_("wdn", [128, NFC, D], BF16)
                for c in range(NFC):
                    DMA('pool', wdn[:, c, :], w_down[l, c * 128:(c + 1) * 128, :], w=[('wdn', c)], max_dma_last_dim=4096)
                htr = Rot(nc, ph, 'ht_G', 2, [128, 8, 512], F32)
                actr = Rot(nc, ph, 'act_G', 2, [128, NFC, 512], BF16)
                obanks = RotI([0, 1, 2, 3])
                def f2_load(ti):
                    t0_, n_ = TILES_Q[ti]
                    ht_, htk_ = htr.next()
                    DMA('sp', ht_[:, :, :n_], hT_v[:, :, t0_:t0_ + n_], w=[htk_])
                    at_, atk_ = actr.next()
                    DMA('sp', at_[:, :, :n_], actT_v[:, :, t0_:t0_ + n_], w=[atk_])
                    return ht_, htk_, at_, atk_

                nxt_l = f2_load(0)
                for ti, (t0, n) in enumerate(TILES_Q):
                    ht, htk, at, atk = nxt_l
                    if ti + 1 < len(TILES_Q):
                        nxt_l = f2_load(ti + 1)
                    for oc in range(8):
                        bo = obanks.next()
                        for fc in range(NFC):
                            MM(ps[bo][:, :n], wdn[:, fc, oc * 128:(oc + 1) * 128], at[:, fc, :n], start=(fc == 0),
                               stop=(fc == NFC - 1), r=[('wdn', fc), atk], w=[PK[bo]], noinc=(fc < NFC - 1))
                        TT('dve', ht[:, oc, :n], ps[bo][:, :n], ht[:, oc, :n], ALU.add, r=[PK[bo], htk], w=[htk])
                    DMA('sp', hT_v[:, :, t0:t0 + n], ht[:, :, :n], r=[htk])
            S.barrier()

        phase0()
        for l in range(n_layers):
            phaseA(l)
            if stop_after == 'A':
                break
            if 'B' in run_phases:
                phaseB(l)
            if 'C' in run_phases:
                phaseC(l)
            if 'D' in run_phases:
                phaseD(l)
            if stop_after == 'D':
                break
            if 'E' in run_phases:
                phaseE(l)
            if 'F' in run_phases:
                phaseF(l)
        phase_out()
        S.emit_all()
        print("ops", S.n_ops, "waits", S.n_waits)
    return nc


def _bucket_idx():
    b = np.zeros(128, np.int64)
    for n in range(128):
        if n < 16:
            b[n] = n
        else:
            v = np.float32(np.log(np.float32(n) / np.float32(16.0))) / np.float32(math.log(128 / 16)) * np.float32(16.0)
            b[n] = min(16 + int(np.float32(v)), 31)
    return b


def _consts():
    cbv = np.zeros((128, CB_W), np.float32)
    p = np.arange(128)[:, None]
    j = np.arange(128)[None, :]
    cbv[:, CB_ONES:CB_ONES + 128] = 1.0
    cbv[:, CB_NEGONES:CB_NEGONES + 128] = -1.0
    cbv[:, CB_NEGTRI:CB_NEGTRI + 128] = np.where(p >= j, -1.0, 0.0)
    cbv[:, CB_MSTRICT:CB_MSTRICT + 128] = np.where(p < j, 1.0, 0.0)
    cbv[:, CB_IDENT:CB_IDENT + 128] = np.eye(128)
    cbv[:, CB_ONES512:CB_ONES512 + 512] = 1.0
    cfv = np.zeros((128, CF_W), np.float32)
    cfv[:, CF_IDENT:CF_IDENT + 128] = np.eye(128)
    cfv[:, CF_ONESD:CF_ONESD + 128] = 1.0 / D
    blk = np.zeros((128, 128), np.float32)
    blk[:64, :64] = 1.0 / 64
    blk[64:, 64:] = 1.0 / 64
    cfv[:, CF_BLK64:CF_BLK64 + 128] = blk
    cfv[:, CF_ONES128TH:CF_ONES128TH + 128] = 1.0 / 128
    cfv[:, CF_MNEG:CF_MNEG + 128] = np.where(j > p, -1e30, 0.0)
    return cbv, cfv


def _prep_shared(inp):
    f = lambda a: np.ascontiguousarray(np.asarray(a, dtype=np.float32))
    sh = {}
    for k in ('w_in', 'w_br_sb', 'w_br_sp', 'w_br_df', 'w_out', 'w_up', 'w_down'):
        sh[k] = f(inp[k])
    sh['p_an'] = f(np.asarray(inp['attn_norm']).reshape(DEPTH, 8, 128).transpose(0, 2, 1))
    sh['p_fn'] = f(np.asarray(inp['ffn_norm']).reshape(DEPTH, 8, 128).transpose(0, 2, 1))
    sh['p_bg'] = f(np.asarray(inp['b_gate']).reshape(DEPTH, 24, 128).transpose(0, 2, 1))
    g = np.stack([np.tile(np.asarray(inp[k]), (1, 2)) for k in ('q_norm_sp', 'k_norm_sp', 'q_norm_df', 'k_norm_df')],
                 axis=-1)
    sh['p_gn'] = f(g)
    sh['p_sub'] = f(np.asarray(inp['subln_df']).reshape(DEPTH, 128, 1))
    lam = np.concatenate([np.asarray(inp[k]) for k in ('lam_q1', 'lam_k1', 'lam_q2', 'lam_k2')], axis=-1)
    sh['p_lam'] = f(np.broadcast_to(lam[:, None, :], (DEPTH, 128, 256)))
    cw = np.asarray(inp['conv_w']).reshape(DEPTH, 3, NFC, 128).transpose(0, 3, 2, 1)
    sh['p_cw'] = f(cw.reshape(DEPTH, 128, NFC * 3))
    sh['p_cb'] = f(np.asarray(inp['conv_b']).reshape(DEPTH, NFC, 128).transpose(0, 2, 1))
    rb = np.asarray(inp['rel_bias'], dtype=np.float32)
    bidx = _bucket_idx()
    bv = np.full((8, 1151), -30000.0, np.float32)
    nn = np.arange(0, 640)
    bsel = np.where(nn < 128, bidx[np.minimum(nn, 127)], 31)
    bv[:, 511:] = rb[bsel, :].T
    pp = np.arange(128)[:, None, None]
    aa = np.arange(5)[None, :, None]
    cc = np.arange(512)[None, None, :]
    tidx = (127 + 128 * aa + cc - pp).reshape(128, 2560)
    sh['p_eb'] = f(bv[:, tidx])
    sh['p_b31'] = f(np.broadcast_to(rb[31][None, :], (128, 8)))
    cbv, cfv = _consts()
    sh['c_b'] = cbv
    sh['c_f'] = cfv
    return sh


def _h0(inp, b):
    x = np.asarray(inp['x'], dtype=np.float32)
    meta = np.asarray(inp['meta_tokens'], dtype=np.float32)
    h = np.zeros((T, D), np.float32)
    h[:NMETA] = meta
    h[NMETA:NMETA + SEQ] = x[b]
    return h


def kernel(**inputs):
    sh = _prep_shared(inputs)
    nc = build()
    in_maps = []
    for b in range(NCORES):
        m = dict(sh)
        m['h0'] = _h0(inputs, b)
        in_maps.append(m)
    res = run_bass_kernel_spmd(nc, in_maps, core_ids=list(range(NCORES)))
    return np.stack([np.asarray(r['out'], dtype=np.float32) for r in res.results], axis=0)
```

```python
import math
from contextlib import ExitStack

import numpy as np
import concourse.bass as bass
import concourse.mybir as mybir
from concourse.bass_utils import run_bass_kernel_spmd

F32 = mybir.dt.float32
BF16 = mybir.dt.bfloat16
AF = mybir.ActivationFunctionType
ALU = mybir.AluOpType
AX = mybir.AxisListType

D = 1024
SEQ = 4096
NMETA = 16
T = 4224
NB = 33
DEPTH = 2
DIN = 6440
DFF = 2816
NFC = 22
EPS = 1e-6
TOPK = 256
NIT = 15
TILES = [(i * 512, 512) for i in range(8)] + [(4096, 128)]
TILES_Q = [(i * 512, 512) for i in range(8)] + [(4096, NMETA + SEQ - 4096)]
NCORES = 8
SKIP_D1 = False
PE_FILL_B = 0
SKIP_D2 = False

CHUNKS = []
for i in range(2):
    CHUNKS.append((0 + 128 * i, 128, 'copy', 1.0))
for i in range(2):
    CHUNKS.append((256 + 128 * i, 128, 'copy', 0.125))
for i in range(2):
    CHUNKS.append((768 + 128 * i, 128, 'norm', 0))
for i in range(2):
    CHUNKS.append((1024 + 128 * i, 128, 'norm', 1))
for i in range(2):
    CHUNKS.append((1536 + 128 * i, 128, 'copy', 1.0))
CHUNKS.append((1792, 32, 'copy', 1.0))
for i in range(4):
    CHUNKS.append((1832 + 128 * i, 128, 'norm', 2))
for i in range(4):
    CHUNKS.append((2344 + 128 * i, 128, 'norm', 3))
for i in range(24):
    CHUNKS.append((3368 + 128 * i, 128, 'gate', i))
NCH = len(CHUNKS)

CB_ONES, CB_NEGONES, CB_NEGTRI, CB_MSTRICT, CB_ZEROS, CB_IDENT, CB_ONES512 = 0, 128, 256, 384, 512, 640, 768
CB_W = 768 + 512
CF_IDENT, CF_ONESD, CF_BLK64, CF_ONES128TH, CF_MNEG = 0, 128, 256, 384, 512
CF_W = 640


class Sched:
    NDSEM = 16

    def __init__(self, nc, stack):
        self.nc = nc
        self.eng = {'pe': nc.tensor, 'act': nc.scalar, 'dve': nc.vector,
                    'pool': nc.gpsimd, 'sp': nc.sync}
        self.ops = {k: [] for k in self.eng}
        self.sem = {k: stack.enter_context(nc.semaphore("s_" + k)) for k in self.eng}
        self.cnt = {k: 0 for k in self.eng}
        self.seen = {k: {} for k in self.eng}
        self.dsem, self.dval, self.dcnt = {}, {}, {}
        for q in ('sp', 'pool'):
            self.dsem[q] = [stack.enter_context(nc.semaphore("d_%s%d" % (q, i)))
                            for i in range(self.NDSEM)]
            self.dval[q] = [0] * self.NDSEM
            self.dcnt[q] = 0
        self.w, self.r, self.semobj = {}, {}, {}
        self.n_ops = 0
        self.n_waits = 0

    def _sid(self, sem):
        i = id(sem)
        self.semobj[i] = sem
        return i

    def op(self, eng, fn, reads=(), writes=(), dma=False, noinc=False):
        psr = [k for k in reads if isinstance(k, tuple) and k[0] == 'ps']
        if psr:
            reads = [k for k in reads if not (isinstance(k, tuple) and k[0] == 'ps')]
            writes = list(writes) + psr
        waits = {}
        seen = self.seen[eng]
        own = self._sid(self.sem[eng])

        def need(ev):
            if ev is None:
                return
            if isinstance(ev, list):
                for e1 in ev:
                    need(e1)
                return
            sid, val = ev
            if eng == 'pe' and sid == own:
                return
            if seen.get(sid, 0) >= val:
                return
            if waits.get(sid, 0) < val:
                waits[sid] = val

        for k in reads:
            need(self.w.get(k))
        for k in writes:
            need(self.w.get(k))
            rd = self.r.get(k)
            if rd:
                for ev in rd.items():
                    need(ev)
        if dma:
            i = self.dcnt[eng] % self.NDSEM
            self.dcnt[eng] += 1
            sem = self.dsem[eng][i]
            prev = self.dval[eng][i]
            if prev > 0:
                need((self._sid(sem), prev))
            val = prev + 16
            self.dval[eng][i] = val
            inc = 16
        elif noinc:
            sem = self.sem[eng]
            val = self.cnt[eng] + 1
            inc = 0
        else:
            sem = self.sem[eng]
            self.cnt[eng] += 1
            val = self.cnt[eng]
            inc = 1
        sid = self._sid(sem)
        for s, v in waits.items():
            seen[s] = v
        wl = [(self.semobj[s], v) for s, v in waits.items()]
        self.n_ops += 1
        self.n_waits += len(wl)

        def emit(e, wl=wl, fn=fn, sem=sem, inc=inc):
            for s, v in wl:
                e.wait_ge(s, v)
            ins = fn(e)
            if inc:
                ins.then_inc(sem, inc)

        self.ops[eng].append(emit)
        for k in reads:
            d = self.r.setdefault(k, {})
            if d.get(sid, 0) < val:
                d[sid] = val
        for k in writes:
            self.w[k] = (sid, val)
            self.r[k] = {}
        return (sid, val)

    def dma_group(self, eng, fns, reads=(), writes=()):
        waits = {}
        seen = self.seen[eng]

        def need(ev):
            if ev is None:
                return
            if isinstance(ev, list):
                for e1 in ev:
                    need(e1)
                return
            sid, val = ev
            if seen.get(sid, 0) >= val:
                return
            if waits.get(sid, 0) < val:
                waits[sid] = val

        for k in reads:
            need(self.w.get(k))
        for k in writes:
            need(self.w.get(k))
            rd = self.r.get(k)
            if rd:
                for ev in rd.items():
                    need(ev)
        evs = []
        for fn in fns:
            i = self.dcnt[eng] % self.NDSEM
            self.dcnt[eng] += 1
            sem = self.dsem[eng][i]
            prev = self.dval[eng][i]
            if prev > 0:
                need((self._sid(sem), prev))
            val = prev + 16
            self.dval[eng][i] = val
            for s_, v_ in waits.items():
                seen[s_] = v_
            wl = [(self.semobj[s_], v_) for s_, v_ in waits.items()]
            waits = {}
            self.n_ops += 1
            self.n_waits += len(wl)

            def emit(e, wl=wl, fn=fn, sem=sem):
                for s_, v_ in wl:
                    e.wait_ge(s_, v_)
                fn(e).then_inc(sem, 16)

            self.ops[eng].append(emit)
            evs.append((self._sid(sem), val))
        for k in reads:
            d = self.r.setdefault(k, {})
            for sid, val in evs:
                if d.get(sid, 0) < val:
                    d[sid] = val
        for k in writes:
            self.w[k] = list(evs)
            self.r[k] = {}

    def barrier(self):
        evs = []
        for e in self.eng:
            if self.cnt[e] > 0:
                evs.append((self._sid(self.sem[e]), self.cnt[e], e))
        for q in self.dsem:
            for i in range(self.NDSEM):
                if self.dval[q][i] > 0:
                    evs.append((self._sid(self.dsem[q][i]), self.dval[q][i], None))
        for eng in self.eng:
            wl = []
            for sid, val, src in evs:
                if eng == 'pe' and src == 'pe':
                    continue
                if self.seen[eng].get(sid, 0) >= val:
                    continue
                self.seen[eng][sid] = val
                wl.append((self.semobj[sid], val))

            def emit(e, wl=wl):
                for s, v in wl:
                    e.wait_ge(s, v)
            self.ops[eng].append(emit)
        self.w.clear()
        self.r.clear()

    def emit_all(self):
        nc = self.nc
        ops = self.ops
        with nc.Block() as block:
            @block.tensor
            def _(e):
                for f in ops['pe']:
                    f(e)

            @block.scalar
            def _(e):
                for f in ops['act']:
                    f(e)

            @block.vector
            def _(e):
                for f in ops['dve']:
                    f(e)

            @block.gpsimd
            def _(e):
                for f in ops['pool']:
                    f(e)

            @block.sync
            def _(e):
                for f in ops['sp']:
                    f(e)


_minrem = [1 << 30]
_uid = [0]


def _sbt(nc, name, shape, dt):
    _uid[0] += 1
    return nc.sbuf_tensor("%s_u%d" % (name, _uid[0]), shape, dt)


class Rot:
    def __init__(self, nc, stack, name, n, shape, dt):
        self.t = [stack.enter_context(_sbt(nc, "%s%d" % (name, i), list(shape), dt)) for i in range(n)]
        _minrem[0] = min(_minrem[0], nc.sbuf_bytes_remaining)
        self.k = [(name, i) for i in range(n)]
        self.i = 0

    def next(self):
        j = self.i % len(self.t)
        self.i += 1
        return self.t[j], self.k[j]


class RotI:
    def __init__(self, items):
        self.items = list(items)
        self.i = 0

    def next(self):
        j = self.i % len(self.items)
        self.i += 1
        return self.items[j]


def build(n_layers=DEPTH, dbg=False, stop_after=None, run_phases='BCDEF'):
    nc = bass.Bass("TRN2", target_bir_lowering=False)
    skind = "ExternalOutput" if dbg else "Internal"

    def din(name, shape, dt=F32):
        return nc.dram_tensor(name, list(shape), dt, kind="ExternalInput").ap()

    def dscr(name, shape, dt):
        return nc.dram_tensor(name, list(shape), dt, kind=skind).ap()

    h0 = din("h0", [T, D])
    w_in = din("w_in", [DEPTH, D, DIN])
    w_br_sb = din("w_br_sb", [DEPTH, 256, D])
    w_br_sp = din("w_br_sp", [DEPTH, 256, D])
    w_br_df = din("w_br_df", [DEPTH, 512, D])
    w_out = din("w_out", [DEPTH, D, D])
    w_up = din("w_up", [DEPTH, D, 2 * DFF])
    w_down = din("w_down", [DEPTH, DFF, D])
    p_an = din("p_an", [DEPTH, 128, 8])
    p_fn = din("p_fn", [DEPTH, 128, 8])
    p_bg = din("p_bg", [DEPTH, 128, 24])
    p_gn = din("p_gn", [DEPTH, 128, 4])
    p_sub = din("p_sub", [DEPTH, 128, 1])
    p_lam = din("p_lam", [DEPTH, 128, 256])
    p_cw = din("p_cw", [DEPTH, 128, NFC * 3])
    p_cb = din("p_cb", [DEPTH, 128, NFC])
    p_eb = din("p_eb", [8, 128, 2560])
    p_b31 = din("p_b31", [128, 8])
    c_b = din("c_b", [128, CB_W])
    c_f = din("c_f", [128, CF_W])
    out = nc.dram_tensor("out", [SEQ, D], F32, kind="ExternalOutput").ap()

    hT = dscr("hT", [128, 8 * T], F32)
    zT = dscr("zT", [19 * 128, T], BF16)
    gT = dscr("gT", [128, 24 * T], BF16)
    v_all = dscr("v_all", [128, NB * 1024], BF16)
    wix_d = dscr("wix", [128, NB * 8], F32)
    yT = dscr("yT", [128, 8 * T], BF16)
    maskT = dscr("maskT", [128, NB * T], BF16)

    hT_v = hT.rearrange("p (c t) -> p c t", c=8)
    g_v = gT.rearrange("p (c t) -> p c t", c=24)
    maskT_v = maskT.rearrange("p (k t) -> p k t", k=NB)
    actT = dscr("actT", [128, NFC * T], BF16)
    actT_v = actT.rearrange("p (c t) -> p c t", c=NFC)

    with ExitStack() as st:
        S = Sched(nc, st)
        ps = [st.enter_context(nc.psum_tensor("ps%d" % i, [128, 512], F32)) for i in range(8)]
        PK = [('ps', i) for i in range(8)]
        cb = st.enter_context(_sbt(nc, "cb", [128, CB_W], BF16))
        cf = st.enter_context(_sbt(nc, "cf", [128, CF_W], F32))
        b31 = st.enter_context(_sbt(nc, "b31", [128, 8], F32))

        def DMA(q, out_, in_, r=(), w=(), **kw):
            if len(out_.shape) == 3:
                assert len(in_.shape) == 3 and in_.shape[1] == out_.shape[1]
                fns = [(lambda e, i=i: e.dma_start(out=out_[:, i, :], in_=in_[:, i, :], **kw))
                       for i in range(out_.shape[1])]
                S.dma_group(q, fns, r, w)
            else:
                assert len(in_.shape) == 2, in_.shape
                S.op(q, lambda e: e.dma_start(out=out_, in_=in_, **kw), r, w, dma=True)

        def MM(out_, lhsT, rhs, start=True, stop=True, r=(), w=(), noinc=False):
            S.op('pe', lambda e: e.matmul(out_, lhsT=lhsT, rhs=rhs, start=start, stop=stop), r, w, noinc=noinc)

        def TR(out_, in_, ident, r=(), w=(), noinc=False):
            S.op('pe', lambda e: e.transpose(out_, in_, ident), r, w, noinc=noinc)

        def ACT(out_, in_, func, r=(), w=(), **kw):
            S.op('act', lambda e: e.activation(out=out_, in_=in_, func=func, **kw), r, w)

        def TS(eng, out_, in0, s1, s2, op0, op1=None, r=(), w=(), **kw):
            if op1 is None:
                S.op(eng, lambda e: e.tensor_scalar(out=out_, in0=in0, scalar1=s1, scalar2=None, op0=op0, **kw), r, w)
            else:
                S.op(eng, lambda e: e.tensor_scalar(out=out_, in0=in0, scalar1=s1, scalar2=s2, op0=op0, op1=op1, **kw), r, w)

        def STT(out_, in0, scalar, in1, op0, op1, r=(), w=()):
            S.op('dve', lambda e: e.scalar_tensor_tensor(out=out_, in0=in0, scalar=scalar, in1=in1, op0=op0, op1=op1), r, w)

        def TT(eng, out_, in0, in1, op, r=(), w=()):
            S.op(eng, lambda e: e.tensor_tensor(out=out_, in0=in0, in1=in1, op=op), r, w)

        def CP(eng, out_, in_, r=(), w=()):
            if eng == 'act':
                S.op('act', lambda e: e.copy(out=out_, in_=in_), r, w)
            else:
                S.op(eng, lambda e: e.tensor_copy(out=out_, in_=in_), r, w)

        def MSET(eng, ap, val, w=()):
            S.op(eng, lambda e: e.memset(ap, val), (), w)

        def RED(out_, in_, op, r=(), w=()):
            S.op('dve', lambda e: e.tensor_reduce(out=out_, in_=in_, axis=AX.X, op=op), r, w)

        DMA('pool', cb[:], c_b, w=['cb'])
        DMA('sp', cf[:], c_f, w=['cf'])
        DMA('sp', b31[:], p_b31, w=['b31'])
        ones_bf = cb[:, CB_ONES:CB_ONES + 128]
        negones_bf = cb[:, CB_NEGONES:CB_NEGONES + 128]
        negtri_bf = cb[:, CB_NEGTRI:CB_NEGTRI + 128]
        mstrict_bf = cb[:, CB_MSTRICT:CB_MSTRICT + 128]
        zeros_bf = cb[:, CB_ZEROS:CB_ZEROS + 128]
        ident_bf = cb[:, CB_IDENT:CB_IDENT + 128]
        ones512_bf = cb[:, CB_ONES512:CB_ONES512 + 512]
        ident_f = cf[:, CF_IDENT:CF_IDENT + 128]
        onesD_f = cf[:, CF_ONESD:CF_ONESD + 128]
        blk64_f = cf[:, CF_BLK64:CF_BLK64 + 128]
        ones128th_f = cf[:, CF_ONES128TH:CF_ONES128TH + 128]
        mneg_f = cf[:, CF_MNEG:CF_MNEG + 128]

        evac_rr = RotI(['act', 'dve'])

        def phase0():
            with ExitStack() as ph:
                xin = Rot(nc, ph, 'xin', 3, [128, D], F32)
                hto = Rot(nc, ph, 'hto', 2, [128, 8, 512], F32)
                banks = RotI([0, 1, 2, 3])
                for (t0, n) in TILES:
                    ht, hk = hto.next()
                    for bi in range(n // 128):
                        b = t0 // 128 + bi
                        xt, xk = xin.next()
                        DMA('sp', xt[:], h0[b * 128:(b + 1) * 128, :], w=[xk])
                        for half in range(2):
                            pb = banks.next()
                            for j in range(4):
                                c = half * 4 + j
                                TR(ps[pb][:, j * 128:(j + 1) * 128], xt[:, c * 128:(c + 1) * 128], ident_f,
                                   r=[xk, 'cf'], w=[PK[pb]], noinc=(j < 3))
                            CP(evac_rr.next(), ht[:, half * 4:(half + 1) * 4, bi * 128:(bi + 1) * 128],
                               ps[pb][:, :].rearrange("p (c t) -> p c t", c=4), r=[PK[pb]], w=[hk])
                    DMA('sp', hT_v[:, :, t0:t0 + n], ht[:, :, :n], r=[hk])
            S.barrier()

        def phase_out():
            with ExitStack() as ph:
                hin = Rot(nc, ph, 'hin', 2, [128, 8, 512], F32)
                oto = Rot(nc, ph, 'oto', 3, [128, D], F32)
                banks = RotI([0, 1, 2, 3])
                for (t0, n) in TILES:
                    ht, hk = hin.next()
                    DMA('sp', ht[:, :, :n], hT_v[:, :, t0:t0 + n], w=[hk])
                    for bi in range(n // 128):
                        b = t0 // 128 + bi
                        ot, ok = oto.next()
                        for half in range(2):
                            pb = banks.next()
                            for j in range(4):
                                c = half * 4 + j
                                TR(ps[pb][:, j * 128:(j + 1) * 128], ht[:, c, bi * 128:(bi + 1) * 128], ident_f,
                                   r=[hk, 'cf'], w=[PK[pb]], noinc=(j < 3))
                            CP(evac_rr.next(), ot[:, half * 512:(half + 1) * 512], ps[pb][:, :], r=[PK[pb]], w=[ok])
                        lo_tok = max(b * 128, NMETA)
                        hi_tok = min((b + 1) * 128, NMETA + SEQ)
                        DMA('sp', out[lo_tok - NMETA:hi_tok - NMETA, :], ot[lo_tok - b * 128:hi_tok - b * 128, :],
                            r=[ok], w=[('out', b)])
            S.barrier()

        def rms_tile(ht, hk, n, gn, gk, uT, uk, sqr, lnr, rsr, bank):
            for c in range(8):
                sq, sk = sqr.next()
                ACT(sq[:, :n], ht[:, c, :n], AF.Square, r=[hk], w=[sk])
                MM(ps[bank][:, :n], onesD_f, sq[:, :n], start=(c == 0), stop=(c == 7), r=[sk, 'cf'], w=[PK[bank]])
            lnv, lk = lnr.next()
            ACT(lnv[:, :n], ps[bank][:, :n], AF.Ln, r=[PK[bank]], w=[lk], bias=EPS)
            rs, rk = rsr.next()
            ACT(rs[:, :n], lnv[:, :n], AF.Exp, r=[lk], w=[rk], scale=-0.5)
            for c in range(8):
                STT(uT[:, c, :n], ht[:, c, :n], gn[:, c:c + 1], rs[:, :n], ALU.mult, ALU.mult,
                    r=[hk, gk, rk], w=[uk])

        def phaseA(l):
            with ExitStack() as ph:
                wbf = ph.enter_context(_sbt(nc, "wbf", [128, 8, DIN], BF16))
                for c in range(8):
                    DMA('pool', wbf[:, c, :], w_in[l, c * 128:(c + 1) * 128, :], w=[('wbf', c)], max_dma_last_dim=4096)
                WK = [('wbf', c) for c in range(8)]
                gn = ph.enter_context(_sbt(nc, "gnA", [128, 8], F32))
                g4 = ph.enter_context(_sbt(nc, "g4A", [128, 4], F32))
                bg = ph.enter_context(_sbt(nc, "bgA", [128, 24], F32))
                DMA('sp', gn[:], p_an[l], w=['gn'])
                DMA('sp', g4[:], p_gn[l], w=['g4'])
                DMA('sp', bg[:], p_bg[l], w=['bg'])
                TS('dve', g4[:, 1:2], g4[:, 1:2], 0.125, None, ALU.mult, r=['g4'], w=['g4'])
                TS('dve', g4[:, 3:4], g4[:, 3:4], 0.125, None, ALU.mult, r=['g4'], w=['g4'])
                htr = Rot(nc, ph, 'htA', 2, [128, 8, 512], F32)
                sqr = Rot(nc, ph, 'sqA', 2, [128, 512], F32)
                lnr = Rot(nc, ph, 'lnA', 2, [128, 512], F32)
                rsr = Rot(nc, ph, 'rsA', 2, [128, 512], F32)
                uTr = Rot(nc, ph, 'uTA', 2, [128, 8, 512], BF16)
                stg = Rot(nc, ph, 'stgA', 4, [128, 512], BF16)
                vstr = Rot(nc, ph, 'vstA', 2, [128, 1024], BF16)
                wstr = Rot(nc, ph, 'wstA', 2, [128, 8], F32)
                zbanks = RotI([0, 1, 2])
                mbanks = RotI([3, 4])
                tbanks = RotI([6, 7])
                def a_load(ti):
                    t0_, n_ = TILES[ti]
                    ht_, hk_ = htr.next()
                    DMA('sp', ht_[:, :, :n_], hT_v[:, :, t0_:t0_ + n_], w=[hk_])
                    return ht_, hk_, n_

                def a_norm(ld):
                    ht_, hk_, n_ = ld
                    uT_, uk_ = uTr.next()
                    rms_tile(ht_, hk_, n_, gn, 'gn', uT_, uk_, sqr, lnr, rsr, 5)
                    return uT_, uk_

                nxt_a = a_norm(a_load(0))
                for ti, (t0, n) in enumerate(TILES):
                    uT, uk = nxt_a
                    for ci, (col0, rows, kind, param) in enumerate(CHUNKS):
                        if ci == 0 and ti + 1 < len(TILES):
                            nxt_ld = a_load(ti + 1)
                        if ci == 19 and ti + 1 < len(TILES):
                            nxt_a = a_norm(nxt_ld)
                        zb = zbanks.next()
                        for c in range(8):
                            MM(ps[zb][:rows, :n], wbf[:, c, col0:col0 + rows], uT[:, c, :n],
                               start=(c == 0), stop=(c == 7), r=[WK[c], uk], w=[PK[zb]], noinc=(c < 7))
                        sg, sk = stg.next()
                        if kind == 'copy':
                            if evac_rr.next() == 'act':
                                ACT(sg[:rows, :n], ps[zb][:rows, :n], AF.Copy, r=[PK[zb]], w=[sk], scale=float(param))
                            else:
                                TS('dve', sg[:rows, :n], ps[zb][:rows, :n], float(param), None, ALU.mult,
                                   r=[PK[zb]], w=[sk])
                        elif kind == 'gate':
                            ACT(sg[:, :n], ps[zb][:, :n], AF.Sigmoid, r=[PK[zb], 'bg'], w=[sk],
                                bias=bg[:, param:param + 1])
                        else:
                            sq, sqk = sqr.next()
                            ACT(sq[:, :n], ps[zb][:, :n], AF.Square, r=[PK[zb]], w=[sqk])
                            mb = mbanks.next()
                            MM(ps[mb][:, :n], blk64_f, sq[:, :n], r=[sqk, 'cf'], w=[PK[mb]])
                            lnv, lk = lnr.next()
                            ACT(lnv[:, :n], ps[mb][:, :n], AF.Ln, r=[PK[mb]], w=[lk], bias=EPS)
                            rs, rk = rsr.next()
                            ACT(rs[:, :n], lnv[:, :n], AF.Exp, r=[lk], w=[rk], scale=-0.5)
                            STT(sg[:, :n], ps[zb][:, :n], g4[:, param:param + 1], rs[:, :n], ALU.mult, ALU.mult,
                                r=[PK[zb], 'g4', rk], w=[sk])
                        if kind == 'gate':
                            DMA('sp', g_v[:, param, t0:t0 + n], sg[:, :n], r=[sk])
                        else:
                            DMA('sp', zT[ci * 128:ci * 128 + rows, t0:t0 + n], sg[:rows, :n], r=[sk])
                    for b in range(n // 128):
                        vs, vk = vstr.next()
                        for (col0, wd, dst0) in ((512, 256, 0), (1280, 256, 256), (2856, 512, 512)):
                            tb = tbanks.next()
                            for c in range(8):
                                MM(ps[tb][:, :wd], uT[:, c, b * 128:(b + 1) * 128], wbf[:, c, col0:col0 + wd],
                                   start=(c == 0), stop=(c == 7), r=[WK[c], uk], w=[PK[tb]], noinc=(c < 7))
                            CP(evac_rr.next(), vs[:, dst0:dst0 + wd], ps[tb][:, :wd], r=[PK[tb]], w=[vk])
                        tb = tbanks.next()
                        for c in range(8):
                            MM(ps[tb][:, :8], uT[:, c, b * 128:(b + 1) * 128], wbf[:, c, 1824:1832],
                               start=(c == 0), stop=(c == 7), r=[WK[c], uk], w=[PK[tb]], noinc=(c < 7))
                        ws, wk = wstr.next()
                        CP('dve', ws[:, :], ps[tb][:, :8], r=[PK[tb]], w=[wk])
                        bb = (t0 + b * 128) // 128
                        DMA('sp', v_all[:, bb * 1024:(bb + 1) * 1024], vs[:, :], r=[vk])
                        DMA('sp', wix_d[:, bb * 8:(bb + 1) * 8], ws[:, :], r=[wk])
            assert _minrem[0] >= 33000, ('SBUF over the 192KiB partition', _minrem[0])
            _minrem[0] = 1 << 30
            S.barrier()

        v_view = v_all.rearrange("p (b c) -> p b c", c=1024)
        yT_v = yT.rearrange("p (c t) -> p c t", c=8)
        psbf = [ps[i][:, :].bitcast(BF16) for i in range(8)]

        def emit_pipelined(iters, skews):
            n = len(iters)
            for t in range(n + max(skews)):
                for j, sk in enumerate(skews):
                    i = t - sk
                    if 0 <= i < n:
                        iters[i][j]()

        def phaseB(l):
            with ExitStack() as ph:
                qr_ = Rot(nc, ph, 'qT_B', 2, [128, T], BF16)
                kzr = [Rot(nc, ph, 'kz0_B', 2, [128, T], BF16), Rot(nc, ph, 'kz1_B', 2, [128, T], BF16)]
                vr_ = Rot(nc, ph, 'vt_B', 2, [128, NB, 128], BF16)
                bufs = []
                for hp in range(2):
                    qTt, qk = qr_.next()
                    kz0, kk0 = kzr[0].next()
                    kz1, kk1 = kzr[1].next()
                    vt, vk = vr_.next()
                    DMA('sp', qTt[:], zT[(0 + hp) * 128:(1 + hp) * 128, :], w=[qk])
                    MSET('pool', kz0[64:128, :], 0.0, w=[(kk0, 'z')])
                    MSET('pool', kz1[0:64, :], 0.0, w=[(kk1, 'z')])
                    DMA('sp', kz0[0:64, :], zT[(2 + hp) * 128:(2 + hp) * 128 + 64, :], w=[(kk0, 'd')])
                    DMA('sp', kz1[64:128, :], zT[(2 + hp) * 128 + 64:(3 + hp) * 128, :], w=[(kk1, 'd')])
                    DMA('sp', vt[:], v_view[:, :, hp * 128:(hp + 1) * 128], w=[vk])
                    bufs.append((qTt, qk, (kz0, kz1), (kk0, kk1), vt, vk))
                R32 = ph.enter_context(_sbt(nc, "R32_B", [128, 512], F32))
                Rbr = Rot(nc, ph, 'RbfB', 3, [128, 512], BF16)
                er = Rot(nc, ph, 'eB', 3, [128, 512], F32)
                spr = Rot(nc, ph, 'spB', 5, [128, 512], BF16)
                Ar = Rot(nc, ph, 'AB', 4, [128, 512], BF16)
                ystr = Rot(nc, ph, 'yB', 2, [128, 512], BF16)
                zbanks = RotI([0, 1])
                lbanks = RotI([2, 3])
                obanks = RotI([4, 5])
                for hp in range(2):
                    qTt, qk, kzs, kks, vt, vk = bufs[hp]
                    for hh in range(2):
                        pb = 64 * hh
                        kTt = kzs[hh]
                        kkd, kkz = (kks[hh], 'd'), (kks[hh], 'z')
                        for (q0, n) in TILES_Q:
                            nkb = (q0 + n + 127) // 128
                            ob = obanks.next()
                            MM(ps[ob][:, :n], zeros_bf[:, 0:128], ones512_bf[:, :n], start=True, stop=False,
                               r=['cb'], w=[PK[ob]])
                            MSET('pool', R32[:, :n], 0.0, w=['R32'])
                            iters = []
                            order = list(reversed(range(nkb)))
                            rb_prev = [None]
                            for pi, kb in enumerate(order):
                                c0 = max(0, kb * 128 - q0)
                                nn = n - c0
                                diag = kb * 128 >= q0
                                first = (pi == 0)
                                lastp = (pi == len(order) - 1)
                                ksl = kTt[:, kb * 128:(kb + 1) * 128]
                                qsl = qTt[:, q0 + c0:q0 + n]
                                zb = zbanks.next()
                                lb = lbanks.next()
                                et, ek = er.next()
                                spt, spk = spr.next()
                                At, Ak = Ar.next()
                                rb_in = rb_prev[0]
                                if not lastp:
                                    rb_out = Rbr.next()
                                    rb_prev[0] = rb_out
                                else:
                                    rb_out = None

                                def S1(zb=zb, ksl=ksl, qsl=qsl, nn=nn, et=et, ek=ek):
                                    MM(ps[zb][:, :nn], ksl, qsl, r=[kkd, kkz, qk], w=[PK[zb]])
                                    ACT(et[:, :nn], ps[zb][:, :nn], AF.Exp, r=[PK[zb]], w=[ek])

                                def S1b(nn=nn, et=et, ek=ek, spt=spt, spk=spk, diag=diag):
                                    ACT(spt[:, :nn], et[:, :nn], AF.Ln, r=[ek], w=[spk], bias=1.0)
                                    if diag:
                                        TT('dve', spt[:, 0:min(128, nn)], spt[:, 0:min(128, nn)], mstrict_bf[:, 0:min(128, nn)], ALU.mult, r=[spk, 'cb'], w=[spk])

                                def SR(spt=spt, spk=spk, nn=nn, c0=c0, rb_out=rb_out, n=n):
                                    if rb_out is None:
                                        return
                                    TT('dve', R32[:, c0:n], R32[:, c0:n], spt[:, :nn], ALU.add, r=['R32', spk], w=['R32'])
                                    CP('dve', rb_out[0][:, :n], R32[:, :n], r=['R32'], w=[rb_out[1]])

                                def S2(lb=lb, ksl=ksl, qsl=qsl, nn=nn, spt=spt, spk=spk, first=first, rb_in=rb_in, c0=c0, n=n,
                                       At=At, Ak=Ak, diag=diag):
                                    MM(ps[lb][:, :nn], ksl, qsl, start=True, stop=False, r=[kkd, kkz, qk], w=[PK[lb]], noinc=True)
                                    MM(ps[lb][:, :nn], negtri_bf, spt[:, :nn], start=False, stop=first,
                                       r=['cb', spk], w=[PK[lb]], noinc=(not first))
                                    if not first:
                                        MM(ps[lb][:, :nn], negones_bf, rb_in[0][:, c0:n], start=False, stop=True,
                                           r=['cb', rb_in[1]], w=[PK[lb]])
                                    ACT(At[:, :nn], ps[lb][:, :nn], AF.Exp, r=[PK[lb]], w=[Ak])
                                    if diag:
                                        TT('dve', At[:, 0:min(128, nn)], At[:, 0:min(128, nn)], mstrict_bf[:, 0:min(128, nn)], ALU.mult, r=[Ak, 'cb'], w=[Ak])

                                def S3(ob=ob, kb=kb, c0=c0, n=n, nn=nn, At=At, Ak=Ak, hh=hh, lastp=lastp):
                                    MM(ps[ob][:, c0:n], vt[:, kb, :], At[:, :nn], start=False,
                                       stop=lastp, r=[vk, Ak], w=[PK[ob]])
                                    for _f in range(PE_FILL_B):
                                        MM(ps[6 + (_f % 2)][:, :512], ones_bf, ones512_bf, r=['cb'], w=[], noinc=True)

                                iters.append([S1, S2, S1b, SR, S3])
                            emit_pipelined(iters, [0, 2, 0, 1, 3])
                            ys, yk = ystr.next()
                            CP(evac_rr.next(), ys[pb:pb + 64, :n], ps[ob][pb:pb + 64, :n], r=[PK[ob]], w=[yk])
                            DMA('sp', yT_v[pb:pb + 64, hp, q0:q0 + n], ys[pb:pb + 64, :n], r=[yk])
            S.barrier()

        def softmax_pass(qTt, kz, r0, vt, vsl, dv, q0, n, eb, head, use_mask, o_dst, o_key, Rr):
            qt4 = q0 // 128
            nkb = (q0 + n + 127) // 128
            ob = Rr['ob'].next()
            lb = Rr['lb'].next()
            MM(ps[ob][:, :n], zeros_bf[:, :128], ones512_bf[:, :n], start=True, stop=False, r=['cb'], w=[PK[ob]])
            MM(ps[lb][:, :n], zeros_bf[:, :128], ones512_bf[:, :n], start=True, stop=False, r=['cb'], w=[PK[lb]])
            iters = []
            for kb in range(nkb):
                c0 = max(0, kb * 128 - q0)
                nn = n - c0
                delta = qt4 - kb
                sbk = Rr['sb'].next()
                P, Pk = Rr['pr'].next()
                P0k = None
                if delta <= 1:
                    P0, P0k = Rr['p0r'].next()
                else:
                    P0 = None
                if use_mask:
                    mk, mkk = Rr['mkr'].next()
                else:
                    mk, mkk = None, None
                last = (kb == nkb - 1)

                def S1(sbk=sbk, kb=kb, c0=c0, nn=nn, mk=mk, mkk=mkk):
                    MM(ps[sbk][:, :nn], kz[:, kb * 128:(kb + 1) * 128], qTt[:, q0 + c0:q0 + n],
                       r=list(Rr['kk']) + [Rr['qk']], w=[PK[sbk]])
                    if mk is not None:
                        DMA('sp', mk[:, :nn], maskT_v[:, kb, q0 + c0:q0 + n], w=[mkk])

                def S2(sbk=sbk, c0=c0, nn=nn, delta=delta, P=P, Pk=Pk, P0=P0, P0k=P0k, mk=mk, mkk=mkk):
                    if delta <= 1:
                        ACT(P0[:, :nn], ps[sbk][:, :nn], AF.Exp, r=[PK[sbk]], w=[P0k])
                        TT('dve', P[:, :nn], P0[:, :nn], eb[:, delta + 3, c0:n], ALU.mult, r=[P0k, 'eb'], w=[Pk])
                    else:
                        ACT(P[:, :nn], ps[sbk][:, :nn], AF.Exp, r=[PK[sbk], 'b31'], w=[Pk], bias=b31[:, head:head + 1])
                    if mk is not None:
                        TT('dve', P[:, :nn], P[:, :nn], mk[:, :nn], ALU.mult, r=[Pk, mkk], w=[Pk])

                def S3(kb=kb, c0=c0, nn=nn, P=P, Pk=Pk, last=last):
                    MM(ps[ob][:, c0:n], vt[:, kb, vsl], P[:, :nn], start=False, stop=last, r=[Rr['vk'], Pk], w=[PK[ob]],
                       noinc=True)
                    MM(ps[lb][:, c0:n], ones_bf[:, :128], P[:, :nn], start=False, stop=last, r=['cb', Pk], w=[PK[lb]])

                iters.append([S1, S2, S3])
            emit_pipelined(iters, [0, 2, 4])
            rl, rlk = Rr['rlr'].next()
            S.op('dve', lambda e: e.reciprocal(out=rl[r0:r0 + dv, :n], in_=ps[lb][r0:r0 + dv, :n]), [PK[lb]], [rlk])
            TT('dve', o_dst, ps[ob][r0:r0 + dv, :n], rl[r0:r0 + dv, :n], ALU.mult, r=[PK[ob], rlk], w=[o_key])

        def load_eb_dma(ebraw, head):
            DMA('sp', ebraw[:], p_eb[head], w=['ebraw'])

        def load_eb_exp(ebraw, eb):
            ACT(eb[:].rearrange("p a b -> p (a b)"), ebraw[:], AF.Exp, r=['ebraw'], w=['eb'])

        def phaseC(l):
            lam_init = 0.8 - 0.6 * math.exp(-0.3 * l)
            with ExitStack() as ph:
                sb_ = lambda name, shape, dt: ph.enter_context(_sbt(nc, name, list(shape), dt))
                lamv = sb_("lamv", [128, 256], F32)
                sub = sb_("sub", [128, 1], F32)
                prod = sb_("prod", [128, 128], F32)
                s12 = sb_("s12", [128, 2], F32)
                e12 = sb_("e12", [128, 2], F32)
                nlam = sb_("nlam", [128, 1], F32)
                subc = sb_("subc", [128, 1], F32)
                DMA('sp', lamv[:], p_lam[l], w=['lamv'])
                DMA('sp', sub[:], p_sub[l], w=['sub'])
                TT('dve', prod[:, 0:64], lamv[:, 0:64], lamv[:, 64:128], ALU.mult, r=['lamv'], w=['prod'])
                TT('dve', prod[:, 64:128], lamv[:, 128:192], lamv[:, 192:256], ALU.mult, r=['lamv'], w=['prod'])
                RED(s12[:, 0:1], prod[:, 0:64], ALU.add, r=['prod'], w=['s12'])
                RED(s12[:, 1:2], prod[:, 64:128], ALU.add, r=['prod'], w=['s12'])
                ACT(e12[:], s12[:], AF.Exp, r=['s12'], w=['e12'])
                TT('dve', nlam[:], e12[:, 1:2], e12[:, 0:1], ALU.subtract, r=['e12'], w=['nlam'])
                TS('dve', nlam[:], nlam[:], -lam_init, None, ALU.add, r=['nlam'], w=['nlam'])
                TS('dve', subc[:], sub[:], 1.0 - lam_init, None, ALU.mult, r=['sub'], w=['subc'])
                qr_ = Rot(nc, ph, 'qT_C', 2, [128, T], BF16)
                kzr = [Rot(nc, ph, 'kz0_C', 2, [128, T], BF16), Rot(nc, ph, 'kz1_C', 2, [128, T], BF16)]
                vr_ = Rot(nc, ph, 'vt_C', 2, [128, NB, 128], BF16)
                ebraw = sb_("ebraw_C", [128, 2560], F32)

                def prefetchC(h):
                    qTt, qk = qr_.next()
                    kz0, kk0 = kzr[0].next()
                    kz1, kk1 = kzr[1].next()
                    vt, vk = vr_.next()
                    DMA('sp', qTt[:], zT[(11 + h) * 128:(12 + h) * 128, :], w=[qk])
                    MSET('pool', kz0[64:128, :], 0.0, w=[(kk0, 'z')])
                    MSET('pool', kz1[0:64, :], 0.0, w=[(kk1, 'z')])
                    DMA('sp', kz0[0:64, :], zT[(15 + h) * 128:(15 + h) * 128 + 64, :], w=[(kk0, 'd')])
                    DMA('sp', kz1[64:128, :], zT[(15 + h) * 128 + 64:(16 + h) * 128, :], w=[(kk1, 'd')])
                    DMA('sp', vt[:], v_view[:, :, 512 + 128 * h:512 + 128 * (h + 1)], w=[vk])
                    load_eb_dma(ebraw, 4 + h)
                    return (qTt, qk, (kz0, kz1), (kk0, kk1), vt, vk)
                eb = sb_("eb_C", [128, 5, 512], BF16)
                Rr = {'sb': RotI([0, 1, 7]), 'ob': RotI([2, 4]), 'lb': RotI([3, 5]),
                      'pr': Rot(nc, ph, 'P_C', 6, [128, 512], BF16), 'p0r': Rot(nc, ph, 'P0_C', 4, [128, 512], BF16),
                      'rlr': Rot(nc, ph, 'rl_C', 2, [128, 512], F32)}
                o1r = Rot(nc, ph, 'o1_C', 2, [128, 512], F32)
                o2r = Rot(nc, ph, 'o2_C', 2, [128, 512], F32)
                yr = Rot(nc, ph, 'y_C', 2, [128, 512], F32)
                sqr = Rot(nc, ph, 'sq_C', 2, [128, 512], F32)
                lnr = Rot(nc, ph, 'ln_C', 2, [128, 512], F32)
                yor = Rot(nc, ph, 'yo_C', 2, [128, 512], BF16)
                nxt = prefetchC(0)
                for h in range(4):
                    qTt, qk, kzs, kks, vt, vk = nxt
                    Rr['qk'], Rr['vk'] = qk, vk
                    load_eb_exp(ebraw, eb)
                    if h + 1 < 4:
                        nxt = prefetchC(h + 1)
                    for (q0, n) in TILES_Q:
                        o1, o1k = o1r.next()
                        Rr['kk'] = [(kks[0], 'd'), (kks[0], 'z')]
                        softmax_pass(qTt, kzs[0], 0, vt, slice(0, 128), 128, q0, n, eb, 4 + h, False, o1[:, :n], o1k, Rr)
                        o2, o2k = o2r.next()
                        Rr['kk'] = [(kks[1], 'd'), (kks[1], 'z')]
                        softmax_pass(qTt, kzs[1], 0, vt, slice(0, 128), 128, q0, n, eb, 4 + h, False, o2[:, :n], o2k, Rr)
                        y, yk = yr.next()
                        STT(y[:, :n], o2[:, :n], nlam[:, 0:1], o1[:, :n], ALU.mult, ALU.add, r=[o1k, o2k, 'nlam'], w=[yk])
                        sq, sqk = sqr.next()
                        ACT(sq[:, :n], y[:, :n], AF.Square, r=[yk], w=[sqk])
                        MM(ps[6][:, :n], ones128th_f, sq[:, :n], r=['cf', sqk], w=[PK[6]])
                        lnv, lk = lnr.next()
                        ACT(lnv[:, :n], ps[6][:, :n], AF.Ln, r=[PK[6]], w=[lk], bias=EPS)
                        ACT(sq[:, :n], lnv[:, :n], AF.Exp, r=[lk], w=[sqk], scale=-0.5)
                        yo, yok = yor.next()
                        STT(yo[:, :n], y[:, :n], subc[:, 0:1], sq[:, :n], ALU.mult, ALU.mult, r=[yk, 'subc', sqk], w=[yok])
                        DMA('sp', yT_v[:, 4 + h, q0:q0 + n], yo[:, :n], r=[yok])
            assert _minrem[0] >= 33000, ('SBUF over the 192KiB partition', _minrem[0])
            _minrem[0] = 1 << 30
            S.barrier()

        def phaseD(l):
            with ExitStack() as ph:
                sb_ = lambda name, shape, dt: ph.enter_context(_sbt(nc, name, list(shape), dt))
                kix = sb_("kix", [32, T], BF16)
                wixa = sb_("wixa", [128, NB, 8], F32)
                aw = sb_("aw", [128, NB, 8], F32)
                sg = sb_("sg", [128, NB, 8], F32)
                DMA('sp', kix[:], zT[10 * 128:10 * 128 + 32, :], w=['kix'])
                DMA('sp', wixa[:].rearrange("p b j -> p (b j)"), wix_d, w=['wixa'])
                TS('dve', aw[:], wixa[:], -1.0, None, ALU.mult, r=['wixa'], w=['aw'])
                TT('dve', aw[:], aw[:], wixa[:], ALU.max, r=['wixa', 'aw'], w=['aw'])
                TS('dve', sg[:], wixa[:], 0.0, 2.0, ALU.is_ge, ALU.mult, r=['wixa'], w=['sg'])
                TS('dve', sg[:], sg[:], -1.0, None, ALU.add, r=['sg'], w=['sg'])
                qixr = Rot(nc, ph, 'qix', 2, [32, 8, 128], BF16)
                scr = Rot(nc, ph, 'sc', 2, [128, T], F32)
                rr = Rot(nc, ph, 'rl', 4, [128, 512], BF16)
                dgr = Rot(nc, ph, 'dg', 2, [128, 8, 128], BF16)
                scbanks = RotI([6, 7])
                junk = sb_("junk", [128, T], BF16)
                maskr = Rot(nc, ph, 'mk', 2, [128, T], BF16)
                mstr = Rot(nc, ph, 'mst', 3, [128, 512], BF16)
                hi = sb_("hi", [128, 1], F32)
                lo = sb_("lo", [128, 1], F32)
                w0 = sb_("w0", [128, 1], F32)
                mid = sb_("mid", [128, 1], F32)
                cnt = sb_("cnt", [128, 1], F32)
                tmp = sb_("tmp", [128, 1], F32)
                tauc = sb_("tauc", [128, 1], F32)
                MSET('dve', tauc[:], -1e29, w=['tauc'])
                fvec = sb_("fvec", [128, 16], F32)
                WF = sb_("WF", [128, 16], F32)
                for k in range(16):
                    MSET('pool', fvec[:, k:k + 1], 2.0 ** -k, w=['fvec'])
                xbanks = RotI([0, 1, 2, 3])
                tbanks = RotI([4, 5])
                d1state = {}

                def stageS(qb):
                        nk = 128 * (qb + 1)
                        qx, qxk = qixr.next()
                        for j in range(8):
                            DMA('sp', qx[:, j, :], zT[1024 + 32 * j:1024 + 32 * (j + 1), qb * 128:(qb + 1) * 128], w=[(qxk, j)])
                        sc, sck = scr.next()
                        allk = []
                        dg, dgk = dgr.next()
                        for j in range(8):
                            TS('dve', dg[:, j, :], ident_bf, sg[:, qb, j:j + 1], None, ALU.mult, r=['cb', 'sg'], w=[(dgk, j)])
                        iters = []
                        for s0 in range(0, nk, 512):
                            wd = min(512, nk - s0)
                            sk0 = (sck, s0)
                            allk.append(sk0)
                            scb = scbanks.next()
                            for j in range(8):
                                xb = xbanks.next()
                                rt, rtk = rr.next()

                                def S1(xb=xb, j=j, s0=s0, wd=wd, qx=qx, qxk=qxk):
                                    MM(ps[xb][:, :wd], qx[:, j, :], kix[:, s0:s0 + wd], r=[(qxk, j), 'kix'], w=[PK[xb]])

                                def S2(xb=xb, j=j, wd=wd, rt=rt, rtk=rtk, qb=qb):
                                    ACT(rt[:, :wd], ps[xb][:, :wd], AF.Relu, r=[PK[xb], 'aw'], w=[rtk], scale=aw[:, qb, j:j + 1])

                                def S3(scb=scb, j=j, wd=wd, rt=rt, rtk=rtk, dg=dg, dgk=dgk, sc=sc, sk0=sk0, s0=s0):
                                    MM(ps[scb][:, :wd], dg[:, j, :], rt[:, :wd], start=(j == 0), stop=(j == 7),
                                       r=[(dgk, j), rtk], w=[PK[scb]], noinc=(j < 7))
                                    if j == 7:
                                        CP('act', sc[:, s0:s0 + wd], ps[scb][:, :wd], r=[PK[scb]], w=[sk0])

                                iters.append([S1, S2, S3])
                        emit_pipelined(iters, [0, 0, 2])
                        d1state[qb] = (sc, sck, allk, nk)

                def stageB(qb):
                        sc, sck, allk, nk = d1state.pop(qb)
                        if nk > TOPK:
                            S.op('dve', lambda e, sc=sc, nk=nk: e.tensor_reduce(
                                out=hi[:], in_=sc[:, :nk], axis=AX.X, op=ALU.max, apply_absolute_value=True),
                                allk, ['hi'])
                            TS('dve', lo[:], hi[:], -1.0, None, ALU.mult, r=['hi'], w=['lo'])
                        dk = (sck, ((nk - 128) // 512) * 512)
                        TT('dve', sc[:, nk - 128:nk], sc[:, nk - 128:nk], mneg_f, ALU.add, r=[dk, 'cf'], w=[dk])
                        if nk > TOPK:
                            TS('dve', w0[:], hi[:], 2.002, 1e-6, ALU.mult, ALU.add, r=['hi'], w=['w0'])
                            TS('dve', WF[:], fvec[:], w0[:, 0:1], None, ALU.mult, r=['fvec', 'w0'], w=['WF'])
                            TT('dve', mid[:], lo[:], WF[:, 1:2], ALU.add, r=['lo', 'WF'], w=['mid'])
                            for k in range(1, NIT + 1):
                                S.op('dve', lambda e, nk=nk, sc=sc: e.tensor_scalar(
                                    out=junk[:, :nk], in0=sc[:, :nk], scalar1=mid[:, 0:1], scalar2=None,
                                    op0=ALU.is_ge, op1=ALU.add, accum_out=cnt[:, 0:1]), allk + ['mid'], ['junk', 'cnt'])
                                STT(tmp[:], cnt[:], TOPK - 0.5, WF[:, k:k + 1], ALU.is_ge, ALU.mult, r=['cnt', 'WF'], w=['tmp'])
                                kk = k + 1 if k < NIT else k
                                STT(mid[:], mid[:], WF[:, kk:kk + 1], tmp[:], ALU.subtract, ALU.add,
                                    r=['mid', 'WF', 'tmp'], w=['mid'])
                            tau, tauk = mid, 'mid'
                        else:
                            tau, tauk = tauc, 'tauc'
                        mk, mkk = maskr.next()
                        TS('dve', mk[:, :nk], sc[:, :nk], tau[:, 0:1], None, ALU.is_ge, r=allk + [tauk], w=[mkk])
                        for g0 in range(0, qb + 1, 4):
                            nb_ = min(4, qb + 1 - g0)
                            tb = tbanks.next()
                            for j in range(nb_):
                                kb = g0 + j
                                TR(psbf[tb][:, j * 128:(j + 1) * 128], mk[:, kb * 128:(kb + 1) * 128], ident_bf,
                                   r=[mkk, 'cb'], w=[PK[tb]], noinc=(j < nb_ - 1))
                            ms, msk = mstr.next()
                            CP('act', ms[:, :nb_ * 128], psbf[tb][:, :nb_ * 128], r=[PK[tb]], w=[msk])
                            DMA('sp', maskT_v[:, g0:g0 + nb_, qb * 128:(qb + 1) * 128],
                                ms[:, :nb_ * 128].rearrange("p (k t) -> p k t", k=nb_), r=[msk])
                assert _minrem[0] >= 33000, ('SBUF over the 192KiB partition', _minrem[0])
                _minrem[0] = 1 << 30

                nqb = NB if not SKIP_D1 else 0
                if nqb:
                    stageS(0)
                for qb in range(nqb):
                    if qb + 1 < nqb:
                        stageS(qb + 1)
                    stageB(qb)
            S.barrier()
            with ExitStack() as ph:
                sb_ = lambda name, shape, dt: ph.enter_context(_sbt(nc, name, list(shape), dt))
                qr_ = Rot(nc, ph, 'qT_D', 2, [128, T], BF16)
                kr_ = Rot(nc, ph, 'kz_D', 2, [128, T], BF16)
                vr_ = Rot(nc, ph, 'vt_D', 2, [128, NB, 128], BF16)
                ebraw = sb_("ebraw_D", [128, 2560], F32)
                qkcur = [None]

                def prefetchD(h):
                    if h % 2 == 0:
                        qTt, qk = qr_.next()
                        vt, vk = vr_.next()
                        DMA('sp', qTt[:], zT[(4 + h // 2) * 128:(5 + h // 2) * 128, :], w=[qk])
                        DMA('sp', vt[:], v_view[:, :, 256 + 128 * (h // 2):256 + 128 * (h // 2 + 1)], w=[vk])
                        qkcur[0] = (qTt, qk, vt, vk)
                    kz, kk = kr_.next()
                    r0 = 64 * (h % 2)
                    MSET('pool', kz[64 - r0:128 - r0, :], 0.0, w=[(kk, 'z')])
                    DMA('sp', kz[r0:r0 + 64, :], zT[(6 + h // 2) * 128 + r0:(6 + h // 2) * 128 + r0 + 64, :], w=[(kk, 'd')])
                    load_eb_dma(ebraw, h)
                    return qkcur[0] + (kz, kk)
                eb = sb_("eb_D", [128, 5, 512], BF16)
                Rr = {'sb': RotI([0, 1, 6, 7]), 'ob': RotI([2, 4]), 'lb': RotI([3, 5]),
                      'pr': Rot(nc, ph, 'P_D', 6, [128, 512], BF16), 'p0r': Rot(nc, ph, 'P0_D', 4, [128, 512], BF16),
                      'rlr': Rot(nc, ph, 'rl_D', 2, [128, 512], F32), 'mkr': Rot(nc, ph, 'mk_D', 8, [128, 512], BF16)}
                yor = Rot(nc, ph, 'yo_D', 3, [128, 512], BF16)
                nh_ = 4 if not SKIP_D2 else 0
                pend_st = [None]
                if nh_:
                    nxt = prefetchD(0)
                for h in range(nh_):
                    qTt, qk, vt, vk, kz, kk = nxt
                    Rr['qk'], Rr['kk'], Rr['vk'] = qk, [(kk, 'd'), (kk, 'z')], vk
                    load_eb_exp(ebraw, eb)
                    if h + 1 < nh_:
                        nxt = prefetchD(h + 1)
                    for (q0, n) in TILES_Q:
                        yo, yok = yor.next()
                        r0 = 64 * (h % 2)
                        softmax_pass(qTt, kz, r0, vt, slice(0, 128), 64, q0, n, eb, h, True, yo[r0:r0 + 64, :n], yok, Rr)
                        if pend_st[0] is not None:
                            pend_st[0]()

                        def _store(yo=yo, yok=yok, r0=r0, h=h, q0=q0, n=n):
                            DMA('sp', yT_v[r0:r0 + 64, 2 + h // 2, q0:q0 + n], yo[r0:r0 + 64, :n], r=[yok])

                        pend_st[0] = _store
                if pend_st[0] is not None:
                    pend_st[0]()
                    pend_st[0] = None
            assert _minrem[0] >= 33000, ('SBUF over the 192KiB partition', _minrem[0])
            _minrem[0] = 1 << 30
            S.barrier()

        def phaseE(l):
            with ExitStack() as ph:
                sb_ = lambda name, shape, dt: ph.enter_context(_sbt(nc, name, list(shape), dt))
                wbr = sb_("wbr", [128, 8, D], BF16)
                wo = sb_("wo", [128, 8, D], BF16)
                for c in range(2):
                    DMA('pool', wbr[:, c, :], w_br_sb[l, c * 128:(c + 1) * 128, :], w=[('wbr', c)], max_dma_last_dim=4096)
                    DMA('pool', wbr[:, 2 + c, :], w_br_sp[l, c * 128:(c + 1) * 128, :], w=[('wbr', 2 + c)], max_dma_last_dim=4096)
                for c in range(4):
                    DMA('pool', wbr[:, 4 + c, :], w_br_df[l, c * 128:(c + 1) * 128, :], w=[('wbr', 4 + c)], max_dma_last_dim=4096)
                for c in range(8):
                    DMA('pool', wo[:, c, :], w_out[l, c * 128:(c + 1) * 128, :], w=[('wo', c)], max_dma_last_dim=4096)
                ytr = Rot(nc, ph, 'yt_E', 2, [128, 8, 512], BF16)
                gtr = Rot(nc, ph, 'gt_E', 2, [128, 24, 512], BF16)
                htr = Rot(nc, ph, 'ht_E', 2, [128, 8, 512], F32)
                mgr = Rot(nc, ph, 'mg_E', 2, [128, 8, 512], BF16)
                tar = Rot(nc, ph, 'ta_E', 2, [128, 512], F32)
                tbr = Rot(nc, ph, 'tb_E', 2, [128, 512], F32)
                tcr = Rot(nc, ph, 'tc_E', 2, [128, 512], F32)
                bbanks = RotI([0, 1, 2, 3, 4, 5])
                obanks = RotI([6, 7])
                def e_load(ti):
                    t0_, n_ = TILES_Q[ti]
                    yt_, ytk_ = ytr.next()
                    DMA('sp', yt_[:, :, :n_], yT_v[:, :, t0_:t0_ + n_], w=[ytk_])
                    gt_, gtk_ = gtr.next()
                    DMA('sp', gt_[:, :, :n_], g_v[:, :, t0_:t0_ + n_], w=[gtk_])
                    ht_, htk_ = htr.next()
                    DMA('sp', ht_[:, :, :n_], hT_v[:, :, t0_:t0_ + n_], w=[htk_])
                    return yt_, ytk_, gt_, gtk_, ht_, htk_

                nxt_e = e_load(0)
                for ti, (t0, n) in enumerate(TILES_Q):
                    yt, ytk, gt, gtk, ht, htk = nxt_e
                    if ti + 1 < len(TILES_Q):
                        nxt_e = e_load(ti + 1)
                    mg, mgk = mgr.next()
                    for fc in range(8):
                        b0, b1, b2 = bbanks.next(), bbanks.next(), bbanks.next()
                        fs = slice(fc * 128, (fc + 1) * 128)
                        for c in range(2):
                            MM(ps[b0][:, :n], wbr[:, c, fs], yt[:, c, :n], start=(c == 0), stop=(c == 1),
                               r=[('wbr', c), ytk], w=[PK[b0]], noinc=(c < 1))
                        for c in range(2, 4):
                            MM(ps[b1][:, :n], wbr[:, c, fs], yt[:, c, :n], start=(c == 2), stop=(c == 3),
                               r=[('wbr', c), ytk], w=[PK[b1]], noinc=(c < 3))
                        for c in range(4, 8):
                            MM(ps[b2][:, :n], wbr[:, c, fs], yt[:, c, :n], start=(c == 4), stop=(c == 7),
                               r=[('wbr', c), ytk], w=[PK[b2]], noinc=(c < 7))
                        ta, tak = tar.next()
                        tb_, tbk = tbr.next()
                        tc_, tck = tcr.next()
                        TT('dve', ta[:, :n], ps[b0][:, :n], gt[:, fc, :n], ALU.mult, r=[PK[b0], gtk], w=[tak])
                        TT('dve', tb_[:, :n], ps[b1][:, :n], gt[:, 8 + fc, :n], ALU.mult, r=[PK[b1], gtk], w=[tbk])
                        TT('dve', tc_[:, :n], ps[b2][:, :n], gt[:, 16 + fc, :n], ALU.mult, r=[PK[b2], gtk], w=[tck])
                        TT('pool', ta[:, :n], ta[:, :n], tb_[:, :n], ALU.add, r=[tak, tbk], w=[tak])
                        TT('pool', mg[:, fc, :n], ta[:, :n], tc_[:, :n], ALU.add, r=[tak, tck], w=[(mgk, fc)])
                    for oc in range(8):
                        bo = obanks.next()
                        for fc in range(8):
                            MM(ps[bo][:, :n], wo[:, fc, oc * 128:(oc + 1) * 128], mg[:, fc, :n], start=(fc == 0),
                               stop=(fc == 7), r=[('wo', fc), (mgk, fc)], w=[PK[bo]], noinc=(fc < 7))
                        TT('dve', ht[:, oc, :n], ps[bo][:, :n], ht[:, oc, :n], ALU.add, r=[PK[bo], htk], w=[htk])
                    DMA('sp', hT_v[:, :, t0:t0 + n], ht[:, :, :n], r=[htk])
            assert _minrem[0] >= 33000, ('SBUF over the 192KiB partition', _minrem[0])
            _minrem[0] = 1 << 30
            S.barrier()

        def phaseF(l):
            with ExitStack() as ph:
                sb_ = lambda name, shape, dt: ph.enter_context(_sbt(nc, name, list(shape), dt))
                wup = sb_("wup", [128, 8, 2 * DFF], BF16)
                for c in range(8):
                    DMA('pool', wup[:, c, :], w_up[l, c * 128:(c + 1) * 128, :], w=[('wup', c)], max_dma_last_dim=4096)
                WUK = [('wup', c) for c in range(8)]
                gn = sb_("gnF", [128, 8], F32)
                cw = sb_("cwF", [128, NFC * 3], F32)
                cbs = sb_("cbF", [128, NFC], F32)
                halo = sb_("haloF", [128, NFC, 2], F32)
                DMA('sp', gn[:], p_fn[l], w=['gn'])
                DMA('sp', cw[:], p_cw[l], w=['cw'])
                DMA('sp', cbs[:], p_cb[l], w=['cbs'])
                MSET('pool', halo[:].rearrange("p a b -> p (a b)"), 0.0, w=[('halo', fc) for fc in range(NFC)])
                htr = Rot(nc, ph, 'ht_F', 1, [128, 8, 512], F32)
                uTr = Rot(nc, ph, 'uT_F', 2, [128, 8, 512], BF16)
                actr = Rot(nc, ph, 'act_F', 4, [128, 512], BF16)
                sqr = Rot(nc, ph, 'sq_F', 2, [128, 512], F32)
                lnr = Rot(nc, ph, 'ln_F', 1, [128, 512], F32)
                rsr = Rot(nc, ph, 'rs_F', 1, [128, 512], F32)
                gtr = Rot(nc, ph, 'g_F', 3, [128, 514], F32)
                ccr = Rot(nc, ph, 'cc_F', 3, [128, 512], F32)
                ssr = Rot(nc, ph, 'ss_F', 3, [128, 512], F32)
                gbanks = RotI([0, 1, 2])
                vbanks = RotI([3, 4, 6, 7])
                def f1_load(ti):
                    t0_, n_ = TILES_Q[ti]
                    ht, htk = htr.next()
                    DMA('sp', ht[:, :, :n_], hT_v[:, :, t0_:t0_ + n_], w=[htk])
                    return ht, htk, n_

                def f1_norm(ld):
                    ht, htk, n_ = ld
                    uT_, uk_ = uTr.next()
                    rms_tile(ht, htk, n_, gn, 'gn', uT_, uk_, sqr, lnr, rsr, 5)
                    return uT_, uk_

                nxt_u = f1_norm(f1_load(0))
                for ti, (t0, n) in enumerate(TILES_Q):
                    uT, uk = nxt_u
                    nn = n
                    for fc in range(NFC):
                        if fc == 0 and ti + 1 < len(TILES_Q):
                            nxt_ld = f1_load(ti + 1)
                        if fc == 8 and ti + 1 < len(TILES_Q):
                            nxt_u = f1_norm(nxt_ld)
                        gb = gbanks.next()
                        vb = vbanks.next()
                        for c in range(8):
                            MM(ps[gb][:, :nn], wup[:, c, fc * 128:(fc + 1) * 128], uT[:, c, :nn], start=(c == 0),
                               stop=(c == 7), r=[WUK[c], uk], w=[PK[gb]], noinc=(c < 7))
                        for c in range(8):
                            MM(ps[vb][:, :nn], wup[:, c, DFF + fc * 128:DFF + (fc + 1) * 128], uT[:, c, :nn],
                               start=(c == 0), stop=(c == 7), r=[WUK[c], uk], w=[PK[vb]], noinc=(c < 7))
                        g_, gk = gtr.next()
                        CP('pool', g_[:, 0:2], halo[:, fc, :], r=[('halo', fc)], w=[(gk, 'h')])
                        ACT(g_[:, 2:2 + nn], ps[gb][:, :nn], AF.Copy, r=[PK[gb]], w=[(gk, 'b')])
                        cc, cck = ccr.next()
                        ACT(cc[:, :nn], ps[gb][:, :nn], AF.Identity, r=[PK[gb], 'cw', 'cbs'], w=[cck],
                            scale=cw[:, fc * 3 + 2:fc * 3 + 3], bias=cbs[:, fc:fc + 1])
                        STT(cc[:, :nn], g_[:, 1:1 + nn], cw[:, fc * 3 + 1:fc * 3 + 2], cc[:, :nn], ALU.mult, ALU.add,
                            r=[(gk, 'h'), (gk, 'b'), 'cw', cck], w=[cck])
                        STT(cc[:, :nn], g_[:, 0:nn], cw[:, fc * 3:fc * 3 + 1], cc[:, :nn], ALU.mult, ALU.add,
                            r=[(gk, 'h'), (gk, 'b'), 'cw', cck], w=[cck])
                        CP('pool', halo[:, fc, :], g_[:, nn:nn + 2], r=[(gk, 'b'), (gk, 'h')], w=[('halo', fc)])
                        ss, ssk = ssr.next()
                        ACT(ss[:, :nn], cc[:, :nn], AF.Silu, r=[cck], w=[ssk])
                        at, atk = actr.next()
                        TT('dve', at[:, :nn], ss[:, :nn], ps[vb][:, :nn], ALU.mult, r=[ssk, PK[vb]], w=[atk])
                        DMA('sp', actT_v[:, fc, t0:t0 + n], at[:, :n], r=[atk])
            S.barrier()
            with ExitStack() as ph:
                sb_ = lambda name, shape, dt: ph.enter_context(_sbt(nc, name, list(shape), dt))
                wdn = sb_("wdn", [128, NFC, D], BF16)
                for c in range(NFC):
                    DMA('pool', wdn[:, c, :], w_down[l, c * 128:(c + 1) * 128, :], w=[('wdn', c)], max_dma_last_dim=4096)
                htr = Rot(nc, ph, 'ht_G', 2, [128, 8, 512], F32)
                actr = Rot(nc, ph, 'act_G', 2, [128, NFC, 512], BF16)
                obanks = RotI([0, 1, 2, 3])
                def f2_load(ti):
                    t0_, n_ = TILES_Q[ti]
                    ht_, htk_ = htr.next()
                    DMA('sp', ht_[:, :, :n_], hT_v[:, :, t0_:t0_ + n_], w=[htk_])
                    at_, atk_ = actr.next()
                    DMA('sp', at_[:, :, :n_], actT_v[:, :, t0_:t0_ + n_], w=[atk_])
                    return ht_, htk_, at_, atk_

                nxt_l = f2_load(0)
                for ti, (t0, n) in enumerate(TILES_Q):
                    ht, htk, at, atk = nxt_l
                    if ti + 1 < len(TILES_Q):
                        nxt_l = f2_load(ti + 1)
                    for oc in range(8):
                        bo = obanks.next()
                        for fc in range(NFC):
                            MM(ps[bo][:, :n], wdn[:, fc, oc * 128:(oc + 1) * 128], at[:, fc, :n], start=(fc == 0),
                               stop=(fc == NFC - 1), r=[('wdn', fc), atk], w=[PK[bo]], noinc=(fc < NFC - 1))
                        TT('dve', ht[:, oc, :n], ps[bo][:, :n], ht[:, oc, :n], ALU.add, r=[PK[bo], htk], w=[htk])
                    DMA('sp', hT_v[:, :, t0:t0 + n], ht[:, :, :n], r=[htk])
            S.barrier()

        phase0()
        for l in range(n_layers):
            phaseA(l)
            if stop_after == 'A':
                break
            if 'B' in run_phases:
                phaseB(l)
            if 'C' in run_phases:
                phaseC(l)
            if 'D' in run_phases:
                phaseD(l)
            if stop_after == 'D':
                break
            if 'E' in run_phases:
                phaseE(l)
            if 'F' in run_phases:
                phaseF(l)
        phase_out()
        S.emit_all()
        print("ops", S.n_ops, "waits", S.n_waits)
    return nc


def _bucket_idx():
    b = np.zeros(128, np.int64)
    for n in range(128):
        if n < 16:
            b[n] = n
        else:
            v = np.float32(np.log(np.float32(n) / np.float32(16.0))) / np.float32(math.log(128 / 16)) * np.float32(16.0)
            b[n] = min(16 + int(np.float32(v)), 31)
    return b


def _consts():
    cbv = np.zeros((128, CB_W), np.float32)
    p = np.arange(128)[:, None]
    j = np.arange(128)[None, :]
    cbv[:, CB_ONES:CB_ONES + 128] = 1.0
    cbv[:, CB_NEGONES:CB_NEGONES + 128] = -1.0
    cbv[:, CB_NEGTRI:CB_NEGTRI + 128] = np.where(p >= j, -1.0, 0.0)
    cbv[:, CB_MSTRICT:CB_MSTRICT + 128] = np.where(p < j, 1.0, 0.0)
    cbv[:, CB_IDENT:CB_IDENT + 128] = np.eye(128)
    cbv[:, CB_ONES512:CB_ONES512 + 512] = 1.0
    cfv = np.zeros((128, CF_W), np.float32)
    cfv[:, CF_IDENT:CF_IDENT + 128] = np.eye(128)
    cfv[:, CF_ONESD:CF_ONESD + 128] = 1.0 / D
    blk = np.zeros((128, 128), np.float32)
    blk[:64, :64] = 1.0 / 64
    blk[64:, 64:] = 1.0 / 64
    cfv[:, CF_BLK64:CF_BLK64 + 128] = blk
    cfv[:, CF_ONES128TH:CF_ONES128TH + 128] = 1.0 / 128
    cfv[:, CF_MNEG:CF_MNEG + 128] = np.where(j > p, -1e30, 0.0)
    return cbv, cfv


def _prep_shared(inp):
    f = lambda a: np.ascontiguousarray(np.asarray(a, dtype=np.float32))
    sh = {}
    for k in ('w_in', 'w_br_sb', 'w_br_sp', 'w_br_df', 'w_out', 'w_up', 'w_down'):
        sh[k] = f(inp[k])
    sh['p_an'] = f(np.asarray(inp['attn_norm']).reshape(DEPTH, 8, 128).transpose(0, 2, 1))
    sh['p_fn'] = f(np.asarray(inp['ffn_norm']).reshape(DEPTH, 8, 128).transpose(0, 2, 1))
    sh['p_bg'] = f(np.asarray(inp['b_gate']).reshape(DEPTH, 24, 128).transpose(0, 2, 1))
    g = np.stack([np.tile(np.asarray(inp[k]), (1, 2)) for k in ('q_norm_sp', 'k_norm_sp', 'q_norm_df', 'k_norm_df')],
                 axis=-1)
    sh['p_gn'] = f(g)
    sh['p_sub'] = f(np.asarray(inp['subln_df']).reshape(DEPTH, 128, 1))
    lam = np.concatenate([np.asarray(inp[k]) for k in ('lam_q1', 'lam_k1', 'lam_q2', 'lam_k2')], axis=-1)
    sh['p_lam'] = f(np.broadcast_to(lam[:, None, :], (DEPTH, 128, 256)))
    cw = np.asarray(inp['conv_w']).reshape(DEPTH, 3, NFC, 128).transpose(0, 3, 2, 1)
    sh['p_cw'] = f(cw.reshape(DEPTH, 128, NFC * 3))
    sh['p_cb'] = f(np.asarray(inp['conv_b']).reshape(DEPTH, NFC, 128).transpose(0, 2, 1))
    rb = np.asarray(inp['rel_bias'], dtype=np.float32)
    bidx = _bucket_idx()
    bv = np.full((8, 1151), -30000.0, np.float32)
    nn = np.arange(0, 640)
    bsel = np.where(nn < 128, bidx[np.minimum(nn, 127)], 31)
    bv[:, 511:] = rb[bsel, :].T
    pp = np.arange(128)[:, None, None]
    aa = np.arange(5)[None, :, None]
    cc = np.arange(512)[None, None, :]
    tidx = (127 + 128 * aa + cc - pp).reshape(128, 2560)
    sh['p_eb'] = f(bv[:, tidx])
    sh['p_b31'] = f(np.broadcast_to(rb[31][None, :], (128, 8)))
    cbv, cfv = _consts()
    sh['c_b'] = cbv
    sh['c_f'] = cfv
    return sh


def _h0(inp, b):
    x = np.asarray(inp['x'], dtype=np.float32)
    meta = np.asarray(inp['meta_tokens'], dtype=np.float32)
    h = np.zeros((T, D), np.float32)
    h[:NMETA] = meta
    h[NMETA:NMETA + SEQ] = x[b]
    return h


def kernel(**inputs):
    sh = _prep_shared(inputs)
    nc = build()
    in_maps = []
    for b in range(NCORES):
        m = dict(sh)
        m['h0'] = _h0(inputs, b)
        in_maps.append(m)
    res = run_bass_kernel_spmd(nc, in_maps, core_ids=list(range(NCORES)))
    return np.stack([np.asarray(r['out'], dtype=np.float32) for r in res.results], axis=0)
```
